# Optimizing a Trainium2 kernel written in Bass

```python
import math
import jax, jax.numpy as jnp
from jax import lax
import numpy as np

D_MODEL = 1024
BATCH = 4
SEQ = 4096
DEPTH = 2

N_ATT_HEADS = 4
ATT_HEAD_DIM = 64
ATT_V_DIM = 2 * ATT_HEAD_DIM
ATT_QK_WIDTH = N_ATT_HEADS * 2 * ATT_HEAD_DIM
ATT_WIDTH = N_ATT_HEADS * ATT_V_DIM
ATT_SCALE = ATT_HEAD_DIM ** -0.5
Q_BLOCK = 128
SSM_WIDTH = D_MODEL - ATT_WIDTH
SSM_HEAD_DIM = 64
SSM_HEADS = SSM_WIDTH // SSM_HEAD_DIM
SSM_GROUPS = 2
SSM_STATE = 128
SSM_CONV = 4
SSM_CHUNK = 128
XBC_WIDTH = SSM_WIDTH + 2 * SSM_GROUPS * SSM_STATE
IN_SPLITS = (ATT_QK_WIDTH, ATT_QK_WIDTH, ATT_WIDTH, SSM_WIDTH, XBC_WIDTH, SSM_HEADS)
IN_WIDTH = sum(IN_SPLITS)
CONF_WIDTH = D_MODEL
CONF_WIDTH_K = 31
FFN_DIM = 2816
FFN_CONV = 3
RMS_EPS = 1e-6
LN_EPS = 1e-5

kernel_name = "hybrid_diffattn_ssd_conformer_convffn"


def rms_norm(x, w):
    xf = x.astype(jnp.float32)
    y = xf * lax.rsqrt(jnp.mean(xf * xf, axis=-1, keepdims=True) + RMS_EPS)
    return (y * w.astype(jnp.float32)).astype(x.dtype)


def layer_norm(x, w, b):
    xf = x.astype(jnp.float32)
    mu = jnp.mean(xf, axis=-1, keepdims=True)
    var = jnp.mean(jnp.square(xf - mu), axis=-1, keepdims=True)
    y = (xf - mu) * lax.rsqrt(var + LN_EPS)
    return (y * w.astype(jnp.float32) + b.astype(jnp.float32)).astype(x.dtype)


def causal_dwconv(x, w, b):
    k_w, c = w.shape
    xp = jnp.pad(x, ((0, 0), (k_w - 1, 0), (0, 0)))
    y = lax.conv_general_dilated(xp, w[:, None, :].astype(x.dtype), window_strides=(1,),
                                 padding='VALID', dimension_numbers=('NWC', 'WIO', 'NWC'),
                                 feature_group_count=c)
    return y + b.astype(x.dtype)


def segsum(x):
    t = x.shape[-1]
    xe = jnp.broadcast_to(x[..., :, None], x.shape + (t,))
    xe = jnp.where(jnp.tril(jnp.ones((t, t), bool), -1), xe, 0.0)
    cs = jnp.cumsum(xe, axis=-2)
    return jnp.where(jnp.tril(jnp.ones((t, t), bool), 0), cs, -jnp.inf)


def ssd_chunked(xdt, a_dt, bm, cm):
    b, l, h, p = xdt.shape
    g, n = bm.shape[2], bm.shape[3]
    r = h // g
    c = l // SSM_CHUNK
    t = SSM_CHUNK
    X = xdt.reshape(b, c, t, g, r, p)
    A = a_dt.reshape(b, c, t, g, r).transpose(0, 3, 4, 1, 2)
    Bc = bm.reshape(b, c, t, g, n)
    Cc = cm.reshape(b, c, t, g, n)
    A_cum = jnp.cumsum(A, axis=-1)
    Lmat = jnp.exp(segsum(A))
    CB = jnp.einsum('bclgn,bcsgn->bgcls', Cc, Bc)
    y_diag = jnp.einsum('bgcls,bgrcls,bcsgrp->bclgrp', CB, Lmat, X)
    decay_states = jnp.exp(A_cum[..., -1:] - A_cum)
    states = jnp.einsum('bclgn,bgrcl,bclgrp->bcgrpn', Bc, decay_states, X)
    chunk_a = jnp.pad(A_cum[..., -1], ((0, 0), (0, 0), (0, 0), (1, 0)))
    decay_chunk = jnp.exp(segsum(chunk_a))
    states_p = jnp.concatenate([jnp.zeros_like(states[:, :1]), states], axis=1)
    new_states = jnp.einsum('bgrzc,bcgrpn->bzgrpn', decay_chunk, states_p)[:, :-1]
    y_off = jnp.einsum('bclgn,bcgrpn,bgrcl->bclgrp', Cc, new_states, jnp.exp(A_cum))
    return (y_diag + y_off).reshape(b, l, h, p)


def diff_attention(q, k, v, lam):
    bsz, s = q.shape[0], q.shape[1]
    nb = s // Q_BLOCK
    qb = q.reshape(bsz, nb, Q_BLOCK, N_ATT_HEADS, 2, ATT_HEAD_DIM).transpose(1, 0, 2, 3, 4, 5)
    key_pos = jnp.arange(s)

    def one_block(args):
        q_blk, start = args
        sc = jnp.einsum('bqhcd,bkhcd->bhcqk', q_blk, k).astype(jnp.float32)
        causal = (start + jnp.arange(Q_BLOCK))[:, None] >= key_pos[None, :]
        sc = jnp.where(causal, sc, -jnp.inf)
        pr = jax.nn.softmax(sc, axis=-1)
        a = pr[:, :, 0] - lam * pr[:, :, 1]
        return jnp.einsum('bhqk,bkhv->bqhv', a.astype(v.dtype), v)

    o = lax.map(one_block, (qb, jnp.arange(nb) * Q_BLOCK))
    return o.transpose(1, 0, 2, 3, 4).reshape(bsz, s, N_ATT_HEADS, ATT_V_DIM)


def hybrid_mixer(h, lam_init, w_in, q_norm_w, k_norm_w, lam_q1, lam_k1, lam_q2, lam_k2,
                 attn_subln_w, ssm_conv_w, ssm_conv_b, ssm_dt_bias, ssm_A_log, ssm_D,
                 ssm_norm_w, w_out):
    bsz, s, _ = h.shape
    proj = h @ w_in.astype(h.dtype)
    idx = np.cumsum(IN_SPLITS)[:-1].tolist()
    q, k, v, z, xbc, dt_raw = jnp.split(proj, idx, axis=-1)
    q = rms_norm(q.reshape(bsz, s, N_ATT_HEADS, 2, ATT_HEAD_DIM), q_norm_w) * ATT_SCALE
    k = rms_norm(k.reshape(bsz, s, N_ATT_HEADS, 2, ATT_HEAD_DIM), k_norm_w)
    v = v.reshape(bsz, s, N_ATT_HEADS, ATT_V_DIM)
    f32 = jnp.float32
    lam = (jnp.exp(jnp.sum(lam_q1.astype(f32) * lam_k1.astype(f32)))
           - jnp.exp(jnp.sum(lam_q2.astype(f32) * lam_k2.astype(f32))) + lam_init)
    att = diff_attention(q, k, v, lam)
    att = (rms_norm(att, attn_subln_w) * (1.0 - lam_init)).reshape(bsz, s, ATT_WIDTH)
    xbc = jax.nn.silu(causal_dwconv(xbc, ssm_conv_w, ssm_conv_b))
    xs, bm, cm = jnp.split(xbc, [SSM_WIDTH, SSM_WIDTH + SSM_GROUPS * SSM_STATE], axis=-1)
    xs = xs.reshape(bsz, s, SSM_HEADS, SSM_HEAD_DIM)
    bm = bm.reshape(bsz, s, SSM_GROUPS, SSM_STATE)
    cm = cm.reshape(bsz, s, SSM_GROUPS, SSM_STATE)
    dt = jax.nn.softplus(dt_raw.astype(f32) + ssm_dt_bias.astype(f32))
    a = -jnp.exp(ssm_A_log.astype(f32))
    y = ssd_chunked(xs.astype(f32) * dt[..., None], dt * a, bm.astype(f32), cm.astype(f32))
    y = y + xs.astype(f32) * ssm_D.astype(f32)[:, None]
    y = y.reshape(bsz, s, SSM_WIDTH) * jax.nn.silu(z.astype(f32))
    y = rms_norm(y.reshape(bsz, s, SSM_GROUPS, SSM_WIDTH // SSM_GROUPS),
                 ssm_norm_w.reshape(SSM_GROUPS, -1)).reshape(bsz, s, SSM_WIDTH).astype(h.dtype)
    return jnp.concatenate([att, y], axis=-1) @ w_out.astype(h.dtype)


def conformer_conv(h, pw1_w, pw1_b, dw_w, dw_b, ln_w, ln_b, pw2_w, pw2_b):
    u = h @ pw1_w.astype(h.dtype) + pw1_b.astype(h.dtype)
    u = u[..., :CONF_WIDTH] * jax.nn.sigmoid(u[..., CONF_WIDTH:])
    u = causal_dwconv(u, dw_w, dw_b)
    u = jax.nn.silu(layer_norm(u, ln_w, ln_b))
    return u @ pw2_w.astype(h.dtype) + pw2_b.astype(h.dtype)


def conv_ffn(h, up_w, conv_w, conv_b, down_w):
    u = causal_dwconv(h @ up_w.astype(h.dtype), conv_w, conv_b)
    gate, val = u[..., :FFN_DIM], u[..., FFN_DIM:]
    return (jax.nn.silu(gate) * val) @ down_w.astype(h.dtype)


def setup_inputs(seed: int = 0) -> dict:
    key = jax.random.key(seed)
    ks = iter(jax.random.split(key, 48))
    ne, no = (DEPTH + 1) // 2, DEPTH // 2
    f32 = jnp.float32

    def nrm(shape, scale):
        return jax.random.normal(next(ks), shape, f32) * scale

    def gain(shape):
        return 1.0 + nrm(shape, 0.02)

    dt0 = jnp.exp(jax.random.uniform(next(ks), (ne, SSM_HEADS), f32)
                  * (math.log(0.1) - math.log(0.001)) + math.log(0.001))
    inputs = {
        "x": nrm((BATCH, SEQ, D_MODEL), 1.0),
        "mix_norm_w": gain((ne, D_MODEL)),
        "w_in": nrm((ne, D_MODEL, IN_WIDTH), D_MODEL ** -0.5),
        "q_norm_w": gain((ne, ATT_HEAD_DIM)),
        "k_norm_w": gain((ne, ATT_HEAD_DIM)),
        "lambda_q1": nrm((ne, ATT_HEAD_DIM), 0.1),
        "lambda_k1": nrm((ne, ATT_HEAD_DIM), 0.1),
        "lambda_q2": nrm((ne, ATT_HEAD_DIM), 0.1),
        "lambda_k2": nrm((ne, ATT_HEAD_DIM), 0.1),
        "attn_subln_w": gain((ne, ATT_V_DIM)),
        "ssm_conv_w": nrm((ne, SSM_CONV, XBC_WIDTH), SSM_CONV ** -0.5),
        "ssm_conv_b": nrm((ne, XBC_WIDTH), 0.02),
        "ssm_dt_bias": dt0 + jnp.log(-jnp.expm1(-dt0)),
        "ssm_A_log": jnp.log(jax.random.uniform(next(ks), (ne, SSM_HEADS), f32, 1.0, 16.0)),
        "ssm_D": gain((ne, SSM_HEADS)),
        "ssm_norm_w": gain((ne, SSM_WIDTH)),
        "w_out": nrm((ne, ATT_WIDTH + SSM_WIDTH, D_MODEL), (ATT_WIDTH + SSM_WIDTH) ** -0.5),
        "conf_norm_w": gain((no, D_MODEL)),
        "conf_pw1_w": nrm((no, D_MODEL, 2 * CONF_WIDTH), D_MODEL ** -0.5),
        "conf_pw1_b": nrm((no, 2 * CONF_WIDTH), 0.02),
        "conf_dw_w": nrm((no, CONF_WIDTH_K, CONF_WIDTH), CONF_WIDTH_K ** -0.5),
        "conf_dw_b": nrm((no, CONF_WIDTH), 0.02),
        "conf_ln_w": gain((no, CONF_WIDTH)),
        "conf_ln_b": nrm((no, CONF_WIDTH), 0.02),
        "conf_pw2_w": nrm((no, CONF_WIDTH, D_MODEL), CONF_WIDTH ** -0.5),
        "conf_pw2_b": nrm((no, D_MODEL), 0.02),
        "ffn_norm_w": gain((DEPTH, D_MODEL)),
        "ffn_up_w": nrm((DEPTH, D_MODEL, 2 * FFN_DIM), D_MODEL ** -0.5),
        "ffn_conv_w": nrm((DEPTH, FFN_CONV, 2 * FFN_DIM), FFN_CONV ** -0.5),
        "ffn_conv_b": nrm((DEPTH, 2 * FFN_DIM), 0.02),
        "ffn_down_w": nrm((DEPTH, FFN_DIM, D_MODEL), FFN_DIM ** -0.5),
    }
    return inputs


def reference(x, mix_norm_w, w_in, q_norm_w, k_norm_w, lambda_q1, lambda_k1, lambda_q2,
              lambda_k2, attn_subln_w, ssm_conv_w, ssm_conv_b, ssm_dt_bias, ssm_A_log, ssm_D,
              ssm_norm_w, w_out, conf_norm_w, conf_pw1_w, conf_pw1_b, conf_dw_w, conf_dw_b,
              conf_ln_w, conf_ln_b, conf_pw2_w, conf_pw2_b, ffn_norm_w, ffn_up_w, ffn_conv_w,
              ffn_conv_b, ffn_down_w):
    for i in range(DEPTH):
        if i % 2 == 0:
            e = i // 2
            lam_init = 0.8 - 0.6 * math.exp(-0.3 * i)
            x = x + hybrid_mixer(rms_norm(x, mix_norm_w[e]), lam_init, w_in[e], q_norm_w[e],
                                 k_norm_w[e], lambda_q1[e], lambda_k1[e], lambda_q2[e],
                                 lambda_k2[e], attn_subln_w[e], ssm_conv_w[e], ssm_conv_b[e],
                                 ssm_dt_bias[e], ssm_A_log[e], ssm_D[e], ssm_norm_w[e], w_out[e])
        else:
            o = i // 2
            x = x + conformer_conv(rms_norm(x, conf_norm_w[o]), conf_pw1_w[o], conf_pw1_b[o],
                                   conf_dw_w[o], conf_dw_b[o], conf_ln_w[o], conf_ln_b[o],
                                   conf_pw2_w[o], conf_pw2_b[o])
        x = x + conv_ffn(rms_norm(x, ffn_norm_w[i]), ffn_up_w[i], ffn_conv_w[i],
                         ffn_conv_b[i], ffn_down_w[i])
    return x
```

```python
import numpy as np
from contextlib import ExitStack
import concourse.bass as bass
import concourse.mybir as mybir
from concourse.bass_utils import run_bass_kernel_spmd

F32 = mybir.dt.float32
BF16 = mybir.dt.bfloat16
U8 = mybir.dt.uint8
AF = mybir.ActivationFunctionType
ALU = mybir.AluOpType
AX = mybir.AxisListType

D = 1024
KC = 8
S = 4096
PRE = 1920
EXT = 2176
OWN0 = 128
FFN = 2816
NJ = 22
RMS_EPS = 1e-6
LN_EPS = 1e-5
IN_W = 3080
QOFF, KOFF, VOFF, ZOFF, XOFF, DTOFF = 0, 512, 1024, 1536, 2048, 3072

CTX_BLOCKS = [(0, 512, False), (512, 512, False), (1024, 512, False), (1536, 384, False),
              (1920, 512, True), (2432, 512, True), (2944, 384, True), (3328, 384, True),
              (3712, 384, True)]
OWN_BLOCKS = [(0, 128), (128, 512), (640, 512), (1152, 512), (1664, 512)]
FM_BLOCKS = [(0, 448), (448, 432), (880, 432), (1312, 432), (1744, 432)]

PV = {}
_o = 0
for _n, _w in [("mixw", 8), ("confw", 8), ("ffnw0", 8), ("ffnw1", 8), ("qg", 1), ("kg", 1),
               ("scw", 32), ("scb", 8), ("pw1b", 16), ("dww", 8 * 31), ("dwb", 8), ("lnw", 8),
               ("lnb", 8), ("pw2b", 8), ("fcw0", 44 * 3), ("fcw1", 44 * 3), ("fcb0", 44),
               ("fcb1", 44), ("flag", 1), ("kbias", 32)]:
    PV[_n] = _o
    _o += _w
NPV = _o
RB = {}
_o = 0
for _n, _w in [("subln", 128), ("dtb", 8), ("alog", 8), ("dD", 8), ("snw", 512),
               ("lq1", 64), ("lk1", 64), ("lq2", 64), ("lk2", 64)]:
    RB[_n] = _o
    _o += _w
NRB = _o
NCB = 4 * 128
NCF = 4 * 128 + 512


class Prog:
    MAXV = 30000

    def __init__(self, nc, es, need=None):
        self.nc = nc
        self.es = es
        self.pass2 = need is not None
        self.need = need if need is not None else set()
        self.newneed = set()
        self.E = {"pe": nc.tensor, "act": nc.scalar, "dve": nc.vector, "pool": nc.gpsimd,
                  "sp": nc.sync}
        self.sems = {}
        self.sigcnt = {e: 0 for e in self.E}
        self.icnt = {e: 0 for e in self.E}
        self.seen = {e: {} for e in self.E}
        self.opidx = 0
        self.st = {}
        self.slots = {}
        self.sigval = {}
        self.nsem = 0
        self.big = es.enter_context(nc.sbuf_tensor("big", [128, 207 * 1024], U8))
        self.off = 0
        self.ps = [es.enter_context(nc.psum_tensor(f"psb{i}", [128, 512], F32)) for i in range(8)]

    def alloc(self, shape, dt):
        n = int(np.prod(shape[1:]))
        bpe = {F32: 4, BF16: 2, U8: 1}[dt]
        nb = (n * bpe + 31) // 32 * 32
        assert self.off + nb <= 207 * 1024, f"SBUF overflow {self.off}+{nb}"
        v = self.big[:, self.off:self.off + nb]
        self.off += nb
        v = v[:, 0:n * bpe]
        if dt != U8:
            v = v.bitcast(dt)
        if len(shape) == 3:
            v = v.rearrange("p (a b) -> p a b", a=shape[1])
        elif len(shape) == 4:
            v = v.rearrange("p (a b c) -> p a b c", a=shape[1], b=shape[2])
        return v

    def view_at(self, off, shape, dt):
        n = int(np.prod(shape[1:]))
        bpe = {F32: 4, BF16: 2, U8: 1}[dt]
        v = self.big[:, off:off + n * bpe]
        if dt != U8:
            v = v.bitcast(dt)
        if len(shape) == 3:
            v = v.rearrange("p (a b) -> p a b", a=shape[1])
        return v

    def mark(self):
        return self.off

    def release(self, m):
        self.off = m

    def _sem(self, key):
        if key not in self.sems:
            self.sems[key] = self.es.enter_context(self.nc.semaphore(f"s{self.nsem}"))
            self.nsem += 1
        return self.sems[key]

    def _deps(self, reads, writes):
        deps = []
        for k in reads:
            s = self.st.get(k)
            if s and s[0]:
                deps.append(s[0])
        for k in writes:
            s = self.st.get(k)
            if s:
                if s[0]:
                    deps.append(s[0])
                deps.extend(s[1].values())
        return deps

    def _wait(self, eng, deps):
        h = self.E[eng]
        for d in deps:
            if d[0] == "c":
                _, q, qidx, qic = d
                if q == eng:
                    if eng == "pe":
                        continue
                    if self.icnt[eng] - qic > 2:
                        continue
                self.newneed.add(qidx)
                if self.pass2:
                    semkey, sem, val = self.sigval[qidx]
                    if self.seen[eng].get(semkey, 0) >= val:
                        continue
                    h.wait_ge(sem, val)
                    self.seen[eng][semkey] = val
            else:
                _, slot, val = d
                if self.seen[eng].get(("d", slot), 0) >= val:
                    continue
                if self.pass2:
                    h.wait_ge(self.slots[slot][0], val)
                self.seen[eng][("d", slot)] = val

    def _record(self, rec, eng, reads, writes):
        for k in reads:
            self.st.setdefault(k, [None, {}])[1][eng] = rec
        for k in writes:
            self.st[k] = [rec, {}]

    PSK = frozenset(f"ps{i}" for i in range(8))

    def op(self, eng, fn, reads=(), writes=()):
        if any(k in self.PSK for k in reads):
            writes = list(writes) + [k for k in reads if k in self.PSK]
            reads = [k for k in reads if k not in self.PSK]
        idx = self.opidx
        self.opidx += 1
        self._wait(eng, self._deps(reads, writes))
        ic = self.icnt[eng]
        self.icnt[eng] += 1
        if self.pass2:
            ins = fn(self.E[eng])
            if idx in self.need:
                self.sigcnt[eng] += 1
                n = self.sigcnt[eng]
                si = (n - 1) // self.MAXV
                val = (n - 1) % self.MAXV + 1
                sem = self._sem((eng, si))
                ins.then_inc(sem, 1)
                self.sigval[idx] = ((eng, si), sem, val)
        self._record(("c", eng, idx, ic), eng, reads, writes)

    def dma(self, q, out, in_, reads=(), writes=(), slot=None):
        self.opidx += 1
        self._wait(q, self._deps(reads, writes))
        self.icnt[q] += 1
        if slot not in self.slots:
            self.slots[slot] = [self._sem(("dma", slot)) if self.pass2 else None, 0]
        sl = self.slots[slot]
        sl[1] += 16
        if self.pass2:
            self.E[q].dma_start(out=out, in_=in_).then_inc(sl[0], 16)
        self._record(("d", slot, sl[1]), "dma:" + slot, reads, writes)

    def barrier(self, cst):
        ident = cst["ident"]
        scr = cst["scr"]
        self.op("pe", lambda e: e.matmul(self.ps[0][:, 0:1], ident[:, 0:128], ident[:, 0:1],
                                         start=True, stop=True),
                reads=["cb"], writes=["ps0", ("bar", "pe")])
        self.op("act", lambda e: e.activation(scr[:, 0:1], scr[:, 4:5], AF.Copy), writes=[("bar", "act")])
        for i, en in enumerate(("dve", "pool")):
            self.op(en, lambda e, i=i: e.memset(scr[:, 1 + i:2 + i], 0.0), writes=[("bar", en)])
        deps = [self.st[("bar", q)][0] for q in ("pe", "act", "dve", "pool")]
        deps += [("d", s, v[1]) for s, v in self.slots.items()]
        for e in self.E:
            self._wait(e, deps)
        self.st.clear()

    def finish(self):
        deps = [("d", s, v[1]) for s, v in self.slots.items()]
        for e in ("sp", "pool"):
            self._wait(e, deps)


    def mm(self, out, lhsT, rhs, start=True, stop=True, r=(), w=()):
        self.op("pe", lambda e: e.matmul(out, lhsT, rhs, start=start, stop=stop), r, w)

    def tr(self, out, in_, ident, r=(), w=()):
        self.op("pe", lambda e: e.transpose(out, in_, ident), r, w)

    def act(self, out, in_, func, r=(), w=(), bias=None, scale=None):
        kw = {}
        if bias is not None:
            kw["bias"] = bias
        if scale is not None:
            kw["scale"] = scale
        self.op("act", lambda e: e.activation(out, in_, func, **kw), r, w)

    def tt(self, out, a, b, op, r=(), w=(), eng="dve"):
        self.op(eng, lambda e: e.tensor_tensor(out, a, b, op), r, w)

    def ts(self, out, a, s1, s2, op0, op1=None, r=(), w=(), eng="dve"):
        if op1 is None:
            self.op(eng, lambda e: e.tensor_scalar(out, a, s1, None, op0), r, w)
        else:
            self.op(eng, lambda e: e.tensor_scalar(out, a, s1, s2, op0, op1), r, w)

    def stt(self, out, a, s, b, op0, op1, r=(), w=(), eng="dve"):
        self.op(eng, lambda e: e.scalar_tensor_tensor(out, a, s, b, op0, op1), r, w)

    def red(self, out, in_, r=(), w=(), eng="dve"):
        self.op(eng, lambda e: e.tensor_reduce(out, in_, AX.X, ALU.add), r, w)

    def cp(self, out, in_, r=(), w=(), eng="dve"):
        self.op(eng, lambda e: e.tensor_copy(out, in_), r, w)

    def ms(self, out, val, r=(), w=(), eng="dve"):
        self.op(eng, lambda e: e.memset(out, val), r, w)


class Buf:
    def __init__(self, P, shape, dt, key):
        self.ap = P.alloc(shape, dt)
        self.k = key


def bufs(P, n, shape, dt, key):
    return [Buf(P, shape, dt, f"{key}{i}") for i in range(n)]


class Rot:
    def __init__(self, items):
        self.items = items
        self.i = 0

    def next(self):
        x = self.items[self.i % len(self.items)]
        self.i += 1
        return x

def _build(cfg, need):
    nc = bass.Bass("TRN2", target_bir_lowering=False)

    def din(name, shape):
        return nc.dram_tensor(name, list(shape), F32, kind="ExternalInput").ap()

    xT = din("xT", [D, S])
    w_in = din("w_in", [D, IN_W])
    w_out = din("w_out", [D, D])
    pw1 = din("pw1", [D, 2 * D])
    pw2 = din("pw2", [D, D])
    upw = [din("up0", [D, 2 * FFN]), din("up1", [D, 2 * FFN])]
    dnw = [din("dn0", [FFN, D]), din("dn1", [FFN, D])]
    pvec = din("pvec", [128, NPV])
    rowbc = din("rowbc", [128, NRB])
    cbf = din("cbf", [128, NCB])
    cf32 = din("cf32", [128, NCF])
    yT = nc.dram_tensor("yT", [D, 2048], F32, kind="ExternalOutput").ap()
    stop_after = cfg.get("stop_after", "all")

    es = ExitStack()
    with es:
        P = Prog(nc, es, need)
        ps = P.ps
        K = [f"ps{i}" for i in range(8)]
        pv = P.alloc([128, NPV], F32)
        rb = P.alloc([128, NRB], F32)
        cb = P.alloc([128, NCB], BF16)
        sm = P.alloc([128, 16], F32)
        negA = P.alloc([128, 8], F32)
        sw8 = P.alloc([128, 128], F32)
        scr = P.alloc([128, 8], F32)
        hv = P.alloc([128, 24], F32)
        lt = P.alloc([128, 128], F32)
        ident, tri, cmean, bd64 = cb[:, 0:128], cb[:, 128:256], cb[:, 256:384], cb[:, 384:512]
        cst = {"ident": ident, "scr": scr}

        def pvc(name, i=0):
            c = PV[name] + i
            return pv[:, c:c + 1]

        def rbc(name, n):
            return rb[:, RB[name]:RB[name] + n]

        P.dma("sp", pv, pvec, w=["pv"], slot="pv") if False else P.dma("sp", pv, pvec, writes=["pv"], slot="pv")
        P.dma("sp", rb, rowbc, writes=["rb"], slot="rb")
        P.dma("pool", cb, cbf, writes=["cb"], slot="cb")
        P.ts(sm[:, 0:1], pvc("qg"), 0.125, None, ALU.mult, r=["pv"], w=["sm0"])
        P.tt(lt[:, 0:64], rbc("lq1", 64), rbc("lk1", 64), ALU.mult, r=["rb"], w=["lt"])
        P.tt(lt[:, 64:128], rbc("lq2", 64), rbc("lk2", 64), ALU.mult, r=["rb", "lt"], w=["lt"])
        P.red(sm[:, 2:4], lt.rearrange("p (a b) -> p a b", a=2), r=["lt"], w=["sm2"])
        P.act(sm[:, 4:6], sm[:, 2:4], AF.Exp, r=["sm2"], w=["sm4"])
        P.tt(sm[:, 6:7], sm[:, 5:6], sm[:, 4:5], ALU.subtract, r=["sm4"], w=["sm6"])
        P.ts(sm[:, 7:8], sm[:, 6:7], -0.2, None, ALU.add, r=["sm6"], w=["neglam"])
        neglam = sm[:, 7:8]
        P.act(negA, rbc("alog", 8), AF.Exp, r=["rb"], w=["negA0"])
        P.ts(negA, negA, -1.0, None, ALU.mult, r=["negA0"], w=["negA"])
        P.ts(sw8, rbc("subln", 128), 0.8, None, ALU.mult, r=["rb"], w=["sw8"])
        P.ts(hv[:, 0:8], pv[:, PV["pw1b"] + 8:PV["pw1b"] + 16], 0.5, None, ALU.mult, r=["pv"], w=["hv"])
        P.ts(hv[:, 8:16], pv[:, PV["lnw"]:PV["lnw"] + 8], 0.5, None, ALU.mult, r=["pv", "hv"], w=["hv"])
        P.ts(hv[:, 16:24], pv[:, PV["lnb"]:PV["lnb"] + 8], 0.5, None, ALU.mult, r=["pv", "hv"], w=["hv"])
        flag = pvc("flag")
        epsc = scr[:, 5:6]
        P.ms(epsc, RMS_EPS, w=["epsc"])
        onec = scr[:, 6:7]
        P.ms(onec, 1.0, w=["onec"])
        CK = ["pv", "rb", "cb", "sm0", "neglam", "negA", "sw8", "hv"]
        base = P.mark()
        TOP = 207 * 1024
        xT_v = xT.rearrange("(c p) t -> p c t", p=128)

        def wview(src):
            return src.rearrange("(c p) w -> p c w", p=128)

        def rmsnorm(xap, xkeys, N, wname, hT, hkey, sq, sqkey, rstd, rkey, psi):
            P.act(sq[:, :, 0:N], xap, AF.Square, r=xkeys, w=[sqkey])
            for c in range(KC):
                P.mm(ps[psi][:, 0:N], cmean, sq[:, c, 0:N], start=(c == 0), stop=(c == KC - 1),
                     r=[sqkey], w=[K[psi]])
            P.act(rstd[:, 0:N], ps[psi][:, 0:N], AF.Ln, r=[K[psi]], w=[rkey], bias=epsc)
            P.act(rstd[:, 0:N], rstd[:, 0:N], AF.Exp, r=[rkey], w=[rkey], scale=-0.5)
            for c in range(KC):
                P.stt(hT[:, c, 0:N], xap[:, c, :], pvc(wname, c), rstd[:, 0:N], ALU.mult, ALU.mult,
                      r=xkeys + [rkey], w=[hkey])

        P.off_top = TOP - 4 * EXT * 2
        attT = P.big[:, P.off_top:TOP].bitcast(BF16).rearrange("p (a b) -> p a b", a=4)

        def pass_a():
            KT = P.alloc([128, 4, S], BF16)
            V = P.alloc([128, 32, 4, 132], BF16)
            wq_off = P.off
            Wqkv = P.alloc([128, KC, 1536], BF16)
            xa = bufs(P, 2, [128, KC, 512], F32, "xa")
            xa_end = P.off
            hTb = bufs(P, 2, [128, KC, 512], BF16, "hT")
            rstd = P.alloc([128, 512], F32)
            QTb = bufs(P, 1, [128, 4, 512], BF16, "QT") * 2
            sq2 = Rot(bufs(P, 2, [128, 512], BF16, "sq2"))
            raw = Rot(bufs(P, 2, [128, 512], F32, "raw"))
            rs2 = Rot(bufs(P, 2, [128, 512], F32, "rs2"))
            pt = Rot(bufs(P, 6, [128, 512], BF16, "pt"))
            Ocp = P.alloc([128, 4, 512], F32)
            o1 = bufs(P, 4, [128, 128], F32, "o1")
            araw4 = bufs(P, 4, [128, 128], F32, "araw")
            asq = P.alloc([128, 128], F32)
            st = P.alloc([128, 32], F32)
            atok = bufs(P, 4, [128, 512], BF16, "atok")
            print('passA mem', P.off, P.off_top)
            assert P.off <= P.off_top, (P.off, P.off_top)
            for h_ in range(4):
                c0 = 512 + h_ * 128
                P.dma("pool", Wqkv[:, :, c0:c0 + 128], wview(w_in[:, c0:c0 + 128]), writes=[f"Wk{h_}"], slot=f"Wk{h_}")
            for i in (2, 0):
                P.dma("pool", Wqkv[:, :, i * 512:(i + 1) * 512], wview(w_in[:, i * 512:(i + 1) * 512]),
                      writes=[f"Wqkv{i}"], slot=f"Wqkv{i}")
            P.ms(V[:, :, :, 128:129], 1.0, w=["Vones"])
            psrot = Rot([0, 1, 3])
            orot = Rot([4, 6])
            qkrot = Rot([(3, 2), (1, 0)])

            def qk_a(hT, N, woff, wkey):
                a, b, c2 = sq2.next(), raw.next(), rs2.next()
                pr, pst = qkrot.next()
                for c in range(KC):
                    P.mm(ps[pr][:, 0:N], Wqkv[:, c, woff:woff + 128], hT.ap[:, c, 0:N], start=(c == 0),
                         stop=(c == KC - 1), r=[hT.k, wkey], w=[K[pr]])
                P.act(b.ap[:, 0:N], ps[pr][:, 0:N], AF.Copy, r=[K[pr]], w=[b.k])
                P.tt(a.ap[:, 0:N], b.ap[:, 0:N], b.ap[:, 0:N], ALU.mult, r=[b.k], w=[a.k])
                return (a, b, c2, pst)

            def qk_b(ctx, N, gcol, dst, dkey):
                a, b, c2, pst = ctx
                P.mm(ps[pst][:, 0:N], bd64, a.ap[:, 0:N], r=[a.k], w=[K[pst]])
                P.act(c2.ap[:, 0:N], ps[pst][:, 0:N], AF.Ln, r=[K[pst]], w=[c2.k], bias=epsc)
                P.act(c2.ap[:, 0:N], c2.ap[:, 0:N], AF.Exp, r=[c2.k], w=[c2.k], scale=-0.5)
                P.stt(dst, b.ap[:, 0:N], gcol, c2.ap[:, 0:N], ALU.mult, ALU.mult, r=[b.k, c2.k], w=[dkey])

            def pa_norm(bi):
                t0, N, own = CTX_BLOCKS[bi]
                xb, hT = xa[bi % 2], hTb[bi % 2]
                P.dma("sp", xb.ap[:, :, 0:N], xT_v[:, :, t0:t0 + N], writes=[xb.k], slot=xb.k)
                rmsnorm(xb.ap[:, :, 0:N], [xb.k], N, "mixw", hT.ap, hT.k, hT.ap, hT.k, rstd, "rstd", 2)

            pa_norm(0)
            deferred = []
            pending_tr = [None]
            for bi, (t0, N, own) in enumerate(CTX_BLOCKS):
                xb, hT = xa[bi % 2], hTb[bi % 2]
                NT, tile0 = N // 128, t0 // 128
                early_norm = (not own) and bi + 1 < len(CTX_BLOCKS)
                if early_norm:
                    pa_norm(bi + 1)
                e0 = t0 - PRE
                qt = QTb[bi % 2]
                qitems = [(512 + h * 128, f"Wk{h}", pvc("kg"), KT[:, h, t0:t0 + N], ("KT", h)) for h in range(4)]
                if own:
                    qitems += [(h * 128, "Wqkv0", sm[:, 0:1], qt.ap[:, h, 0:N], (qt.k, h)) for h in range(4)]
                vdone = [0]

                def vtile():
                    i = vdone[0]
                    if i >= NT:
                        return
                    vdone[0] += 1
                    pi = 4 + (i % 2)
                    for c in range(KC):
                        P.mm(ps[pi][:, 0:512], hT.ap[:, c, i * 128:(i + 1) * 128], Wqkv[:, c, 1024:1536],
                             start=(c == 0), stop=(c == KC - 1), r=[hT.k, "Wqkv2"], w=[K[pi]])
                    P.act(V[:, tile0 + i, :, 0:128], ps[pi][:, 0:512].rearrange("p (h v) -> p h v", h=4), AF.Copy,
                          r=[K[pi]], w=[("V", tile0 + i)])

                ctxs = {}
                nq = len(qitems)
                for t_ in range(nq + 1):
                    if t_ < nq:
                        ctxs[t_] = qk_a(hT, N, qitems[t_][0], qitems[t_][1])
                    if t_ >= 1:
                        woff_, wk_, gcol_, dst_, dkey_ = qitems[t_ - 1]
                        qk_b(ctxs.pop(t_ - 1), N, gcol_, dst_, dkey_)
                        vtile()
                        if deferred:
                            f_, a_ = deferred.pop(0)
                            f_(a_)
                while vdone[0] < NT:
                    vtile()
                while deferred:
                    f_, a_ = deferred.pop(0)
                    f_(a_)
                if pending_tr[0] is not None:
                    pending_tr[0]()
                    pending_tr[0] = None
                if bi + 1 < len(CTX_BLOCKS):
                    if not early_norm:
                        pa_norm(bi + 1)
                else:
                    o_wz = base + KC * EXT * 4
                    o_wo = o_wz + KC * 1544 * 2
                    o_cf = o_wo + KC * 1024 * 2
                    assert o_wz >= wq_off and o_cf + NCF * 4 <= xa_end, (o_wz, wq_off, o_cf, xa_end)
                    dead = ["Wqkv0", "Wk0", "Wk1", "Wk2", "Wk3", "Wqkv2", xa[0].k, xa[1].k]
                    WzP = P.view_at(o_wz, [128, KC, 1544], BF16)
                    WoP = P.view_at(o_wo, [128, KC, 1024], BF16)
                    cfP = P.view_at(o_cf, [128, NCF], F32)
                    P.dma("pool", WzP[:, :, 512:1024], wview(w_in[:, 2048:2560]), writes=["pfB"] + dead, slot="Wz1")
                    P.dma("pool", WzP[:, :, 0:512], wview(w_in[:, 1536:2048]), writes=["pfB"] + dead, slot="Wz0")
                    P.dma("pool", WzP[:, :, 1024:1544], wview(w_in[:, 2560:3080]), writes=["pfB"] + dead, slot="Wz2")
                    P.dma("pool", WoP, wview(w_out), writes=["pfB"] + dead, slot="Wo")
                    P.dma("sp", cfP, cf32, writes=["pfB"] + dead, slot="cf")
                if not own:
                    continue
                nkt = tile0 + NT
                ats = [atok[i] for i in range(NT)]
                sbank = [Rot([0, 1]), Rot([2, 3])]
                for h in range(4):

                    def score(j, h=h):
                        qs = max(0, j - tile0) * 128
                        res = []
                        for c in range(2):
                            r0 = 64 * c
                            psi = sbank[c].next()
                            p_ = pt.next()
                            P.mm(ps[psi][:, qs:N], KT[r0:r0 + 64, h, j * 128:(j + 1) * 128],
                                 qt.ap[r0:r0 + 64, h, qs:N], r=[("KT", h), (qt.k, h)], w=[K[psi]])
                            res.append((psi, p_))
                        for c in range(2):
                            psi, p_ = res[c]
                            P.act(p_.ap[:, qs:N], ps[psi][:, qs:N], AF.Exp, r=[K[psi]], w=[p_.k],
                                  bias=pvc("kbias", j))
                            if j >= tile0:
                                P.tt(p_.ap[:, qs:qs + 128], p_.ap[:, qs:qs + 128], tri, ALU.mult,
                                     r=[p_.k], w=[p_.k])
                        return [p for _, p in res]

                    def av(j, pts, h=h):
                        for c in range(2):
                            for i in range(NT):
                                if tile0 + i < j:
                                    continue
                                bank, col = 4 + 2 * c + i // 2, (i % 2) * 256
                                P.mm(ps[bank][:, col:col + 129], pts[c].ap[:, i * 128:(i + 1) * 128],
                                     V[:, j, h, 0:129], start=(j == 0 and i % 2 == 0), stop=(j == tile0 + i),
                                     r=[pts[c].k, ("V", j), "Vones"], w=[K[bank]])

                    pend = []
                    for j in range(nkt):
                        if j >= 2 and deferred:
                            f_, a_ = deferred.pop(0)
                            f_(a_)
                        pend.append((j, score(j)))
                        if len(pend) > 1:
                            av(*pend.pop(0))
                    while pend:
                        av(*pend.pop(0))
                    nb_ = (NT + 1) // 2
                    for c in range(2):
                        for bb in range(nb_):
                            bank = 4 + 2 * c + bb
                            P.cp(Ocp[:, 2 * c + bb, 0:385], ps[bank][:, 0:385], r=[K[bank]], w=[("Ocp", 2 * c + bb)])
                    def stage_a(i, h=h):
                        ar = araw4[i]
                        for c in range(2):
                            ok = ("Ocp", 2 * c + i // 2)
                            src = Ocp[:, 2 * c + i // 2, :]
                            col = (i % 2) * 256
                            sc = 4 * c + i
                            P.ts(st[:, sc:sc + 1], src[:, col + 128:col + 129], 1e-30, None, ALU.add, r=[ok],
                                 w=[("st", sc)])
                            P.op("dve", lambda e, sc=sc: e.reciprocal(st[:, sc:sc + 1], st[:, sc:sc + 1]),
                                 [("st", sc)], [("st", sc)])
                            if c == 0:
                                P.ts(o1[i].ap, src[:, col:col + 128], st[:, sc:sc + 1], None, ALU.mult,
                                     r=[ok, ("st", sc)], w=[o1[i].k])
                                continue
                            s2 = 8 + sc
                            s3 = 16 + sc
                            P.tt(st[:, s2:s2 + 1], st[:, sc:sc + 1], neglam, ALU.mult, r=[("st", sc)],
                                 w=[("st", s2)])
                            P.stt(ar.ap, src[:, col:col + 128], st[:, s2:s2 + 1], o1[i].ap, ALU.mult,
                                  ALU.add, r=[ok, ("st", s2), o1[i].k], w=[ar.k])
                            P.tt(asq, ar.ap, ar.ap, ALU.mult, r=[ar.k], w=["asq"])
                            P.red(st[:, s3:s3 + 1], asq, r=["asq"], w=[("st", s3)])
                            P.ts(st[:, s3:s3 + 1], st[:, s3:s3 + 1], 1.0 / 128, RMS_EPS, ALU.mult, ALU.add,
                                 r=[("st", s3)], w=[("st", s3)])

                    def stage_b(i, h=h):
                        ar = araw4[i]
                        s3 = 16 + 4 + i
                        P.act(st[:, s3:s3 + 1], st[:, s3:s3 + 1], AF.Ln, r=[("st", s3)], w=[("st", s3)])
                        P.act(st[:, s3:s3 + 1], st[:, s3:s3 + 1], AF.Exp, r=[("st", s3)], w=[("st", s3)],
                              scale=-0.5)
                        P.stt(ats[i].ap[:, h * 128:(h + 1) * 128], ar.ap, st[:, s3:s3 + 1], sw8, ALU.mult,
                              ALU.mult, r=[ar.k, ("st", s3)], w=[ats[i].k])

                    order = []
                    for t_ in range(NT + 2):
                        if t_ < NT:
                            order.append((stage_a, t_))
                        if 0 <= t_ - 2 < NT:
                            order.append((stage_b, t_ - 2))
                    deferred.extend(order)
                def do_tr(NT=NT, e0=e0):
                    for i in range(NT):
                        tb = 2 + (i % 2)
                        pbf = ps[tb][:, :].bitcast(BF16)
                        for h in range(4):
                            P.tr(pbf[:, h * 128:(h + 1) * 128], atok[i].ap[:, h * 128:(h + 1) * 128], ident,
                                 r=[atok[i].k], w=[K[tb]])
                        P.cp(attT[:, :, e0 + i * 128:e0 + (i + 1) * 128],
                             pbf[:, 0:512].rearrange("p (h t) -> p h t", h=4), r=[K[tb]],
                             w=[("attT", e0 // 128 + i)])
                pending_tr[0] = do_tr
            xres_v = P.view_at(base, [128, KC, EXT], F32)
            N0 = CTX_BLOCKS[0][1]
            deadkv = [("KT", h_) for h_ in range(4)] + [("V", j_) for j_ in range(32)] + ["Vones"]
            P.dma("sp", xres_v[:, :, 0:N0], xT_v[:, :, 0:N0], writes=["pfX"] + deadkv, slot="xtmp0")
            while deferred:
                f_, a_ = deferred.pop(0)
                f_(a_)
            if pending_tr[0] is not None:
                pending_tr[0]()

        pass_a()
        P.barrier(cst)
        P.release(base)
        xres = P.alloc([128, KC, EXT], F32)
        m_x = P.mark()
        if stop_after == "A":
            for c in range(4):
                P.cp(xres[:, c, :], attT[:, c, :], w=["xres"])
            P.dma("sp", yT.rearrange("(c p) t -> p c t", p=128), xres[:, :, OWN0:EXT], reads=["xres"],
                  slot="out")
            P.finish()
            return nc, P

        def ffn_prefetch(l, dead):
            o_g = base + KC * EXT * 4
            gj0 = 6
            WgP = P.view_at(o_g, [128, KC, 768], BF16)
            WvP = P.view_at(o_g + KC * 768 * 2, [128, KC, 768], BF16)
            P.dma("pool", WgP[:, :, 0:gj0 * 128], wview(upw[l][:, 0:gj0 * 128]), writes=["pfF"] + dead, slot="Wg")
            P.dma("pool", WvP[:, :, 0:gj0 * 128], wview(upw[l][:, FFN:FFN + gj0 * 128]), writes=["pfF"] + dead, slot="Wv")

        def pass_b():
            Wz = P.alloc([128, KC, 1544], BF16)
            Wo = P.alloc([128, KC, 1024], BF16)
            cf = P.alloc([128, NCF], F32)
            U32, SL32, ones32, id32, negrep = cf[:, 0:128], cf[:, 128:256], cf[:, 256:384], cf[:, 384:512], cf[:, 512:1024]
            hTb = bufs(P, 1, [128, KC, 512], BF16, "hT") * 2
            rstd = P.alloc([128, 512], F32)
            us = [P.alloc([128, 515], BF16) for _ in range(2)]
            dgall = P.alloc([128, 32, 128], BF16)
            xbcTs = [P.alloc([128, 8, 512], BF16) for _ in range(2)]
            ucar = P.alloc([128, 8, 3], BF16)
            dtbs = [P.alloc([128, 4, 8], F32) for _ in range(2)]
            adts = [P.alloc([128, 4, 8], F32) for _ in range(2)]
            t8 = P.alloc([128, 4, 8], F32)
            Rs = [P.alloc([128, 8, 128], F32)] * 2
            MTs = [P.alloc([128, 8, 128], BF16)] * 2
            xsBs = [P.alloc([128, 768], BF16) for _ in range(2)]
            xdts = [P.alloc([128, 512], BF16)] * 2
            xdds = [P.alloc([128, 512], BF16) for _ in range(2)]
            Sst = P.alloc([128, 512], F32)
            Sbf = P.alloc([128, 512], BF16)
            szb = P.alloc([128, 4, 512], BF16)
            Dg = P.alloc([128, 8, 128], BF16)
            y1s = [P.alloc([128, 512], F32) for _ in range(2)]
            y2s = [rstd] * 2
            ytoks = [P.alloc([128, 512], BF16) for _ in range(2)]
            yT_ = P.alloc([128, 4, 512], BF16)
            c8s = [P.alloc([128, 48], F32) for _ in range(2)]
            print("passB mem", P.off, P.off_top)
            assert P.off <= P.off_top, (P.off, P.off_top)
            assert m_x == base + KC * EXT * 4
            P.ms(ucar, 0.0, w=["ucar"])
            P.tt(dgall, ident.unsqueeze(1).broadcast_to([128, 32, 128]),
                 pv[:, PV["scw"]:PV["scw"] + 32].unsqueeze(2).broadcast_to([128, 32, 128]), ALU.mult, w=["dgall"])
            P.tt(Dg, ident.unsqueeze(1).broadcast_to([128, 8, 128]), rbc("dD", 8).unsqueeze(2).broadcast_to([128, 8, 128]),
                 ALU.mult, w=["Dg"])
            P.ms(Sst, 0.0, w=["S"])
            P.ms(Sbf, 0.0, w=["Sbf"])
            WZ = ["Wz0", "Wz1", "Wz2"]
            def blk(bi):
                t0, N, own = CTX_BLOCKS[bi]
                hT = hTb[bi % 2]
                return t0, N, own, N // 128, t0 // 128, t0 - PRE, hT, xbcTs[bi % 2], dtbs[bi % 2], adts[bi % 2], bi % 2

            def xkey(bi):
                t0, N, own = CTX_BLOCKS[bi]
                return ("xres", t0 - PRE) if own else f"xtmp{bi % 2}"

            def front_norm(bi):
                t0, N, own, NT, tile0, e0, hT, xbcT, dtb, adt, par = blk(bi)
                if own:
                    xap = xres[:, :, e0:e0 + N]
                    xk = ("xres", e0)
                    P.dma("sp", xap, xT_v[:, :, t0:t0 + N], writes=[xk, "xtmp0", "xtmp1"], slot=f"xr{e0}")
                else:
                    o_ = (bi % 2) * 512
                    xap = xres[:, :, o_:o_ + N]
                    xk = f"xtmp{bi % 2}"
                    if bi > 0:
                        P.dma("sp", xap, xT_v[:, :, t0:t0 + N], writes=[xk], slot=xk)
                rmsnorm(xap, [xk], N, "mixw", hT.ap, hT.k, hT.ap, hT.k, rstd, "rstd", 0)

            def fx_a(bi, ch):
                t0, N, own, NT, tile0, e0, hT, xbcT, dtb, adt, par = blk(bi)
                pi = (1, 7)[ch % 2]
                u, uk = us[ch % 2], f"u{ch % 2}"
                for c in range(KC):
                    P.mm(ps[pi][:, 0:N], Wz[:, c, 512 + ch * 128:512 + (ch + 1) * 128], hT.ap[:, c, 0:N],
                         start=(c == 0), stop=(c == KC - 1), r=[hT.k] + WZ, w=[K[pi]])
                P.cp(u[:, 0:3], ucar[:, ch, :], r=["ucar"], w=[uk])
                P.act(u[:, 3:3 + N], ps[pi][:, 0:N], AF.Copy, r=[K[pi], uk], w=[uk])
                if t0 == PRE:
                    P.ts(u[:, 3:3 + OWN0], u[:, 3:3 + OWN0], flag, None, ALU.mult, r=[uk], w=[uk])
                P.cp(ucar[:, ch, :], u[:, N:N + 3], r=[uk], w=["ucar"])

            def fx_b(bi, ch):
                t0, N, own, NT, tile0, e0, hT, xbcT, dtb, adt, par = blk(bi)
                pc = 5 + ch % 2
                u, uk = us[ch % 2], f"u{ch % 2}"
                for k_ in range(4):
                    P.mm(ps[pc][:, 0:N], dgall[:, ch * 4 + k_, :], u[:, k_:k_ + N], start=(k_ == 0), stop=(k_ == 3),
                         r=["dgall", uk], w=[K[pc]])
                P.act(xbcT[:, ch, 0:N], ps[pc][:, 0:N], AF.Silu, r=[K[pc]], w=[("xbc", par, ch)], bias=pvc("scb", ch))

            def fx_stages(bi):
                nch = 8 if CTX_BLOCKS[bi][2] else 6
                out = []
                for t_ in range(nch + 1):
                    if t_ < nch:
                        out.append((fx_a, bi, t_))
                    if t_ >= 1:
                        out.append((fx_b, bi, t_ - 1))
                return out

            def front_dt(bi):
                t0, N, own, NT, tile0, e0, hT, xbcT, dtb, adt, par = blk(bi)
                for i in range(NT):
                    for c in range(KC):
                        P.mm(ps[4][:, i * 8:(i + 1) * 8], hT.ap[:, c, i * 128:(i + 1) * 128], Wz[:, c, 1536:1544],
                             start=(c == 0), stop=(c == KC - 1), r=[hT.k] + WZ, w=[K[4]])
                v3 = ps[4][:, 0:NT * 8].rearrange("p (i h) -> p i h", h=8)
                P.tt(dtb[:, 0:NT, :], v3, rbc("dtb", 8).unsqueeze(1).broadcast_to([128, NT, 8]), ALU.add,
                     r=[K[4]], w=[f"dtb{par}"])
                P.act(t8[:, 0:NT, :], dtb[:, 0:NT, :], AF.Abs, r=[f"dtb{par}"], w=["t8"])
                P.act(t8[:, 0:NT, :], t8[:, 0:NT, :], AF.Exp, r=["t8"], w=["t8"], scale=-1.0)
                P.act(t8[:, 0:NT, :], t8[:, 0:NT, :], AF.Ln, r=["t8"], w=["t8"], bias=onec)
                P.stt(dtb[:, 0:NT, :], dtb[:, 0:NT, :], 0.0, t8[:, 0:NT, :], ALU.max, ALU.add, r=[f"dtb{par}", "t8"],
                      w=[f"dt{par}"])
                P.tt(adt[:, 0:NT, :], dtb[:, 0:NT, :], negA.unsqueeze(1).broadcast_to([128, NT, 8]), ALU.mult,
                     r=[f"dt{par}"], w=[f"adt{par}"])

            def front_z(bi):
                t0, N, own, NT, tile0, e0, hT, xbcT, dtb, adt, par = blk(bi)
                for i in range(NT):
                    for c in range(KC):
                        P.mm(ps[2][:, 0:512], hT.ap[:, c, i * 128:(i + 1) * 128], Wz[:, c, 0:512], start=(c == 0),
                             stop=(c == KC - 1), r=[hT.k] + WZ, w=[K[2]])
                    P.act(szb[:, i, :], ps[2][:, 0:512], AF.Silu, r=[K[2]], w=["sz"])

            def chunk_pre(bi, i, q):
                t0, N, own, NT, tile0, e0, hT, xbcT, dtb, adt, par = blk(bi)
                ti = tile0 + i
                full = ti >= 15
                cs = slice(i * 128, (i + 1) * 128)
                xsB, c8, xdd, xdt, R, MT = xsBs[q], c8s[q], xdds[q], xdts[q], Rs[q], MTs[q]
                kx, kc, kd, kt, kR, kM = f"xsB{q}", f"c8{q}", f"xdd{q}", "xdt", "R", "MT"
                if full:
                    P.tt(R, adt[:, i, :].unsqueeze(2).broadcast_to([128, 8, 128]),
                         U32.unsqueeze(1).broadcast_to([128, 8, 128]), ALU.mult, r=[f"adt{par}", "cf"], w=[kR])
                pbf = ps[3][:, :].bitcast(BF16)
                for q_ in range(6):
                    P.tr(pbf[:, q_ * 128:(q_ + 1) * 128], xbcT[:, q_, cs], ident, r=[("xbc", par, q_)], w=[K[3]])
                P.cp(xsB, pbf[:, 0:768], r=[K[3]], w=[kx])
                P.mm(ps[4][:, 64:72], U32, adt[:, i, :], r=[f"adt{par}", "cf"], w=[K[4]])
                P.mm(ps[4][:, 72:80], SL32, adt[:, i, :], r=[f"adt{par}", "cf"], w=[K[4]])
                P.mm(ps[4][:, 80:88], ones32, adt[:, i, :], r=[f"adt{par}", "cf"], w=[K[4]])
                if full:
                    for g in range(2):
                        P.mm(ps[4][:, 128 + g * 128:256 + g * 128], xbcT[:, 4 + g, cs], xbcT[:, 6 + g, cs],
                             r=[("xbc", par, 4 + g), ("xbc", par, 6 + g)], w=[K[4]])
                    Rf0 = R.rearrange("p h l -> p (h l)")
                    for hf in range(2):
                        P.mm(ps[5 + hf][:, 0:512], ones32, Rf0[:, hf * 512:(hf + 1) * 512], start=True, stop=False,
                             r=[kR, "cf"], w=[K[5 + hf]])
                        P.mm(ps[5 + hf][:, 0:512], id32, negrep, start=False, stop=True, r=["cf"], w=[K[5 + hf]])
                P.act(c8[:, 0:24], ps[4][:, 64:88], AF.Exp, r=[K[4]], w=[kc + "e"])
                P.ts(c8[:, 24:32], ps[4][:, 64:72], -1.0, None, ALU.mult, r=[K[4]], w=[kc + "n"])
                P.tt(c8[:, 32:40], c8[:, 8:16], dtb[:, i, :], ALU.mult, r=[kc + "e", f"dt{par}"], w=[kc + "w"])
                xs3 = xsB[:, 0:512].rearrange("p (h d) -> p h d", h=8)
                P.tt(xdd.rearrange("p (h d) -> p h d", h=8), xs3, c8[:, 32:40].unsqueeze(2).broadcast_to([128, 8, 64]),
                     ALU.mult, r=[kx, kc + "w"], w=[kd])
                if not full:
                    return
                for h_ in range(8):
                    P.act(R[:, h_, :], ps[5 + h_ // 4][:, (h_ % 4) * 128:(h_ % 4 + 1) * 128], AF.Exp,
                          r=[K[5 + h_ // 4], kc + "n"], w=[kR], bias=c8[:, 24 + h_:25 + h_])
                cb4 = ps[4][:, 128:384].rearrange("p (g l) -> p g l", g=2).unsqueeze(2).broadcast_to([128, 2, 4, 128])
                P.tt(MT.rearrange("p (g r) l -> p g r l", g=2), R.rearrange("p (g r) l -> p g r l", g=2), cb4,
                     ALU.mult, r=[kR, K[4]], w=[kM])
                P.tt(xdt.rearrange("p (h d) -> p h d", h=8), xs3, dtb[:, i, :].unsqueeze(2).broadcast_to([128, 8, 64]),
                     ALU.mult, r=[kx, f"dt{par}"], w=[kt])
                pd = (0, 2)[q]
                for h_ in range(8):
                    P.mm(ps[pd][:, h_ * 64:(h_ + 1) * 64], MT[:, h_, :], xdt[:, h_ * 64:(h_ + 1) * 64], start=True,
                         stop=False, r=[kM, kt], w=[K[pd]])
                    P.mm(ps[pd][:, h_ * 64:(h_ + 1) * 64], Dg[:, h_, :], xsB[:, h_ * 64:(h_ + 1) * 64], start=False,
                         stop=True, r=["Dg", kx], w=[K[pd]])

            def chunk_post(bi, i, q):
                t0, N, own, NT, tile0, e0, hT, xbcT, dtb, adt, par = blk(bi)
                ti = tile0 + i
                full = ti >= 15
                cs = slice(i * 128, (i + 1) * 128)
                xsB, c8, xdd, y1, y2, ytok = xsBs[q], c8s[q], xdds[q], y1s[q], y2s[q], ytoks[q]
                kx, kc, kd, k1, k2, ky = f"xsB{q}", f"c8{q}", f"xdd{q}", f"y1{q}", "rstd", f"ytok{q}"
                pd = (0, 2)[q]
                for g in range(2):
                    P.mm(ps[7][:, g * 256:(g + 1) * 256], xsB[:, 512 + g * 128:512 + (g + 1) * 128],
                         xdd[:, g * 256:(g + 1) * 256], r=[kx, kd], w=[K[7]])
                if full:
                    for g in range(2):
                        P.mm(ps[1][:, g * 256:(g + 1) * 256], xbcT[:, 6 + g, cs], Sbf[:, g * 256:(g + 1) * 256],
                             r=[("xbc", par, 6 + g), "Sbf"], w=[K[1]])
                P.tt(Sst.rearrange("p (h d) -> p h d", h=8), Sst.rearrange("p (h d) -> p h d", h=8),
                     c8[:, 16:24].unsqueeze(2).broadcast_to([128, 8, 64]), ALU.mult, r=["S", kc + "e"], w=["S"])
                P.tt(Sst, Sst, ps[7][:, 0:512], ALU.add, r=["S", K[7]], w=["S"])
                if ti == 15:
                    P.ts(Sst, Sst, flag, None, ALU.mult, r=["S"], w=["S"])
                if ti >= 14:
                    P.act(Sbf, Sst, AF.Copy, r=["S"], w=["Sbf"])
                if not full:
                    return
                P.tt(y1.rearrange("p (h d) -> p h d", h=8), ps[1][:, 0:512].rearrange("p (h d) -> p h d", h=8),
                     c8[:, 0:8].unsqueeze(2).broadcast_to([128, 8, 64]), ALU.mult, r=[K[1], kc + "e"], w=[k1])
                P.tt(y1, y1, ps[pd][:, 0:512], ALU.add, r=[k1, K[pd]], w=[k1])
                P.tt(y1, y1, szb[:, i, :], ALU.mult, r=[k1, "sz"], w=[k1])
                P.tt(y2, y1, y1, ALU.mult, r=[k1], w=[k2])
                P.red(c8[:, 40:42], y2.rearrange("p (g f) -> p g f", g=2), r=[k2], w=[kc + "r"])
                P.ts(c8[:, 40:42], c8[:, 40:42], 1.0 / 256, RMS_EPS, ALU.mult, ALU.add, r=[kc + "r"], w=[kc + "r"])

            def chunk_post_b(bi, i, q):
                t0, N, own, NT, tile0, e0, hT, xbcT, dtb, adt, par = blk(bi)
                if tile0 + i < 15:
                    return
                cs = slice(i * 128, (i + 1) * 128)
                c8, y1, ytok = c8s[q], y1s[q], ytoks[q]
                kc, k1, ky = f"c8{q}", f"y1{q}", f"ytok{q}"
                P.act(c8[:, 40:42], c8[:, 40:42], AF.Ln, r=[kc + "r"], w=[kc + "r"])
                P.act(c8[:, 40:42], c8[:, 40:42], AF.Exp, r=[kc + "r"], w=[kc + "r"], scale=-0.5)
                for g in range(2):
                    P.stt(ytok[:, g * 256:(g + 1) * 256], y1[:, g * 256:(g + 1) * 256], c8[:, 40 + g:41 + g],
                          rb[:, RB["snw"] + g * 256:RB["snw"] + (g + 1) * 256], ALU.mult, ALU.mult, r=[k1, kc + "r"],
                          w=[ky])
                pbf2 = ps[3][:, :].bitcast(BF16)
                for q_ in range(4):
                    P.tr(pbf2[:, q_ * 128:(q_ + 1) * 128], ytok[:, q_ * 128:(q_ + 1) * 128], ident, r=[ky],
                         w=[K[3]])
                P.act(yT_[:, :, cs], pbf2[:, 0:512].rearrange("p (h t) -> p h t", h=4), AF.Copy, r=[K[3]],
                      w=["yT"])

            def outproj(bi):
                t0, N, own, NT, tile0, e0, hT, xbcT, dtb, adt, par = blk(bi)
                for m in range(KC):
                    pi = m % 2
                    for c in range(8):
                        rhs = attT[:, c, e0:e0 + N] if c < 4 else yT_[:, c - 4, 0:N]
                        P.mm(ps[pi][:, 0:N], Wo[:, c, m * 128:(m + 1) * 128], rhs, start=(c == 0), stop=(c == 7),
                             r=["Wo", "yT"], w=[K[pi]])
                    P.tt(xres[:, m, e0:e0 + N], xres[:, m, e0:e0 + N], ps[pi][:, 0:N], ALU.add, r=[K[pi], xkey(bi)], w=[xkey(bi)])


            nb = len(CTX_BLOCKS)
            gq = [0]
            front_norm(0)
            for f_, b_, c_ in fx_stages(0):
                f_(b_, c_)
            front_dt(0)
            for bi in range(nb):
                t0, N, own = CTX_BLOCKS[bi]
                NT = N // 128
                nxt = bi + 1 if bi + 1 < nb else None
                nchn = 0
                if nxt is None:
                    ffn_prefetch(0, ["Wz0", "Wz1", "Wz2"])
                if nxt is not None:
                    front_norm(nxt)
                    nchn = 8 if CTX_BLOCKS[nxt][2] else 6
                qs_ = []
                for i in range(NT):
                    qs_.append(gq[0] % 2)
                    gq[0] += 1
                chunk_pre(bi, 0, qs_[0])
                fxq = fx_stages(nxt) if nxt is not None else []
                nst = len(fxq)
                done = 0
                pend_b = []
                for i in range(NT):
                    if i + 1 < NT:
                        chunk_pre(bi, i + 1, qs_[i + 1])
                    chunk_post(bi, i, qs_[i])
                    upto = (nst * (i + 1)) // NT
                    while done < upto:
                        f_, b_, c_ = fxq[done]
                        f_(b_, c_)
                        done += 1
                    while pend_b:
                        chunk_post_b(*pend_b.pop(0))
                    pend_b.append((bi, i, qs_[i]))
                while pend_b:
                    chunk_post_b(*pend_b.pop(0))
                if nxt is not None:
                    if CTX_BLOCKS[nxt][2]:
                        front_z(nxt)
                    front_dt(nxt)
                if own:
                    outproj(bi)

        pass_b()
        P.barrier(cst)
        P.release(m_x)
        P.off_top = TOP

        def dump_out():
            P.dma("sp", yT.rearrange("(c p) t -> p c t", p=128), xres[:, :, OWN0:EXT], slot="out")
            P.finish()

        if stop_after == "B":
            dump_out()
            return nc, P

        GROUPS = [(0, 6), (6, 6), (12, 5), (17, 5)]

        def ffn(l):
            m0 = P.mark()
            assert m0 == base + KC * EXT * 4
            Wg = P.alloc([128, KC, 768], BF16)
            Wv = P.alloc([128, KC, 768], BF16)
            hT = P.alloc([128, KC, EXT], BF16)
            gbuf = P.alloc([128, 6 * EXT], BF16)
            gT = gbuf.rearrange("p (a b) -> p a b", a=6)
            sq = gbuf[:, 0:4096].rearrange("p (a b) -> p a b", a=8)
            Wd = P.alloc([128, 6, 1024], BF16)
            rstds = [P.alloc([128, 512], F32) for _ in range(2)]
            nrm = 0
            dgs = [P.alloc([128, 6, 128], BF16) for _ in range(2)]
            sets = []
            for i in range(2):
                sets.append(dict(ug=P.alloc([128, 514], BF16), uv=P.alloc([128, 514], BF16), th=P.alloc([128, 512], F32),
                                 i=i, banks=(0, 1, 2, 3) if i == 0 else (4, 5, 6, 7)))
            wn = f"ffnw{l}"
            for (e0, N) in FM_BLOCKS:
                rmsnorm(xres[:, :, e0:e0 + N], ["xres"], N, wn, hT[:, :, e0:e0 + N], ("hTall", e0), hT[:, :, e0:e0 + N],
                        ("hTall", e0), rstds[nrm % 2], f"rstd{nrm % 2}", (6, 7)[nrm % 2])
                nrm += 1
            cnt = 0
            for (j0, gj) in GROUPS:
                if j0 > 0:
                    P.dma("pool", Wg[:, :, 0:gj * 128], wview(upw[l][:, j0 * 128:(j0 + gj) * 128]), writes=["Wg"], slot="Wg")
                    P.dma("pool", Wv[:, :, 0:gj * 128], wview(upw[l][:, FFN + j0 * 128:FFN + (j0 + gj) * 128]), writes=["Wv"],
                          slot="Wv")
                P.dma("pool", Wd[:, 0:gj, :], dnw[l][j0 * 128:(j0 + gj) * 128, :].rearrange("(j p) d -> p j d", p=128),
                      writes=["Wd"], slot="Wd")
                items = []
                for jj in range(gj):
                    for (e0, N) in FM_BLOCKS:
                        items.append((jj, j0 + jj, e0, N))
                state = {"prev": None}

                def stage_a(k):
                    jj, j, e0, N = items[k]
                    S_ = sets[(cnt0 + k) % 2]
                    si = S_["i"]
                    pg, pvv, cg, cv = S_["banks"]
                    if e0 == FM_BLOCKS[0][0]:
                        dg, dk = dgs[j % 2], f"dg{j % 2}"
                        for half, jo in ((0, j), (1, 22 + j)):
                            wc = PV[f"fcw{l}"] + jo * 3
                            P.tt(dg[:, half * 3:half * 3 + 3, :], ident.unsqueeze(1).broadcast_to([128, 3, 128]),
                                 pv[:, wc:wc + 3].unsqueeze(2).broadcast_to([128, 3, 128]), ALU.mult, r=[dk], w=[dk])
                        state["prev"] = None
                    for c in range(KC):
                        P.mm(ps[pg][:, 0:N], Wg[:, c, jj * 128:(jj + 1) * 128], hT[:, c, e0:e0 + N], start=(c == 0),
                             stop=(c == KC - 1), r=[("hTall", e0), "Wg"], w=[K[pg]])
                    for c in range(KC):
                        P.mm(ps[pvv][:, 0:N], Wv[:, c, jj * 128:(jj + 1) * 128], hT[:, c, e0:e0 + N], start=(c == 0),
                             stop=(c == KC - 1), r=[("hTall", e0), "Wv"], w=[K[pvv]])
                    prev = state["prev"]
                    for (un, pi) in (("ug", pg), ("uv", pvv)):
                        ub, uk = S_[un], f"{un}{si}"
                        if prev is None:
                            P.ms(ub[:, 0:2], 0.0, w=[uk])
                        else:
                            pS, pN = prev
                            pub, puk = pS[un], f"{un}{pS['i']}"
                            P.cp(ub[:, 0:2], pub[:, pN:pN + 2], r=[puk], w=[uk])
                        P.act(ub[:, 2:2 + N], ps[pi][:, 0:N], AF.Copy, r=[K[pi], uk], w=[uk])
                        if e0 == 0:
                            P.ts(ub[:, 2:2 + OWN0], ub[:, 2:2 + OWN0], flag, None, ALU.mult, r=[uk], w=[uk])
                    state["prev"] = (S_, N)

                def stage_b(k):
                    jj, j, e0, N = items[k]
                    S_ = sets[(cnt0 + k) % 2]
                    si = S_["i"]
                    pg, pvv, cg, cv = S_["banks"]
                    dg, dk = dgs[j % 2], f"dg{j % 2}"
                    for (un, pc, half) in (("ug", cg, 0), ("uv", cv, 1)):
                        ub, uk = S_[un], f"{un}{si}"
                        for k_ in range(3):
                            P.mm(ps[pc][:, 0:N], dg[:, half * 3 + k_, :], ub[:, k_:k_ + N], start=(k_ == 0), stop=(k_ == 2),
                                 r=[dk, uk], w=[K[pc]])
                    P.act(S_["th"][:, 0:N], ps[cg][:, 0:N], AF.Silu, r=[K[cg]], w=[f"th{si}"], bias=pvc(f"fcb{l}", j))
                    P.stt(gT[:, jj, e0:e0 + N], ps[cv][:, 0:N], pvc(f"fcb{l}", 22 + j), S_["th"][:, 0:N], ALU.add, ALU.mult,
                          r=[K[cv], f"th{si}"], w=["gT"])

                cnt0 = cnt
                stage_a(0)
                for k in range(len(items)):
                    if k + 1 < len(items):
                        stage_a(k + 1)
                    stage_b(k)
                cnt += len(items)
                for m in range(KC):
                    for bi_, (e0, N) in enumerate(FM_BLOCKS):
                        pi = (0, 4, 1, 5)[(m * 5 + bi_) % 4]
                        for jj in range(gj):
                            P.mm(ps[pi][:, 0:N], Wd[:, jj, m * 128:(m + 1) * 128], gT[:, jj, e0:e0 + N], start=(jj == 0),
                                 stop=(jj == gj - 1), r=["gT", "Wd"], w=[K[pi]])
                        P.tt(xres[:, m, e0:e0 + N], xres[:, m, e0:e0 + N], ps[pi][:, 0:N], ALU.add, r=[K[pi], "xres", ("hTall", e0)],
                             w=["xres", ("xo", m)])
                    if l == 1 and (j0, gj) == GROUPS[-1]:
                        P.dma("sp", yT[m * 128:(m + 1) * 128, :], xres[:, m, OWN0:EXT], reads=[("xo", m)], slot=f"out{m}")
            P.release(m0)

        ffn(0)
        P.barrier(cst)
        if stop_after == "F0":
            dump_out()
            return nc, P

        def conformer():
            m0 = P.mark()
            assert m0 == base + KC * EXT * 4
            W1c = [P.alloc([128, KC, 256], BF16) for _ in range(3)]
            hTs = [P.alloc([128, KC, 512], BF16) for _ in range(2)]
            W2 = P.alloc([128, KC, 1024], BF16)
            sq = P.alloc([128, KC, 512], BF16)
            rstd = P.alloc([128, 512], F32)
            gl = P.alloc([128, 8, 542], BF16)
            dgs = [P.alloc([128, 31, 128], BF16) for _ in range(2)]
            accbs = [P.alloc([128, 8, 512], BF16) for _ in range(2)]
            ths = [P.alloc([128, 512], F32) for _ in range(2)]
            xns = [P.alloc([128, 512], F32) for _ in range(2)]
            means = [P.alloc([128, 512], F32) for _ in range(2)]
            vars_ = [P.alloc([128, 512], F32) for _ in range(2)]
            glc = P.alloc([128, 8, 30], BF16)
            P.ms(glc, 0.0, w=["glc"])
            items = [(bi, ch) for bi in range(len(FM_BLOCKS)) for ch in range(8)]

            def w1_load(k):
                if k >= len(items):
                    return
                ch = items[k][1]
                i = k % 3
                P.dma("pool", W1c[i][:, :, 0:128], wview(pw1[:, ch * 128:(ch + 1) * 128]), writes=[f"W1c{i}"],
                      slot=f"W1c{i}a")
                P.dma("pool", W1c[i][:, :, 128:256], wview(pw1[:, 1024 + ch * 128:1024 + (ch + 1) * 128]),
                      writes=[f"W1c{i}"], slot=f"W1c{i}g")

            def norm(bi):
                e0, N = FM_BLOCKS[bi]
                rmsnorm(xres[:, :, e0:e0 + N], [("xres", bi)], N, "confw", hTs[bi % 2], f"hT{bi % 2}", hTs[bi % 2], f"hT{bi % 2}",
                        rstd, "rstd", 6)

            def ln_ch(bi, ch):
                e0, N = FM_BLOCKS[bi]
                accb, p = accbs[bi % 2], bi % 2
                xn, xk = xns[ch % 2], f"xn{ch % 2}"
                ak = ("accb", p, ch)
                P.tt(xn[:, 0:N], accb[:, ch, 0:N], means[p][:, 0:N], ALU.subtract, r=[ak, f"mean{p}"], w=[xk])
                P.tt(xn[:, 0:N], xn[:, 0:N], vars_[p][:, 0:N], ALU.mult, r=[xk, f"var{p}"], w=[xk])
                P.act(accb[:, ch, 0:N], xn[:, 0:N], AF.Silu, r=[xk], w=[ak], scale=pvc("lnw", ch), bias=pvc("lnb", ch))

            def pw2_blk(bi):
                e0, N = FM_BLOCKS[bi]
                accb, p = accbs[bi % 2], bi % 2
                for m in range(KC):
                    pi = 6 + m % 2
                    for c in range(8):
                        P.mm(ps[pi][:, 0:N], W2[:, c, m * 128:(m + 1) * 128], accb[:, c, 0:N], start=(c == 0), stop=(c == 7),
                             r=[("accb", p, c), "W2"], w=[K[pi]])
                    P.stt(xres[:, m, e0:e0 + N], ps[pi][:, 0:N], pvc("pw2b", m), xres[:, m, e0:e0 + N], ALU.add, ALU.add,
                          r=[K[pi], ("xres", bi)], w=[("xres", bi)])

            w1_load(0)
            w1_load(1)
            P.dma("pool", W2, wview(pw2), writes=["W2"], slot="W2")
            norm(0)
            k = 0
            for bi, (e0, N) in enumerate(FM_BLOCKS):
                hT, hk = hTs[bi % 2], f"hT{bi % 2}"
                accb, p = accbs[bi % 2], bi % 2
                def part1(ch, k):
                    w1_load(k + 2)
                    wi = k % 3
                    di = k % 2
                    dg, dk, th, tk = dgs[di], f"dg{di}", ths[di], f"th{di}"
                    pa, pg, pc = (0, 1, 2) if di == 0 else (4, 5, 3)
                    wc = PV["dww"] + ch * 31
                    P.tt(dg, ident.unsqueeze(1).broadcast_to([128, 31, 128]),
                         pv[:, wc:wc + 31].unsqueeze(2).broadcast_to([128, 31, 128]), ALU.mult, w=[dk])
                    for c in range(KC):
                        P.mm(ps[pa][:, 0:N], W1c[wi][:, c, 0:128], hT[:, c, 0:N], start=(c == 0),
                             stop=(c == KC - 1), r=[hk, f"W1c{wi}"], w=[K[pa]])
                    for c in range(KC):
                        P.mm(ps[pg][:, 0:N], W1c[wi][:, c, 128:256], hT[:, c, 0:N], start=(c == 0),
                             stop=(c == KC - 1), r=[hk, f"W1c{wi}"], w=[K[pg]])
                    P.act(th[:, 0:N], ps[pg][:, 0:N], AF.Tanh, r=[K[pg]], w=[tk], scale=0.5, bias=hv[:, ch:ch + 1])
                    P.ts(th[:, 0:N], th[:, 0:N], 0.5, 0.5, ALU.mult, ALU.add, r=[tk], w=[tk])
                    gk = ("gl", ch)
                    P.cp(gl[:, ch, 0:30], glc[:, ch, :], r=["glc"], w=[gk])
                    P.stt(gl[:, ch, 30:30 + N], ps[pa][:, 0:N], pvc("pw1b", ch), th[:, 0:N], ALU.add, ALU.mult,
                          r=[K[pa], tk, gk], w=[gk])
                    if e0 == 0:
                        P.ts(gl[:, ch, 30:30 + OWN0], gl[:, ch, 30:30 + OWN0], flag, None, ALU.mult, r=[gk], w=[gk])
                    P.cp(glc[:, ch, :], gl[:, ch, N:N + 30], r=[gk], w=["glc"])

                def part2(ch, k):
                    di = k % 2
                    dg, dk = dgs[di], f"dg{di}"
                    pc = 2 if di == 0 else 3
                    gk = ("gl", ch)
                    for k_ in range(31):
                        P.mm(ps[pc][:, 0:N], dg[:, k_, :], gl[:, ch, k_:k_ + N], start=(k_ == 0), stop=(k_ == 30),
                             r=[dk, gk], w=[K[pc]])
                    ak = ("accb", p, ch)
                    P.act(accb[:, ch, 0:N], ps[pc][:, 0:N], AF.Identity, r=[K[pc]], w=[ak], bias=pvc("dwb", ch))
                    P.act(sq[:, ch, 0:N], accb[:, ch, 0:N], AF.Square, r=[ak], w=["sq"])
                    if bi > 0:
                        ln_ch(bi - 1, ch)

                part1(0, k)
                for ch in range(8):
                    if ch + 1 < 8:
                        part1(ch + 1, k + ch + 1)
                    elif bi == len(FM_BLOCKS) - 1:
                        ffn_prefetch(1, ["W1c0", "W1c1", "W1c2", "hT0", "hT1"])
                    part2(ch, k + ch)
                    if ch == 5 and bi + 1 < len(FM_BLOCKS):
                        norm(bi + 1)
                k += 8
                if bi > 0:
                    pw2_blk(bi - 1)
                for c in range(8):
                    P.mm(ps[6][:, 0:N], cmean, accb[:, c, 0:N], start=(c == 0), stop=(c == 7), r=[("accb", p, c)], w=[K[6]])
                for c in range(8):
                    P.mm(ps[7][:, 0:N], cmean, sq[:, c, 0:N], start=(c == 0), stop=(c == 7), r=["sq"], w=[K[7]])
                mean, var = means[p], vars_[p]
                P.act(mean[:, 0:N], ps[6][:, 0:N], AF.Copy, r=[K[6]], w=[f"mean{p}"])
                P.tt(var[:, 0:N], mean[:, 0:N], mean[:, 0:N], ALU.mult, r=[f"mean{p}"], w=[f"var{p}"])
                P.tt(var[:, 0:N], ps[7][:, 0:N], var[:, 0:N], ALU.subtract, r=[K[7], f"var{p}"], w=[f"var{p}"])
                P.ts(var[:, 0:N], var[:, 0:N], LN_EPS, None, ALU.add, r=[f"var{p}"], w=[f"var{p}"])
                P.act(var[:, 0:N], var[:, 0:N], AF.Ln, r=[f"var{p}"], w=[f"var{p}"])
                P.act(var[:, 0:N], var[:, 0:N], AF.Exp, r=[f"var{p}"], w=[f"var{p}"], scale=-0.5)
            last = len(FM_BLOCKS) - 1
            for ch in range(8):
                ln_ch(last, ch)
            pw2_blk(last)
            P.release(m0)

        conformer()
        P.barrier(cst)
        if stop_after == "C":
            dump_out()
            return nc, P
        ffn(1)
        P.finish()
    return nc, P

def _host_pack(inp):
    f = np.float32
    pvec = np.zeros((128, NPV), f)
    rowbc = np.zeros((128, NRB), f)

    def fm(vec, nch):
        return np.asarray(vec, f).reshape(nch, 128).T

    pvec[:, PV["mixw"]:PV["mixw"] + 8] = fm(inp["mix_norm_w"][0], 8)
    pvec[:, PV["confw"]:PV["confw"] + 8] = fm(inp["conf_norm_w"][0], 8)
    pvec[:, PV["ffnw0"]:PV["ffnw0"] + 8] = fm(inp["ffn_norm_w"][0], 8)
    pvec[:, PV["ffnw1"]:PV["ffnw1"] + 8] = fm(inp["ffn_norm_w"][1], 8)
    pvec[:, PV["qg"]] = np.tile(np.asarray(inp["q_norm_w"][0], f), 2)
    pvec[:, PV["kg"]] = np.tile(np.asarray(inp["k_norm_w"][0], f), 2)
    scw = np.asarray(inp["ssm_conv_w"][0], f)
    pvec[:, PV["scw"]:PV["scw"] + 32] = scw.reshape(4, 8, 128).transpose(2, 1, 0).reshape(128, 32)
    pvec[:, PV["scb"]:PV["scb"] + 8] = fm(inp["ssm_conv_b"][0], 8)
    pvec[:, PV["pw1b"]:PV["pw1b"] + 16] = fm(inp["conf_pw1_b"][0], 16)
    dww = np.asarray(inp["conf_dw_w"][0], f)
    pvec[:, PV["dww"]:PV["dww"] + 248] = dww.reshape(31, 8, 128).transpose(2, 1, 0).reshape(128, 248)
    pvec[:, PV["dwb"]:PV["dwb"] + 8] = fm(inp["conf_dw_b"][0], 8)
    pvec[:, PV["lnw"]:PV["lnw"] + 8] = fm(inp["conf_ln_w"][0], 8)
    pvec[:, PV["lnb"]:PV["lnb"] + 8] = fm(inp["conf_ln_b"][0], 8)
    pvec[:, PV["pw2b"]:PV["pw2b"] + 8] = fm(inp["conf_pw2_b"][0], 8)
    for l in range(2):
        fcw = np.asarray(inp["ffn_conv_w"][l], f)
        pvec[:, PV[f"fcw{l}"]:PV[f"fcw{l}"] + 132] = fcw.reshape(3, 44, 128).transpose(2, 1, 0).reshape(128, 132)
        pvec[:, PV[f"fcb{l}"]:PV[f"fcb{l}"] + 44] = fm(inp["ffn_conv_b"][l], 44)

    def bc(vec):
        return np.broadcast_to(np.asarray(vec, f)[None, :], (128, len(vec)))

    rowbc[:, RB["subln"]:RB["subln"] + 128] = bc(inp["attn_subln_w"][0])
    rowbc[:, RB["dtb"]:RB["dtb"] + 8] = bc(inp["ssm_dt_bias"][0])
    rowbc[:, RB["alog"]:RB["alog"] + 8] = bc(inp["ssm_A_log"][0])
    rowbc[:, RB["dD"]:RB["dD"] + 8] = bc(inp["ssm_D"][0])
    rowbc[:, RB["snw"]:RB["snw"] + 512] = bc(inp["ssm_norm_w"][0])
    for n, k in (("lq1", "lambda_q1"), ("lk1", "lambda_k1"), ("lq2", "lambda_q2"), ("lk2", "lambda_k2")):
        rowbc[:, RB[n]:RB[n] + 64] = bc(inp[k][0])
    p = np.arange(128)
    cbf = np.zeros((128, NCB), f)
    cbf[:, 0:128] = np.eye(128, dtype=f)
    cbf[:, 128:256] = (p[:, None] <= p[None, :]).astype(f)
    cbf[:, 256:384] = 1.0 / 1024
    cbf[:, 384:512] = ((p[:, None] // 64) == (p[None, :] // 64)).astype(f) / 64
    cf = np.zeros((128, NCF), f)
    cf[:, 0:128] = (p[:, None] <= p[None, :]).astype(f)
    sl = (p[:, None] > p[None, :]).astype(f)
    cf[:, 128:256] = sl
    cf[:, 256:384] = 1.0
    cf[:, 384:512] = np.eye(128, dtype=f)
    cf[:, 512:1024] = np.tile(-1e9 * sl, (1, 4))
    return pvec, rowbc, cbf, cf


_CACHE = {}


def _get_nc(cfg_key, cfg):
    if cfg_key not in _CACHE:
        _, P1 = _build(dict(cfg), None)
        nc, P2 = _build(dict(cfg), P1.newneed)
        _CACHE[cfg_key] = nc
    return _CACHE[cfg_key]


def kernel(**inp):
    cfg = inp.pop("_cfg", {})
    f = np.float32
    x = np.asarray(inp["x"], f)
    pvec, rowbc, cbf, cf = _host_pack(inp)
    common = {
        "w_in": np.ascontiguousarray(inp["w_in"][0], f), "w_out": np.ascontiguousarray(inp["w_out"][0], f),
        "pw1": np.ascontiguousarray(inp["conf_pw1_w"][0], f), "pw2": np.ascontiguousarray(inp["conf_pw2_w"][0], f),
        "up0": np.ascontiguousarray(inp["ffn_up_w"][0], f), "up1": np.ascontiguousarray(inp["ffn_up_w"][1], f),
        "dn0": np.ascontiguousarray(inp["ffn_down_w"][0], f), "dn1": np.ascontiguousarray(inp["ffn_down_w"][1], f),
        "rowbc": rowbc, "cbf": cbf, "cf32": cf,
    }
    in_maps = []
    for c in range(8):
        b, half = c // 2, c % 2
        xt = np.zeros((D, S), f)
        pvc = pvec.copy()
        if half == 1:
            xt[:, :] = x[b].T
            pvc[:, PV["flag"]] = 1.0
        else:
            xt[:, 2048:] = x[b, :2048].T
            pvc[:, PV["kbias"]:PV["kbias"] + 16] = -30000.0
        m = dict(common)
        m["xT"] = xt
        m["pvec"] = pvc
        in_maps.append(m)
    nc = _get_nc(str(sorted(cfg.items())), cfg)
    res = run_bass_kernel_spmd(nc, in_maps, core_ids=list(range(8)))
    out = np.zeros((4, S, D), f)
    for c in range(8):
        b, half = c // 2, c % 2
        out[b, half * 2048:(half + 1) * 2048, :] = res.results[c]["yT"].T
    return out
```

```python
import numpy as np
from contextlib import ExitStack
import concourse.bass as bass
import concourse.mybir as mybir
from concourse.bass_utils import run_bass_kernel_spmd

F32 = mybir.dt.float32
BF16 = mybir.dt.bfloat16
U8 = mybir.dt.uint8
AF = mybir.ActivationFunctionType
ALU = mybir.AluOpType
AX = mybir.AxisListType

D = 1024
KC = 8
S = 4096
PRE = 1920
EXT = 2176
OWN0 = 128
FFN = 2816
NJ = 22
RMS_EPS = 1e-6
LN_EPS = 1e-5
IN_W = 3080
QOFF, KOFF, VOFF, ZOFF, XOFF, DTOFF = 0, 512, 1024, 1536, 2048, 3072

CTX_BLOCKS = [(0, 512, False), (512, 512, False), (1024, 512, False), (1536, 384, False),
              (1920, 512, True), (2432, 512, True), (2944, 384, True), (3328, 384, True),
              (3712, 384, True)]
OWN_BLOCKS = [(0, 128), (128, 512), (640, 512), (1152, 512), (1664, 512)]
FM_BLOCKS = [(0, 448), (448, 432), (880, 432), (1312, 432), (1744, 432)]

PV = {}
_o = 0
for _n, _w in [("mixw", 8), ("confw", 8), ("ffnw0", 8), ("ffnw1", 8), ("qg", 1), ("kg", 1),
               ("scw", 32), ("scb", 8), ("pw1b", 16), ("dww", 8 * 31), ("dwb", 8), ("lnw", 8),
               ("lnb", 8), ("pw2b", 8), ("fcw0", 44 * 3), ("fcw1", 44 * 3), ("fcb0", 44),
               ("fcb1", 44), ("flag", 1), ("kbias", 32)]:
    PV[_n] = _o
    _o += _w
NPV = _o
RB = {}
_o = 0
for _n, _w in [("subln", 128), ("dtb", 8), ("alog", 8), ("dD", 8), ("snw", 512),
               ("lq1", 64), ("lk1", 64), ("lq2", 64), ("lk2", 64)]:
    RB[_n] = _o
    _o += _w
NRB = _o
NCB = 4 * 128
NCF = 4 * 128 + 512


class Prog:
    MAXV = 30000

    def __init__(self, nc, es, need=None):
        self.nc = nc
        self.es = es
        self.pass2 = need is not None
        self.need = need if need is not None else set()
        self.newneed = set()
        self.E = {"pe": nc.tensor, "act": nc.scalar, "dve": nc.vector, "pool": nc.gpsimd,
                  "sp": nc.sync}
        self.sems = {}
        self.sigcnt = {e: 0 for e in self.E}
        self.icnt = {e: 0 for e in self.E}
        self.seen = {e: {} for e in self.E}
        self.opidx = 0
        self.st = {}
        self.slots = {}
        self.sigval = {}
        self.nsem = 0
        self.big = es.enter_context(nc.sbuf_tensor("big", [128, 207 * 1024], U8))
        self.off = 0
        self.ps = [es.enter_context(nc.psum_tensor(f"psb{i}", [128, 512], F32)) for i in range(8)]

    def alloc(self, shape, dt):
        n = int(np.prod(shape[1:]))
        bpe = {F32: 4, BF16: 2, U8: 1}[dt]
        nb = (n * bpe + 31) // 32 * 32
        assert self.off + nb <= 207 * 1024, f"SBUF overflow {self.off}+{nb}"
        v = self.big[:, self.off:self.off + nb]
        self.off += nb
        v = v[:, 0:n * bpe]
        if dt != U8:
            v = v.bitcast(dt)
        if len(shape) == 3:
            v = v.rearrange("p (a b) -> p a b", a=shape[1])
        elif len(shape) == 4:
            v = v.rearrange("p (a b c) -> p a b c", a=shape[1], b=shape[2])
        return v

    def view_at(self, off, shape, dt):
        n = int(np.prod(shape[1:]))
        bpe = {F32: 4, BF16: 2, U8: 1}[dt]
        v = self.big[:, off:off + n * bpe]
        if dt != U8:
            v = v.bitcast(dt)
        if len(shape) == 3:
            v = v.rearrange("p (a b) -> p a b", a=shape[1])
        return v

    def mark(self):
        return self.off

    def release(self, m):
        self.off = m

    def _sem(self, key):
        if key not in self.sems:
            self.sems[key] = self.es.enter_context(self.nc.semaphore(f"s{self.nsem}"))
            self.nsem += 1
        return self.sems[key]

    def _deps(self, reads, writes):
        deps = []
        for k in reads:
            s = self.st.get(k)
            if s and s[0]:
                deps.append(s[0])
        for k in writes:
            s = self.st.get(k)
            if s:
                if s[0]:
                    deps.append(s[0])
                deps.extend(s[1].values())
        return deps

    def _wait(self, eng, deps):
        h = self.E[eng]
        for d in deps:
            if d[0] == "c":
                _, q, qidx, qic = d
                if q == eng:
                    if eng == "pe":
                        continue
                    if self.icnt[eng] - qic > 2:
                        continue
                self.newneed.add(qidx)
                if self.pass2:
                    semkey, sem, val = self.sigval[qidx]
                    if self.seen[eng].get(semkey, 0) >= val:
                        continue
                    h.wait_ge(sem, val)
                    self.seen[eng][semkey] = val
            else:
                _, slot, val = d
                if self.seen[eng].get(("d", slot), 0) >= val:
                    continue
                if self.pass2:
                    h.wait_ge(self.slots[slot][0], val)
                self.seen[eng][("d", slot)] = val

    def _record(self, rec, eng, reads, writes):
        for k in reads:
            self.st.setdefault(k, [None, {}])[1][eng] = rec
        for k in writes:
            self.st[k] = [rec, {}]

    PSK = frozenset(f"ps{i}" for i in range(8))

    def op(self, eng, fn, reads=(), writes=()):
        if any(k in self.PSK for k in reads):
            writes = list(writes) + [k for k in reads if k in self.PSK]
            reads = [k for k in reads if k not in self.PSK]
        idx = self.opidx
        self.opidx += 1
        self._wait(eng, self._deps(reads, writes))
        ic = self.icnt[eng]
        self.icnt[eng] += 1
        if self.pass2:
            ins = fn(self.E[eng])
            if idx in self.need:
                self.sigcnt[eng] += 1
                n = self.sigcnt[eng]
                si = (n - 1) // self.MAXV
                val = (n - 1) % self.MAXV + 1
                sem = self._sem((eng, si))
                ins.then_inc(sem, 1)
                self.sigval[idx] = ((eng, si), sem, val)
        self._record(("c", eng, idx, ic), eng, reads, writes)

    def dma(self, q, out, in_, reads=(), writes=(), slot=None):
        self.opidx += 1
        self._wait(q, self._deps(reads, writes))
        self.icnt[q] += 1
        if slot not in self.slots:
            self.slots[slot] = [self._sem(("dma", slot)) if self.pass2 else None, 0]
        sl = self.slots[slot]
        sl[1] += 16
        if self.pass2:
            self.E[q].dma_start(out=out, in_=in_).then_inc(sl[0], 16)
        self._record(("d", slot, sl[1]), "dma:" + slot, reads, writes)

    def barrier(self, cst):
        ident = cst["ident"]
        scr = cst["scr"]
        self.op("pe", lambda e: e.matmul(self.ps[0][:, 0:1], ident[:, 0:128], ident[:, 0:1],
                                         start=True, stop=True),
                reads=["cb"], writes=["ps0", ("bar", "pe")])
        self.op("act", lambda e: e.activation(scr[:, 0:1], scr[:, 4:5], AF.Copy), writes=[("bar", "act")])
        for i, en in enumerate(("dve", "pool")):
            self.op(en, lambda e, i=i: e.memset(scr[:, 1 + i:2 + i], 0.0), writes=[("bar", en)])
        deps = [self.st[("bar", q)][0] for q in ("pe", "act", "dve", "pool")]
        deps += [("d", s, v[1]) for s, v in self.slots.items()]
        for e in self.E:
            self._wait(e, deps)
        self.st.clear()

    def finish(self):
        deps = [("d", s, v[1]) for s, v in self.slots.items()]
        for e in ("sp", "pool"):
            self._wait(e, deps)


    def mm(self, out, lhsT, rhs, start=True, stop=True, r=(), w=()):
        self.op("pe", lambda e: e.matmul(out, lhsT, rhs, start=start, stop=stop), r, w)

    def tr(self, out, in_, ident, r=(), w=()):
        self.op("pe", lambda e: e.transpose(out, in_, ident), r, w)

    def act(self, out, in_, func, r=(), w=(), bias=None, scale=None):
        kw = {}
        if bias is not None:
            kw["bias"] = bias
        if scale is not None:
            kw["scale"] = scale
        self.op("act", lambda e: e.activation(out, in_, func, **kw), r, w)

    def tt(self, out, a, b, op, r=(), w=(), eng="dve"):
        self.op(eng, lambda e: e.tensor_tensor(out, a, b, op), r, w)

    def ts(self, out, a, s1, s2, op0, op1=None, r=(), w=(), eng="dve"):
        if op1 is None:
            self.op(eng, lambda e: e.tensor_scalar(out, a, s1, None, op0), r, w)
        else:
            self.op(eng, lambda e: e.tensor_scalar(out, a, s1, s2, op0, op1), r, w)

    def stt(self, out, a, s, b, op0, op1, r=(), w=(), eng="dve"):
        self.op(eng, lambda e: e.scalar_tensor_tensor(out, a, s, b, op0, op1), r, w)

    def red(self, out, in_, r=(), w=(), eng="dve"):
        self.op(eng, lambda e: e.tensor_reduce(out, in_, AX.X, ALU.add), r, w)

    def cp(self, out, in_, r=(), w=(), eng="dve"):
        self.op(eng, lambda e: e.tensor_copy(out, in_), r, w)

    def ms(self, out, val, r=(), w=(), eng="dve"):
        self.op(eng, lambda e: e.memset(out, val), r, w)


class Buf:
    def __init__(self, P, shape, dt, key):
        self.ap = P.alloc(shape, dt)
        self.k = key


def bufs(P, n, shape, dt, key):
    return [Buf(P, shape, dt, f"{key}{i}") for i in range(n)]


class Rot:
    def __init__(self, items):
        self.items = items
        self.i = 0

    def next(self):
        x = self.items[self.i % len(self.items)]
        self.i += 1
        return x

def _build(cfg, need):
    nc = bass.Bass("TRN2", target_bir_lowering=False)

    def din(name, shape):
        return nc.dram_tensor(name, list(shape), F32, kind="ExternalInput").ap()

    xT = din("xT", [D, S])
    w_in = din("w_in", [D, IN_W])
    w_out = din("w_out", [D, D])
    pw1 = din("pw1", [D, 2 * D])
    pw2 = din("pw2", [D, D])
    upw = [din("up0", [D, 2 * FFN]), din("up1", [D, 2 * FFN])]
    dnw = [din("dn0", [FFN, D]), din("dn1", [FFN, D])]
    pvec = din("pvec", [128, NPV])
    rowbc = din("rowbc", [128, NRB])
    cbf = din("cbf", [128, NCB])
    cf32 = din("cf32", [128, NCF])
    yT = nc.dram_tensor("yT", [D, 2048], F32, kind="ExternalOutput").ap()
    stop_after = cfg.get("stop_after", "all")

    es = ExitStack()
    with es:
        P = Prog(nc, es, need)
        ps = P.ps
        K = [f"ps{i}" for i in range(8)]
        pv = P.alloc([128, NPV], F32)
        rb = P.alloc([128, NRB], F32)
        cb = P.alloc([128, NCB], BF16)
        sm = P.alloc([128, 16], F32)
        negA = P.alloc([128, 8], F32)
        sw8 = P.alloc([128, 128], F32)
        scr = P.alloc([128, 8], F32)
        hv = P.alloc([128, 24], F32)
        lt = P.alloc([128, 128], F32)
        ident, tri, cmean, bd64 = cb[:, 0:128], cb[:, 128:256], cb[:, 256:384], cb[:, 384:512]
        cst = {"ident": ident, "scr": scr}

        def pvc(name, i=0):
            c = PV[name] + i
            return pv[:, c:c + 1]

        def rbc(name, n):
            return rb[:, RB[name]:RB[name] + n]

        P.dma("sp", pv, pvec, w=["pv"], slot="pv") if False else P.dma("sp", pv, pvec, writes=["pv"], slot="pv")
        P.dma("sp", rb, rowbc, writes=["rb"], slot="rb")
        P.dma("pool", cb, cbf, writes=["cb"], slot="cb")
        P.ts(sm[:, 0:1], pvc("qg"), 0.125, None, ALU.mult, r=["pv"], w=["sm0"])
        P.tt(lt[:, 0:64], rbc("lq1", 64), rbc("lk1", 64), ALU.mult, r=["rb"], w=["lt"])
        P.tt(lt[:, 64:128], rbc("lq2", 64), rbc("lk2", 64), ALU.mult, r=["rb", "lt"], w=["lt"])
        P.red(sm[:, 2:4], lt.rearrange("p (a b) -> p a b", a=2), r=["lt"], w=["sm2"])
        P.act(sm[:, 4:6], sm[:, 2:4], AF.Exp, r=["sm2"], w=["sm4"])
        P.tt(sm[:, 6:7], sm[:, 5:6], sm[:, 4:5], ALU.subtract, r=["sm4"], w=["sm6"])
        P.ts(sm[:, 7:8], sm[:, 6:7], -0.2, None, ALU.add, r=["sm6"], w=["neglam"])
        neglam = sm[:, 7:8]
        P.act(negA, rbc("alog", 8), AF.Exp, r=["rb"], w=["negA0"])
        P.ts(negA, negA, -1.0, None, ALU.mult, r=["negA0"], w=["negA"])
        P.ts(sw8, rbc("subln", 128), 0.8, None, ALU.mult, r=["rb"], w=["sw8"])
        P.ts(hv[:, 0:8], pv[:, PV["pw1b"] + 8:PV["pw1b"] + 16], 0.5, None, ALU.mult, r=["pv"], w=["hv"])
        P.ts(hv[:, 8:16], pv[:, PV["lnw"]:PV["lnw"] + 8], 0.5, None, ALU.mult, r=["pv", "hv"], w=["hv"])
        P.ts(hv[:, 16:24], pv[:, PV["lnb"]:PV["lnb"] + 8], 0.5, None, ALU.mult, r=["pv", "hv"], w=["hv"])
        flag = pvc("flag")
        epsc = scr[:, 5:6]
        P.ms(epsc, RMS_EPS, w=["epsc"])
        onec = scr[:, 6:7]
        P.ms(onec, 1.0, w=["onec"])
        CK = ["pv", "rb", "cb", "sm0", "neglam", "negA", "sw8", "hv"]
        base = P.mark()
        TOP = 207 * 1024
        xT_v = xT.rearrange("(c p) t -> p c t", p=128)

        def wview(src):
            return src.rearrange("(c p) w -> p c w", p=128)

        def rmsnorm(xap, xkeys, N, wname, hT, hkey, sq, sqkey, rstd, rkey, psi):
            P.act(sq[:, :, 0:N], xap, AF.Square, r=xkeys, w=[sqkey])
            for c in range(KC):
                P.mm(ps[psi][:, 0:N], cmean, sq[:, c, 0:N], start=(c == 0), stop=(c == KC - 1),
                     r=[sqkey], w=[K[psi]])
            P.act(rstd[:, 0:N], ps[psi][:, 0:N], AF.Ln, r=[K[psi]], w=[rkey], bias=epsc)
            P.act(rstd[:, 0:N], rstd[:, 0:N], AF.Exp, r=[rkey], w=[rkey], scale=-0.5)
            for c in range(KC):
                P.stt(hT[:, c, 0:N], xap[:, c, :], pvc(wname, c), rstd[:, 0:N], ALU.mult, ALU.mult,
                      r=xkeys + [rkey], w=[hkey])

        P.off_top = TOP - 4 * EXT * 2
        attT = P.big[:, P.off_top:TOP].bitcast(BF16).rearrange("p (a b) -> p a b", a=4)

        def pass_a():
            KT = P.alloc([128, 4, S], BF16)
            V = P.alloc([128, 32, 4, 132], BF16)
            wq_off = P.off
            Wqkv = P.alloc([128, KC, 1536], BF16)
            xa = bufs(P, 2, [128, KC, 512], F32, "xa")
            xa_end = P.off
            hTb = bufs(P, 2, [128, KC, 512], BF16, "hT")
            rstd = P.alloc([128, 512], F32)
            QTb = bufs(P, 1, [128, 4, 512], BF16, "QT") * 2
            sq2 = Rot(bufs(P, 2, [128, 512], BF16, "sq2"))
            raw = Rot(bufs(P, 2, [128, 512], F32, "raw"))
            rs2 = Rot(bufs(P, 2, [128, 512], F32, "rs2"))
            pt = Rot(bufs(P, 6, [128, 512], BF16, "pt"))
            Ocp = P.alloc([128, 4, 512], F32)
            o1 = bufs(P, 4, [128, 128], F32, "o1")
            araw4 = bufs(P, 4, [128, 128], F32, "araw")
            asq = P.alloc([128, 128], F32)
            st = P.alloc([128, 32], F32)
            atok = bufs(P, 4, [128, 512], BF16, "atok")
            print('passA mem', P.off, P.off_top)
            assert P.off <= P.off_top, (P.off, P.off_top)
            for h_ in range(4):
                c0 = 512 + h_ * 128
                P.dma("pool", Wqkv[:, :, c0:c0 + 128], wview(w_in[:, c0:c0 + 128]), writes=[f"Wk{h_}"], slot=f"Wk{h_}")
            for i in (2, 0):
                P.dma("pool", Wqkv[:, :, i * 512:(i + 1) * 512], wview(w_in[:, i * 512:(i + 1) * 512]),
                      writes=[f"Wqkv{i}"], slot=f"Wqkv{i}")
            P.ms(V[:, :, :, 128:129], 1.0, w=["Vones"])
            psrot = Rot([0, 1, 3])
            orot = Rot([4, 6])
            qkrot = Rot([(3, 2), (1, 0)])

            def qk_a(hT, N, woff, wkey):
                a, b, c2 = sq2.next(), raw.next(), rs2.next()
                pr, pst = qkrot.next()
                for c in range(KC):
                    P.mm(ps[pr][:, 0:N], Wqkv[:, c, woff:woff + 128], hT.ap[:, c, 0:N], start=(c == 0),
                         stop=(c == KC - 1), r=[hT.k, wkey], w=[K[pr]])
                P.act(b.ap[:, 0:N], ps[pr][:, 0:N], AF.Copy, r=[K[pr]], w=[b.k])
                P.tt(a.ap[:, 0:N], b.ap[:, 0:N], b.ap[:, 0:N], ALU.mult, r=[b.k], w=[a.k])
                return (a, b, c2, pst)

            def qk_b(ctx, N, gcol, dst, dkey):
                a, b, c2, pst = ctx
                P.mm(ps[pst][:, 0:N], bd64, a.ap[:, 0:N], r=[a.k], w=[K[pst]])
                P.act(c2.ap[:, 0:N], ps[pst][:, 0:N], AF.Ln, r=[K[pst]], w=[c2.k], bias=epsc)
                P.act(c2.ap[:, 0:N], c2.ap[:, 0:N], AF.Exp, r=[c2.k], w=[c2.k], scale=-0.5)
                P.stt(dst, b.ap[:, 0:N], gcol, c2.ap[:, 0:N], ALU.mult, ALU.mult, r=[b.k, c2.k], w=[dkey])

            def pa_norm(bi):
                t0, N, own = CTX_BLOCKS[bi]
                xb, hT = xa[bi % 2], hTb[bi % 2]
                P.dma("sp", xb.ap[:, :, 0:N], xT_v[:, :, t0:t0 + N], writes=[xb.k], slot=xb.k)
                rmsnorm(xb.ap[:, :, 0:N], [xb.k], N, "mixw", hT.ap, hT.k, hT.ap, hT.k, rstd, "rstd", 2)

            pa_norm(0)
            deferred = []
            pending_tr = [None]
            for bi, (t0, N, own) in enumerate(CTX_BLOCKS):
                xb, hT = xa[bi % 2], hTb[bi % 2]
                NT, tile0 = N // 128, t0 // 128
                early_norm = (not own) and bi + 1 < len(CTX_BLOCKS)
                if early_norm:
                    pa_norm(bi + 1)
                e0 = t0 - PRE
                qt = QTb[bi % 2]
                qitems = [(512 + h * 128, f"Wk{h}", pvc("kg"), KT[:, h, t0:t0 + N], ("KT", h)) for h in range(4)]
                if own:
                    qitems += [(h * 128, "Wqkv0", sm[:, 0:1], qt.ap[:, h, 0:N], (qt.k, h)) for h in range(4)]
                vdone = [0]

                def vtile():
                    i = vdone[0]
                    if i >= NT:
                        return
                    vdone[0] += 1
                    pi = 4 + (i % 2)
                    for c in range(KC):
                        P.mm(ps[pi][:, 0:512], hT.ap[:, c, i * 128:(i + 1) * 128], Wqkv[:, c, 1024:1536],
                             start=(c == 0), stop=(c == KC - 1), r=[hT.k, "Wqkv2"], w=[K[pi]])
                    P.act(V[:, tile0 + i, :, 0:128], ps[pi][:, 0:512].rearrange("p (h v) -> p h v", h=4), AF.Copy,
                          r=[K[pi]], w=[("V", tile0 + i)])

                ctxs = {}
                nq = len(qitems)
                for t_ in range(nq + 1):
                    if t_ < nq:
                        ctxs[t_] = qk_a(hT, N, qitems[t_][0], qitems[t_][1])
                    if t_ >= 1:
                        woff_, wk_, gcol_, dst_, dkey_ = qitems[t_ - 1]
                        qk_b(ctxs.pop(t_ - 1), N, gcol_, dst_, dkey_)
                        vtile()
                        if deferred:
                            f_, a_ = deferred.pop(0)
                            f_(a_)
                while vdone[0] < NT:
                    vtile()
                while deferred:
                    f_, a_ = deferred.pop(0)
                    f_(a_)
                if pending_tr[0] is not None:
                    pending_tr[0]()
                    pending_tr[0] = None
                if bi + 1 < len(CTX_BLOCKS):
                    if not early_norm:
                        pa_norm(bi + 1)
                else:
                    o_wz = base + KC * EXT * 4
                    o_wo = o_wz + KC * 1544 * 2
                    o_cf = o_wo + KC * 1024 * 2
                    assert o_wz >= wq_off and o_cf + NCF * 4 <= xa_end, (o_wz, wq_off, o_cf, xa_end)
                    dead = ["Wqkv0", "Wk0", "Wk1", "Wk2", "Wk3", "Wqkv2", xa[0].k, xa[1].k]
                    WzP = P.view_at(o_wz, [128, KC, 1544], BF16)
                    WoP = P.view_at(o_wo, [128, KC, 1024], BF16)
                    cfP = P.view_at(o_cf, [128, NCF], F32)
                    P.dma("pool", WzP[:, :, 512:1024], wview(w_in[:, 2048:2560]), writes=["pfB"] + dead, slot="Wz1")
                    P.dma("pool", WzP[:, :, 0:512], wview(w_in[:, 1536:2048]), writes=["pfB"] + dead, slot="Wz0")
                    P.dma("pool", WzP[:, :, 1024:1544], wview(w_in[:, 2560:3080]), writes=["pfB"] + dead, slot="Wz2")
                    P.dma("pool", WoP, wview(w_out), writes=["pfB"] + dead, slot="Wo")
                    P.dma("sp", cfP, cf32, writes=["pfB"] + dead, slot="cf")
                if not own:
                    continue
                nkt = tile0 + NT
                ats = [atok[i] for i in range(NT)]
                sbank = [Rot([0, 1]), Rot([2, 3])]
                for h in range(4):

                    def score(j, h=h):
                        qs = max(0, j - tile0) * 128
                        res = []
                        for c in range(2):
                            r0 = 64 * c
                            psi = sbank[c].next()
                            p_ = pt.next()
                            P.mm(ps[psi][:, qs:N], KT[r0:r0 + 64, h, j * 128:(j + 1) * 128],
                                 qt.ap[r0:r0 + 64, h, qs:N], r=[("KT", h), (qt.k, h)], w=[K[psi]])
                            res.append((psi, p_))
                        for c in range(2):
                            psi, p_ = res[c]
                            P.act(p_.ap[:, qs:N], ps[psi][:, qs:N], AF.Exp, r=[K[psi]], w=[p_.k],
                                  bias=pvc("kbias", j))
                            if j >= tile0:
                                P.tt(p_.ap[:, qs:qs + 128], p_.ap[:, qs:qs + 128], tri, ALU.mult,
                                     r=[p_.k], w=[p_.k])
                        return [p for _, p in res]

                    def av(j, pts, h=h):
                        for c in range(2):
                            for i in range(NT):
                                if tile0 + i < j:
                                    continue
                                bank, col = 4 + 2 * c + i // 2, (i % 2) * 256
                                P.mm(ps[bank][:, col:col + 129], pts[c].ap[:, i * 128:(i + 1) * 128],
                                     V[:, j, h, 0:129], start=(j == 0 and i % 2 == 0), stop=(j == tile0 + i),
                                     r=[pts[c].k, ("V", j), "Vones"], w=[K[bank]])

                    pend = []
                    for j in range(nkt):
                        if j >= 2 and deferred:
                            f_, a_ = deferred.pop(0)
                            f_(a_)
                        pend.append((j, score(j)))
                        if len(pend) > 1:
                            av(*pend.pop(0))
                    while pend:
                        av(*pend.pop(0))
                    nb_ = (NT + 1) // 2
                    for c in range(2):
                        for bb in range(nb_):
                            bank = 4 + 2 * c + bb
                            P.cp(Ocp[:, 2 * c + bb, 0:385], ps[bank][:, 0:385], r=[K[bank]], w=[("Ocp", 2 * c + bb)])
                    def stage_a(i, h=h):
                        ar = araw4[i]
                        for c in range(2):
                            ok = ("Ocp", 2 * c + i // 2)
                            src = Ocp[:, 2 * c + i // 2, :]
                            col = (i % 2) * 256
                            sc = 4 * c + i
                            P.ts(st[:, sc:sc + 1], src[:, col + 128:col + 129], 1e-30, None, ALU.add, r=[ok],
                                 w=[("st", sc)])
                            P.op("dve", lambda e, sc=sc: e.reciprocal(st[:, sc:sc + 1], st[:, sc:sc + 1]),
                                 [("st", sc)], [("st", sc)])
                            if c == 0:
                                P.ts(o1[i].ap, src[:, col:col + 128], st[:, sc:sc + 1], None, ALU.mult,
                                     r=[ok, ("st", sc)], w=[o1[i].k])
                                continue
                            s2 = 8 + sc
                            s3 = 16 + sc
                            P.tt(st[:, s2:s2 + 1], st[:, sc:sc + 1], neglam, ALU.mult, r=[("st", sc)],
                                 w=[("st", s2)])
                            P.stt(ar.ap, src[:, col:col + 128], st[:, s2:s2 + 1], o1[i].ap, ALU.mult,
                                  ALU.add, r=[ok, ("st", s2), o1[i].k], w=[ar.k])
                            P.tt(asq, ar.ap, ar.ap, ALU.mult, r=[ar.k], w=["asq"])
                            P.red(st[:, s3:s3 + 1], asq, r=["asq"], w=[("st", s3)])
                            P.ts(st[:, s3:s3 + 1], st[:, s3:s3 + 1], 1.0 / 128, RMS_EPS, ALU.mult, ALU.add,
                                 r=[("st", s3)], w=[("st", s3)])

                    def stage_b(i, h=h):
                        ar = araw4[i]
                        s3 = 16 + 4 + i
                        P.act(st[:, s3:s3 + 1], st[:, s3:s3 + 1], AF.Ln, r=[("st", s3)], w=[("st", s3)])
                        P.act(st[:, s3:s3 + 1], st[:, s3:s3 + 1], AF.Exp, r=[("st", s3)], w=[("st", s3)],
                              scale=-0.5)
                        P.stt(ats[i].ap[:, h * 128:(h + 1) * 128], ar.ap, st[:, s3:s3 + 1], sw8, ALU.mult,
                              ALU.mult, r=[ar.k, ("st", s3)], w=[ats[i].k])

                    order = []
                    for t_ in range(NT + 2):
                        if t_ < NT:
                            order.append((stage_a, t_))
                        if 0 <= t_ - 2 < NT:
                            order.append((stage_b, t_ - 2))
                    deferred.extend(order)
                def do_tr(NT=NT, e0=e0):
                    for i in range(NT):
                        tb = 2 + (i % 2)
                        pbf = ps[tb][:, :].bitcast(BF16)
                        for h in range(4):
                            P.tr(pbf[:, h * 128:(h + 1) * 128], atok[i].ap[:, h * 128:(h + 1) * 128], ident,
                                 r=[atok[i].k], w=[K[tb]])
                        P.cp(attT[:, :, e0 + i * 128:e0 + (i + 1) * 128],
                             pbf[:, 0:512].rearrange("p (h t) -> p h t", h=4), r=[K[tb]],
                             w=[("attT", e0 // 128 + i)])
                pending_tr[0] = do_tr
            xres_v = P.view_at(base, [128, KC, EXT], F32)
            N0 = CTX_BLOCKS[0][1]
            deadkv = [("KT", h_) for h_ in range(4)] + [("V", j_) for j_ in range(32)] + ["Vones"]
            P.dma("sp", xres_v[:, :, 0:N0], xT_v[:, :, 0:N0], writes=["pfX"] + deadkv, slot="xtmp0")
            while deferred:
                f_, a_ = deferred.pop(0)
                f_(a_)
            if pending_tr[0] is not None:
                pending_tr[0]()

        pass_a()
        P.barrier(cst)
        P.release(base)
        xres = P.alloc([128, KC, EXT], F32)
        m_x = P.mark()
        if stop_after == "A":
            for c in range(4):
                P.cp(xres[:, c, :], attT[:, c, :], w=["xres"])
            P.dma("sp", yT.rearrange("(c p) t -> p c t", p=128), xres[:, :, OWN0:EXT], reads=["xres"],
                  slot="out")
            P.finish()
            return nc, P

        def ffn_prefetch(l, dead):
            o_g = base + KC * EXT * 4
            gj0 = 6
            WgP = P.view_at(o_g, [128, KC, 768], BF16)
            WvP = P.view_at(o_g + KC * 768 * 2, [128, KC, 768], BF16)
            P.dma("pool", WgP[:, :, 0:gj0 * 128], wview(upw[l][:, 0:gj0 * 128]), writes=["pfF"] + dead, slot="Wg")
            P.dma("pool", WvP[:, :, 0:gj0 * 128], wview(upw[l][:, FFN:FFN + gj0 * 128]), writes=["pfF"] + dead, slot="Wv")

        def pass_b():
            Wz = P.alloc([128, KC, 1544], BF16)
            Wo = P.alloc([128, KC, 1024], BF16)
            cf = P.alloc([128, NCF], F32)
            U32, SL32, ones32, id32, negrep = cf[:, 0:128], cf[:, 128:256], cf[:, 256:384], cf[:, 384:512], cf[:, 512:1024]
            hTb = bufs(P, 1, [128, KC, 512], BF16, "hT") * 2
            rstd = P.alloc([128, 512], F32)
            us = [P.alloc([128, 515], BF16) for _ in range(2)]
            dgall = P.alloc([128, 32, 128], BF16)
            xbcTs = [P.alloc([128, 8, 512], BF16) for _ in range(2)]
            ucar = P.alloc([128, 8, 3], BF16)
            dtbs = [P.alloc([128, 4, 8], F32) for _ in range(2)]
            adts = [P.alloc([128, 4, 8], F32) for _ in range(2)]
            t8 = P.alloc([128, 4, 8], F32)
            Rs = [P.alloc([128, 8, 128], F32)] * 2
            MTs = [P.alloc([128, 8, 128], BF16)] * 2
            xsBs = [P.alloc([128, 768], BF16) for _ in range(2)]
            xdts = [P.alloc([128, 512], BF16)] * 2
            xdds = [P.alloc([128, 512], BF16) for _ in range(2)]
            Sst = P.alloc([128, 512], F32)
            Sbf = P.alloc([128, 512], BF16)
            szb = P.alloc([128, 4, 512], BF16)
            Dg = P.alloc([128, 8, 128], BF16)
            y1s = [P.alloc([128, 512], F32) for _ in range(2)]
            y2s = [rstd] * 2
            ytoks = [P.alloc([128, 512], BF16) for _ in range(2)]
            yT_ = P.alloc([128, 4, 512], BF16)
            c8s = [P.alloc([128, 48], F32) for _ in range(2)]
            print("passB mem", P.off, P.off_top)
            assert P.off <= P.off_top, (P.off, P.off_top)
            assert m_x == base + KC * EXT * 4
            P.ms(ucar, 0.0, w=["ucar"])
            P.tt(dgall, ident.unsqueeze(1).broadcast_to([128, 32, 128]),
                 pv[:, PV["scw"]:PV["scw"] + 32].unsqueeze(2).broadcast_to([128, 32, 128]), ALU.mult, w=["dgall"])
            P.tt(Dg, ident.unsqueeze(1).broadcast_to([128, 8, 128]), rbc("dD", 8).unsqueeze(2).broadcast_to([128, 8, 128]),
                 ALU.mult, w=["Dg"])
            P.ms(Sst, 0.0, w=["S"])
            P.ms(Sbf, 0.0, w=["Sbf"])
            WZ = ["Wz0", "Wz1", "Wz2"]
            def blk(bi):
                t0, N, own = CTX_BLOCKS[bi]
                hT = hTb[bi % 2]
                return t0, N, own, N // 128, t0 // 128, t0 - PRE, hT, xbcTs[bi % 2], dtbs[bi % 2], adts[bi % 2], bi % 2

            def xkey(bi):
                t0, N, own = CTX_BLOCKS[bi]
                return ("xres", t0 - PRE) if own else f"xtmp{bi % 2}"

            def front_norm(bi):
                t0, N, own, NT, tile0, e0, hT, xbcT, dtb, adt, par = blk(bi)
                if own:
                    xap = xres[:, :, e0:e0 + N]
                    xk = ("xres", e0)
                    P.dma("sp", xap, xT_v[:, :, t0:t0 + N], writes=[xk, "xtmp0", "xtmp1"], slot=f"xr{e0}")
                else:
                    o_ = (bi % 2) * 512
                    xap = xres[:, :, o_:o_ + N]
                    xk = f"xtmp{bi % 2}"
                    if bi > 0:
                        P.dma("sp", xap, xT_v[:, :, t0:t0 + N], writes=[xk], slot=xk)
                rmsnorm(xap, [xk], N, "mixw", hT.ap, hT.k, hT.ap, hT.k, rstd, "rstd", 0)

            def fx_a(bi, ch):
                t0, N, own, NT, tile0, e0, hT, xbcT, dtb, adt, par = blk(bi)
                pi = (1, 7)[ch % 2]
                u, uk = us[ch % 2], f"u{ch % 2}"
                for c in range(KC):
                    P.mm(ps[pi][:, 0:N], Wz[:, c, 512 + ch * 128:512 + (ch + 1) * 128], hT.ap[:, c, 0:N],
                         start=(c == 0), stop=(c == KC - 1), r=[hT.k] + WZ, w=[K[pi]])
                P.cp(u[:, 0:3], ucar[:, ch, :], r=["ucar"], w=[uk])
                P.act(u[:, 3:3 + N], ps[pi][:, 0:N], AF.Copy, r=[K[pi], uk], w=[uk])
                if t0 == PRE:
                    P.ts(u[:, 3:3 + OWN0], u[:, 3:3 + OWN0], flag, None, ALU.mult, r=[uk], w=[uk])
                P.cp(ucar[:, ch, :], u[:, N:N + 3], r=[uk], w=["ucar"])

            def fx_b(bi, ch):
                t0, N, own, NT, tile0, e0, hT, xbcT, dtb, adt, par = blk(bi)
                pc = 5 + ch % 2
                u, uk = us[ch % 2], f"u{ch % 2}"
                for k_ in range(4):
                    P.mm(ps[pc][:, 0:N], dgall[:, ch * 4 + k_, :], u[:, k_:k_ + N], start=(k_ == 0), stop=(k_ == 3),
                         r=["dgall", uk], w=[K[pc]])
                P.act(xbcT[:, ch, 0:N], ps[pc][:, 0:N], AF.Silu, r=[K[pc]], w=[("xbc", par, ch)], bias=pvc("scb", ch))

            def fx_stages(bi):
                nch = 8 if CTX_BLOCKS[bi][2] else 6
                out = []
                for t_ in range(nch + 1):
                    if t_ < nch:
                        out.append((fx_a, bi, t_))
                    if t_ >= 1:
                        out.append((fx_b, bi, t_ - 1))
                return out

            def front_dt(bi):
                t0, N, own, NT, tile0, e0, hT, xbcT, dtb, adt, par = blk(bi)
                for i in range(NT):
                    for c in range(KC):
                        P.mm(ps[4][:, i * 8:(i + 1) * 8], hT.ap[:, c, i * 128:(i + 1) * 128], Wz[:, c, 1536:1544],
                             start=(c == 0), stop=(c == KC - 1), r=[hT.k] + WZ, w=[K[4]])
                v3 = ps[4][:, 0:NT * 8].rearrange("p (i h) -> p i h", h=8)
                P.tt(dtb[:, 0:NT, :], v3, rbc("dtb", 8).unsqueeze(1).broadcast_to([128, NT, 8]), ALU.add,
                     r=[K[4]], w=[f"dtb{par}"])
                P.act(t8[:, 0:NT, :], dtb[:, 0:NT, :], AF.Abs, r=[f"dtb{par}"], w=["t8"])
                P.act(t8[:, 0:NT, :], t8[:, 0:NT, :], AF.Exp, r=["t8"], w=["t8"], scale=-1.0)
                P.act(t8[:, 0:NT, :], t8[:, 0:NT, :], AF.Ln, r=["t8"], w=["t8"], bias=onec)
                P.stt(dtb[:, 0:NT, :], dtb[:, 0:NT, :], 0.0, t8[:, 0:NT, :], ALU.max, ALU.add, r=[f"dtb{par}", "t8"],
                      w=[f"dt{par}"])
                P.tt(adt[:, 0:NT, :], dtb[:, 0:NT, :], negA.unsqueeze(1).broadcast_to([128, NT, 8]), ALU.mult,
                     r=[f"dt{par}"], w=[f"adt{par}"])

            def front_z(bi):
                t0, N, own, NT, tile0, e0, hT, xbcT, dtb, adt, par = blk(bi)
                for i in range(NT):
                    for c in range(KC):
                        P.mm(ps[2][:, 0:512], hT.ap[:, c, i * 128:(i + 1) * 128], Wz[:, c, 0:512], start=(c == 0),
                             stop=(c == KC - 1), r=[hT.k] + WZ, w=[K[2]])
                    P.act(szb[:, i, :], ps[2][:, 0:512], AF.Silu, r=[K[2]], w=["sz"])

            def chunk_pre(bi, i, q):
                t0, N, own, NT, tile0, e0, hT, xbcT, dtb, adt, par = blk(bi)
                ti = tile0 + i
                full = ti >= 15
                cs = slice(i * 128, (i + 1) * 128)
                xsB, c8, xdd, xdt, R, MT = xsBs[q], c8s[q], xdds[q], xdts[q], Rs[q], MTs[q]
                kx, kc, kd, kt, kR, kM = f"xsB{q}", f"c8{q}", f"xdd{q}", "xdt", "R", "MT"
                if full:
                    P.tt(R, adt[:, i, :].unsqueeze(2).broadcast_to([128, 8, 128]),
                         U32.unsqueeze(1).broadcast_to([128, 8, 128]), ALU.mult, r=[f"adt{par}", "cf"], w=[kR])
                pbf = ps[3][:, :].bitcast(BF16)
                for q_ in range(6):
                    P.tr(pbf[:, q_ * 128:(q_ + 1) * 128], xbcT[:, q_, cs], ident, r=[("xbc", par, q_)], w=[K[3]])
                P.cp(xsB, pbf[:, 0:768], r=[K[3]], w=[kx])
                P.mm(ps[4][:, 64:72], U32, adt[:, i, :], r=[f"adt{par}", "cf"], w=[K[4]])
                P.mm(ps[4][:, 72:80], SL32, adt[:, i, :], r=[f"adt{par}", "cf"], w=[K[4]])
                P.mm(ps[4][:, 80:88], ones32, adt[:, i, :], r=[f"adt{par}", "cf"], w=[K[4]])
                if full:
                    for g in range(2):
                        P.mm(ps[4][:, 128 + g * 128:256 + g * 128], xbcT[:, 4 + g, cs], xbcT[:, 6 + g, cs],
                             r=[("xbc", par, 4 + g), ("xbc", par, 6 + g)], w=[K[4]])
                    Rf0 = R.rearrange("p h l -> p (h l)")
                    for hf in range(2):
                        P.mm(ps[5 + hf][:, 0:512], ones32, Rf0[:, hf * 512:(hf + 1) * 512], start=True, stop=False,
                             r=[kR, "cf"], w=[K[5 + hf]])
                        P.mm(ps[5 + hf][:, 0:512], id32, negrep, start=False, stop=True, r=["cf"], w=[K[5 + hf]])
                P.act(c8[:, 0:24], ps[4][:, 64:88], AF.Exp, r=[K[4]], w=[kc + "e"])
                P.ts(c8[:, 24:32], ps[4][:, 64:72], -1.0, None, ALU.mult, r=[K[4]], w=[kc + "n"])
                P.tt(c8[:, 32:40], c8[:, 8:16], dtb[:, i, :], ALU.mult, r=[kc + "e", f"dt{par}"], w=[kc + "w"])
                xs3 = xsB[:, 0:512].rearrange("p (h d) -> p h d", h=8)
                P.tt(xdd.rearrange("p (h d) -> p h d", h=8), xs3, c8[:, 32:40].unsqueeze(2).broadcast_to([128, 8, 64]),
                     ALU.mult, r=[kx, kc + "w"], w=[kd])

            def chunk_pre_b(bi, i, q):
                t0, N, own, NT, tile0, e0, hT, xbcT, dtb, adt, par = blk(bi)
                if tile0 + i < 15:
                    return
                cs = slice(i * 128, (i + 1) * 128)
                xsB, c8, xdd, xdt, R, MT = xsBs[q], c8s[q], xdds[q], xdts[q], Rs[q], MTs[q]
                kx, kc, kd, kt, kR, kM = f"xsB{q}", f"c8{q}", f"xdd{q}", "xdt", "R", "MT"
                xs3 = xsB[:, 0:512].rearrange("p (h d) -> p h d", h=8)
                for h_ in range(8):
                    P.act(R[:, h_, :], ps[5 + h_ // 4][:, (h_ % 4) * 128:(h_ % 4 + 1) * 128], AF.Exp,
                          r=[K[5 + h_ // 4], kc + "n"], w=[kR], bias=c8[:, 24 + h_:25 + h_])
                cb4 = ps[4][:, 128:384].rearrange("p (g l) -> p g l", g=2).unsqueeze(2).broadcast_to([128, 2, 4, 128])
                P.tt(MT.rearrange("p (g r) l -> p g r l", g=2), R.rearrange("p (g r) l -> p g r l", g=2), cb4,
                     ALU.mult, r=[kR, K[4]], w=[kM])
                P.tt(xdt.rearrange("p (h d) -> p h d", h=8), xs3, dtb[:, i, :].unsqueeze(2).broadcast_to([128, 8, 64]),
                     ALU.mult, r=[kx, f"dt{par}"], w=[kt])
                pd = (0, 2)[q]
                for h_ in range(8):
                    P.mm(ps[pd][:, h_ * 64:(h_ + 1) * 64], MT[:, h_, :], xdt[:, h_ * 64:(h_ + 1) * 64], start=True,
                         stop=False, r=[kM, kt], w=[K[pd]])
                    P.mm(ps[pd][:, h_ * 64:(h_ + 1) * 64], Dg[:, h_, :], xsB[:, h_ * 64:(h_ + 1) * 64], start=False,
                         stop=True, r=["Dg", kx], w=[K[pd]])

            def chunk_post(bi, i, q):
                t0, N, own, NT, tile0, e0, hT, xbcT, dtb, adt, par = blk(bi)
                ti = tile0 + i
                full = ti >= 15
                cs = slice(i * 128, (i + 1) * 128)
                xsB, c8, xdd, y1, y2, ytok = xsBs[q], c8s[q], xdds[q], y1s[q], y2s[q], ytoks[q]
                kx, kc, kd, k1, k2, ky = f"xsB{q}", f"c8{q}", f"xdd{q}", f"y1{q}", "rstd", f"ytok{q}"
                pd = (0, 2)[q]
                for g in range(2):
                    P.mm(ps[7][:, g * 256:(g + 1) * 256], xsB[:, 512 + g * 128:512 + (g + 1) * 128],
                         xdd[:, g * 256:(g + 1) * 256], r=[kx, kd], w=[K[7]])
                if full:
                    for g in range(2):
                        P.mm(ps[1][:, g * 256:(g + 1) * 256], xbcT[:, 6 + g, cs], Sbf[:, g * 256:(g + 1) * 256],
                             r=[("xbc", par, 6 + g), "Sbf"], w=[K[1]])
                P.tt(Sst.rearrange("p (h d) -> p h d", h=8), Sst.rearrange("p (h d) -> p h d", h=8),
                     c8[:, 16:24].unsqueeze(2).broadcast_to([128, 8, 64]), ALU.mult, r=["S", kc + "e"], w=["S"])
                P.tt(Sst, Sst, ps[7][:, 0:512], ALU.add, r=["S", K[7]], w=["S"])
                if ti == 15:
                    P.ts(Sst, Sst, flag, None, ALU.mult, r=["S"], w=["S"])
                if ti >= 14:
                    P.act(Sbf, Sst, AF.Copy, r=["S"], w=["Sbf"])
                if not full:
                    return
                P.tt(y1.rearrange("p (h d) -> p h d", h=8), ps[1][:, 0:512].rearrange("p (h d) -> p h d", h=8),
                     c8[:, 0:8].unsqueeze(2).broadcast_to([128, 8, 64]), ALU.mult, r=[K[1], kc + "e"], w=[k1])
                P.tt(y1, y1, ps[pd][:, 0:512], ALU.add, r=[k1, K[pd]], w=[k1])
                P.tt(y1, y1, szb[:, i, :], ALU.mult, r=[k1, "sz"], w=[k1])
                P.tt(y2, y1, y1, ALU.mult, r=[k1], w=[k2])
                P.red(c8[:, 40:42], y2.rearrange("p (g f) -> p g f", g=2), r=[k2], w=[kc + "r"])
                P.ts(c8[:, 40:42], c8[:, 40:42], 1.0 / 256, RMS_EPS, ALU.mult, ALU.add, r=[kc + "r"], w=[kc + "r"])

            def chunk_post_b(bi, i, q):
                t0, N, own, NT, tile0, e0, hT, xbcT, dtb, adt, par = blk(bi)
                if tile0 + i < 15:
                    return
                cs = slice(i * 128, (i + 1) * 128)
                c8, y1, ytok = c8s[q], y1s[q], ytoks[q]
                kc, k1, ky = f"c8{q}", f"y1{q}", f"ytok{q}"
                P.act(c8[:, 40:42], c8[:, 40:42], AF.Ln, r=[kc + "r"], w=[kc + "r"])
                P.act(c8[:, 40:42], c8[:, 40:42], AF.Exp, r=[kc + "r"], w=[kc + "r"], scale=-0.5)
                for g in range(2):
                    P.stt(ytok[:, g * 256:(g + 1) * 256], y1[:, g * 256:(g + 1) * 256], c8[:, 40 + g:41 + g],
                          rb[:, RB["snw"] + g * 256:RB["snw"] + (g + 1) * 256], ALU.mult, ALU.mult, r=[k1, kc + "r"],
                          w=[ky])
                pbf2 = ps[3][:, :].bitcast(BF16)
                for q_ in range(4):
                    P.tr(pbf2[:, q_ * 128:(q_ + 1) * 128], ytok[:, q_ * 128:(q_ + 1) * 128], ident, r=[ky],
                         w=[K[3]])
                P.act(yT_[:, :, cs], pbf2[:, 0:512].rearrange("p (h t) -> p h t", h=4), AF.Copy, r=[K[3]],
                      w=["yT"])

            def outproj(bi):
                t0, N, own, NT, tile0, e0, hT, xbcT, dtb, adt, par = blk(bi)
                for m in range(KC):
                    pi = m % 2
                    for c in range(8):
                        rhs = attT[:, c, e0:e0 + N] if c < 4 else yT_[:, c - 4, 0:N]
                        P.mm(ps[pi][:, 0:N], Wo[:, c, m * 128:(m + 1) * 128], rhs, start=(c == 0), stop=(c == 7),
                             r=["Wo", "yT"], w=[K[pi]])
                    P.tt(xres[:, m, e0:e0 + N], xres[:, m, e0:e0 + N], ps[pi][:, 0:N], ALU.add, r=[K[pi], xkey(bi)], w=[xkey(bi)])


            nb = len(CTX_BLOCKS)
            gq = [0]
            front_norm(0)
            for f_, b_, c_ in fx_stages(0):
                f_(b_, c_)
            front_dt(0)
            for bi in range(nb):
                t0, N, own = CTX_BLOCKS[bi]
                NT = N // 128
                nxt = bi + 1 if bi + 1 < nb else None
                nchn = 0
                if nxt is None:
                    ffn_prefetch(0, ["Wz0", "Wz1", "Wz2"])
                if nxt is not None:
                    front_norm(nxt)
                    nchn = 8 if CTX_BLOCKS[nxt][2] else 6
                qs_ = []
                for i in range(NT):
                    qs_.append(gq[0] % 2)
                    gq[0] += 1
                chunk_pre(bi, 0, qs_[0])
                fxq = fx_stages(nxt) if nxt is not None else []
                nst = len(fxq)
                done = 0
                chunk_pre_b(bi, 0, qs_[0])
                pend_b = []
                for i in range(NT):
                    if i + 1 < NT:
                        chunk_pre(bi, i + 1, qs_[i + 1])
                    while pend_b:
                        chunk_post_b(*pend_b.pop(0))
                    chunk_post(bi, i, qs_[i])
                    if i + 1 < NT:
                        chunk_pre_b(bi, i + 1, qs_[i + 1])
                    upto = (nst * (i + 1)) // NT
                    while done < upto:
                        f_, b_, c_ = fxq[done]
                        f_(b_, c_)
                        done += 1
                    pend_b.append((bi, i, qs_[i]))
                while pend_b:
                    chunk_post_b(*pend_b.pop(0))
                if nxt is not None:
                    if CTX_BLOCKS[nxt][2]:
                        front_z(nxt)
                    front_dt(nxt)
                if own:
                    outproj(bi)

        pass_b()
        P.barrier(cst)
        P.release(m_x)
        P.off_top = TOP

        def dump_out():
            P.dma("sp", yT.rearrange("(c p) t -> p c t", p=128), xres[:, :, OWN0:EXT], slot="out")
            P.finish()

        if stop_after == "B":
            dump_out()
            return nc, P

        GROUPS = [(0, 6), (6, 6), (12, 5), (17, 5)]

        def ffn(l):
            m0 = P.mark()
            assert m0 == base + KC * EXT * 4
            Wg = P.alloc([128, KC, 768], BF16)
            Wv = P.alloc([128, KC, 768], BF16)
            hT = P.alloc([128, KC, EXT], BF16)
            gbuf = P.alloc([128, 6 * EXT], BF16)
            gT = gbuf.rearrange("p (a b) -> p a b", a=6)
            sq = gbuf[:, 0:4096].rearrange("p (a b) -> p a b", a=8)
            Wd = P.alloc([128, 6, 1024], BF16)
            rstds = [P.alloc([128, 512], F32) for _ in range(2)]
            nrm = 0
            dgs = [P.alloc([128, 6, 128], BF16) for _ in range(2)]
            sets = []
            for i in range(2):
                sets.append(dict(ug=P.alloc([128, 514], BF16), uv=P.alloc([128, 514], BF16), th=P.alloc([128, 512], F32),
                                 i=i, banks=(0, 1, 2, 3) if i == 0 else (4, 5, 6, 7)))
            wn = f"ffnw{l}"
            for (e0, N) in FM_BLOCKS:
                rmsnorm(xres[:, :, e0:e0 + N], ["xres"], N, wn, hT[:, :, e0:e0 + N], ("hTall", e0), hT[:, :, e0:e0 + N],
                        ("hTall", e0), rstds[nrm % 2], f"rstd{nrm % 2}", (6, 7)[nrm % 2])
                nrm += 1
            cnt = 0
            for (j0, gj) in GROUPS:
                if j0 > 0:
                    P.dma("pool", Wg[:, :, 0:gj * 128], wview(upw[l][:, j0 * 128:(j0 + gj) * 128]), writes=["Wg"], slot="Wg")
                    P.dma("pool", Wv[:, :, 0:gj * 128], wview(upw[l][:, FFN + j0 * 128:FFN + (j0 + gj) * 128]), writes=["Wv"],
                          slot="Wv")
                P.dma("pool", Wd[:, 0:gj, :], dnw[l][j0 * 128:(j0 + gj) * 128, :].rearrange("(j p) d -> p j d", p=128),
                      writes=["Wd"], slot="Wd")
                items = []
                for jj in range(gj):
                    for (e0, N) in FM_BLOCKS:
                        items.append((jj, j0 + jj, e0, N))
                state = {"prev": None}

                def stage_a(k):
                    jj, j, e0, N = items[k]
                    S_ = sets[(cnt0 + k) % 2]
                    si = S_["i"]
                    pg, pvv, cg, cv = S_["banks"]
                    if e0 == FM_BLOCKS[0][0]:
                        dg, dk = dgs[j % 2], f"dg{j % 2}"
                        for half, jo in ((0, j), (1, 22 + j)):
                            wc = PV[f"fcw{l}"] + jo * 3
                            P.tt(dg[:, half * 3:half * 3 + 3, :], ident.unsqueeze(1).broadcast_to([128, 3, 128]),
                                 pv[:, wc:wc + 3].unsqueeze(2).broadcast_to([128, 3, 128]), ALU.mult, r=[dk], w=[dk])
                        state["prev"] = None
                    for c in range(KC):
                        P.mm(ps[pg][:, 0:N], Wg[:, c, jj * 128:(jj + 1) * 128], hT[:, c, e0:e0 + N], start=(c == 0),
                             stop=(c == KC - 1), r=[("hTall", e0), "Wg"], w=[K[pg]])
                    for c in range(KC):
                        P.mm(ps[pvv][:, 0:N], Wv[:, c, jj * 128:(jj + 1) * 128], hT[:, c, e0:e0 + N], start=(c == 0),
                             stop=(c == KC - 1), r=[("hTall", e0), "Wv"], w=[K[pvv]])
                    prev = state["prev"]
                    for (un, pi) in (("ug", pg), ("uv", pvv)):
                        ub, uk = S_[un], f"{un}{si}"
                        if prev is None:
                            P.ms(ub[:, 0:2], 0.0, w=[uk])
                        else:
                            pS, pN = prev
                            pub, puk = pS[un], f"{un}{pS['i']}"
                            P.cp(ub[:, 0:2], pub[:, pN:pN + 2], r=[puk], w=[uk])
                        P.act(ub[:, 2:2 + N], ps[pi][:, 0:N], AF.Copy, r=[K[pi], uk], w=[uk])
                        if e0 == 0:
                            P.ts(ub[:, 2:2 + OWN0], ub[:, 2:2 + OWN0], flag, None, ALU.mult, r=[uk], w=[uk])
                    state["prev"] = (S_, N)

                def stage_b(k):
                    jj, j, e0, N = items[k]
                    S_ = sets[(cnt0 + k) % 2]
                    si = S_["i"]
                    pg, pvv, cg, cv = S_["banks"]
                    dg, dk = dgs[j % 2], f"dg{j % 2}"
                    for (un, pc, half) in (("ug", cg, 0), ("uv", cv, 1)):
                        ub, uk = S_[un], f"{un}{si}"
                        for k_ in range(3):
                            P.mm(ps[pc][:, 0:N], dg[:, half * 3 + k_, :], ub[:, k_:k_ + N], start=(k_ == 0), stop=(k_ == 2),
                                 r=[dk, uk], w=[K[pc]])
                    P.act(S_["th"][:, 0:N], ps[cg][:, 0:N], AF.Silu, r=[K[cg]], w=[f"th{si}"], bias=pvc(f"fcb{l}", j))
                    P.stt(gT[:, jj, e0:e0 + N], ps[cv][:, 0:N], pvc(f"fcb{l}", 22 + j), S_["th"][:, 0:N], ALU.add, ALU.mult,
                          r=[K[cv], f"th{si}"], w=["gT"])

                cnt0 = cnt
                stage_a(0)
                for k in range(len(items)):
                    if k + 1 < len(items):
                        stage_a(k + 1)
                    stage_b(k)
                cnt += len(items)
                for m in range(KC):
                    for bi_, (e0, N) in enumerate(FM_BLOCKS):
                        pi = (0, 4, 1, 5)[(m * 5 + bi_) % 4]
                        for jj in range(gj):
                            P.mm(ps[pi][:, 0:N], Wd[:, jj, m * 128:(m + 1) * 128], gT[:, jj, e0:e0 + N], start=(jj == 0),
                                 stop=(jj == gj - 1), r=["gT", "Wd"], w=[K[pi]])
                        P.tt(xres[:, m, e0:e0 + N], xres[:, m, e0:e0 + N], ps[pi][:, 0:N], ALU.add, r=[K[pi], "xres", ("hTall", e0)],
                             w=["xres", ("xo", m)])
                    if l == 1 and (j0, gj) == GROUPS[-1]:
                        P.dma("sp", yT[m * 128:(m + 1) * 128, :], xres[:, m, OWN0:EXT], reads=[("xo", m)], slot=f"out{m}")
            P.release(m0)

        ffn(0)
        P.barrier(cst)
        if stop_after == "F0":
            dump_out()
            return nc, P

        def conformer():
            m0 = P.mark()
            assert m0 == base + KC * EXT * 4
            W1c = [P.alloc([128, KC, 256], BF16) for _ in range(3)]
            hTs = [P.alloc([128, KC, 512], BF16) for _ in range(2)]
            W2 = P.alloc([128, KC, 1024], BF16)
            sq = P.alloc([128, KC, 512], BF16)
            rstd = P.alloc([128, 512], F32)
            gl = P.alloc([128, 8, 542], BF16)
            dgs = [P.alloc([128, 31, 128], BF16) for _ in range(2)]
            accbs = [P.alloc([128, 8, 512], BF16) for _ in range(2)]
            ths = [P.alloc([128, 512], F32) for _ in range(2)]
            xns = [P.alloc([128, 512], F32) for _ in range(2)]
            means = [P.alloc([128, 512], F32) for _ in range(2)]
            vars_ = [P.alloc([128, 512], F32) for _ in range(2)]
            glc = P.alloc([128, 8, 30], BF16)
            P.ms(glc, 0.0, w=["glc"])
            items = [(bi, ch) for bi in range(len(FM_BLOCKS)) for ch in range(8)]

            def w1_load(k):
                if k >= len(items):
                    return
                ch = items[k][1]
                i = k % 3
                P.dma("pool", W1c[i][:, :, 0:128], wview(pw1[:, ch * 128:(ch + 1) * 128]), writes=[f"W1c{i}"],
                      slot=f"W1c{i}a")
                P.dma("pool", W1c[i][:, :, 128:256], wview(pw1[:, 1024 + ch * 128:1024 + (ch + 1) * 128]),
                      writes=[f"W1c{i}"], slot=f"W1c{i}g")

            def norm(bi):
                e0, N = FM_BLOCKS[bi]
                rmsnorm(xres[:, :, e0:e0 + N], [("xres", bi)], N, "confw", hTs[bi % 2], f"hT{bi % 2}", hTs[bi % 2], f"hT{bi % 2}",
                        rstd, "rstd", 6)

            def ln_ch(bi, ch):
                e0, N = FM_BLOCKS[bi]
                accb, p = accbs[bi % 2], bi % 2
                xn, xk = xns[ch % 2], f"xn{ch % 2}"
                ak = ("accb", p, ch)
                P.tt(xn[:, 0:N], accb[:, ch, 0:N], means[p][:, 0:N], ALU.subtract, r=[ak, f"mean{p}"], w=[xk])
                P.tt(xn[:, 0:N], xn[:, 0:N], vars_[p][:, 0:N], ALU.mult, r=[xk, f"var{p}"], w=[xk])
                P.act(accb[:, ch, 0:N], xn[:, 0:N], AF.Silu, r=[xk], w=[ak], scale=pvc("lnw", ch), bias=pvc("lnb", ch))

            def pw2_blk(bi):
                e0, N = FM_BLOCKS[bi]
                accb, p = accbs[bi % 2], bi % 2
                for m in range(KC):
                    pi = 6 + m % 2
                    for c in range(8):
                        P.mm(ps[pi][:, 0:N], W2[:, c, m * 128:(m + 1) * 128], accb[:, c, 0:N], start=(c == 0), stop=(c == 7),
                             r=[("accb", p, c), "W2"], w=[K[pi]])
                    P.stt(xres[:, m, e0:e0 + N], ps[pi][:, 0:N], pvc("pw2b", m), xres[:, m, e0:e0 + N], ALU.add, ALU.add,
                          r=[K[pi], ("xres", bi)], w=[("xres", bi)])

            w1_load(0)
            w1_load(1)
            P.dma("pool", W2, wview(pw2), writes=["W2"], slot="W2")
            norm(0)
            k = 0
            for bi, (e0, N) in enumerate(FM_BLOCKS):
                hT, hk = hTs[bi % 2], f"hT{bi % 2}"
                accb, p = accbs[bi % 2], bi % 2
                def part1(ch, k):
                    w1_load(k + 2)
                    wi = k % 3
                    di = k % 2
                    dg, dk, th, tk = dgs[di], f"dg{di}", ths[di], f"th{di}"
                    pa, pg, pc = (0, 1, 2) if di == 0 else (4, 5, 3)
                    wc = PV["dww"] + ch * 31
                    P.tt(dg, ident.unsqueeze(1).broadcast_to([128, 31, 128]),
                         pv[:, wc:wc + 31].unsqueeze(2).broadcast_to([128, 31, 128]), ALU.mult, w=[dk])
                    for c in range(KC):
                        P.mm(ps[pa][:, 0:N], W1c[wi][:, c, 0:128], hT[:, c, 0:N], start=(c == 0),
                             stop=(c == KC - 1), r=[hk, f"W1c{wi}"], w=[K[pa]])
                    for c in range(KC):
                        P.mm(ps[pg][:, 0:N], W1c[wi][:, c, 128:256], hT[:, c, 0:N], start=(c == 0),
                             stop=(c == KC - 1), r=[hk, f"W1c{wi}"], w=[K[pg]])
                    P.act(th[:, 0:N], ps[pg][:, 0:N], AF.Tanh, r=[K[pg]], w=[tk], scale=0.5, bias=hv[:, ch:ch + 1])
                    P.ts(th[:, 0:N], th[:, 0:N], 0.5, 0.5, ALU.mult, ALU.add, r=[tk], w=[tk])
                    gk = ("gl", ch)
                    P.cp(gl[:, ch, 0:30], glc[:, ch, :], r=["glc"], w=[gk])
                    P.stt(gl[:, ch, 30:30 + N], ps[pa][:, 0:N], pvc("pw1b", ch), th[:, 0:N], ALU.add, ALU.mult,
                          r=[K[pa], tk, gk], w=[gk])
                    if e0 == 0:
                        P.ts(gl[:, ch, 30:30 + OWN0], gl[:, ch, 30:30 + OWN0], flag, None, ALU.mult, r=[gk], w=[gk])
                    P.cp(glc[:, ch, :], gl[:, ch, N:N + 30], r=[gk], w=["glc"])

                def part2(ch, k):
                    di = k % 2
                    dg, dk = dgs[di], f"dg{di}"
                    pc = 2 if di == 0 else 3
                    gk = ("gl", ch)
                    for k_ in range(31):
                        P.mm(ps[pc][:, 0:N], dg[:, k_, :], gl[:, ch, k_:k_ + N], start=(k_ == 0), stop=(k_ == 30),
                             r=[dk, gk], w=[K[pc]])
                    ak = ("accb", p, ch)
                    P.act(accb[:, ch, 0:N], ps[pc][:, 0:N], AF.Identity, r=[K[pc]], w=[ak], bias=pvc("dwb", ch))
                    P.act(sq[:, ch, 0:N], accb[:, ch, 0:N], AF.Square, r=[ak], w=["sq"])
                    if bi > 0:
                        ln_ch(bi - 1, ch)

                part1(0, k)
                for ch in range(8):
                    if ch + 1 < 8:
                        part1(ch + 1, k + ch + 1)
                    elif bi == len(FM_BLOCKS) - 1:
                        ffn_prefetch(1, ["W1c0", "W1c1", "W1c2", "hT0", "hT1"])
                    part2(ch, k + ch)
                    if ch == 5 and bi + 1 < len(FM_BLOCKS):
                        norm(bi + 1)
                k += 8
                if bi > 0:
                    pw2_blk(bi - 1)
                for c in range(8):
                    P.mm(ps[6][:, 0:N], cmean, accb[:, c, 0:N], start=(c == 0), stop=(c == 7), r=[("accb", p, c)], w=[K[6]])
                for c in range(8):
                    P.mm(ps[7][:, 0:N], cmean, sq[:, c, 0:N], start=(c == 0), stop=(c == 7), r=["sq"], w=[K[7]])
                mean, var = means[p], vars_[p]
                P.act(mean[:, 0:N], ps[6][:, 0:N], AF.Copy, r=[K[6]], w=[f"mean{p}"])
                P.tt(var[:, 0:N], mean[:, 0:N], mean[:, 0:N], ALU.mult, r=[f"mean{p}"], w=[f"var{p}"])
                P.tt(var[:, 0:N], ps[7][:, 0:N], var[:, 0:N], ALU.subtract, r=[K[7], f"var{p}"], w=[f"var{p}"])
                P.ts(var[:, 0:N], var[:, 0:N], LN_EPS, None, ALU.add, r=[f"var{p}"], w=[f"var{p}"])
                P.act(var[:, 0:N], var[:, 0:N], AF.Ln, r=[f"var{p}"], w=[f"var{p}"])
                P.act(var[:, 0:N], var[:, 0:N], AF.Exp, r=[f"var{p}"], w=[f"var{p}"], scale=-0.5)
            last = len(FM_BLOCKS) - 1
            for ch in range(8):
                ln_ch(last, ch)
            pw2_blk(last)
            P.release(m0)

        conformer()
        P.barrier(cst)
        if stop_after == "C":
            dump_out()
            return nc, P
        ffn(1)
        P.finish()
    return nc, P

def _host_pack(inp):
    f = np.float32
    pvec = np.zeros((128, NPV), f)
    rowbc = np.zeros((128, NRB), f)

    def fm(vec, nch):
        return np.asarray(vec, f).reshape(nch, 128).T

    pvec[:, PV["mixw"]:PV["mixw"] + 8] = fm(inp["mix_norm_w"][0], 8)
    pvec[:, PV["confw"]:PV["confw"] + 8] = fm(inp["conf_norm_w"][0], 8)
    pvec[:, PV["ffnw0"]:PV["ffnw0"] + 8] = fm(inp["ffn_norm_w"][0], 8)
    pvec[:, PV["ffnw1"]:PV["ffnw1"] + 8] = fm(inp["ffn_norm_w"][1], 8)
    pvec[:, PV["qg"]] = np.tile(np.asarray(inp["q_norm_w"][0], f), 2)
    pvec[:, PV["kg"]] = np.tile(np.asarray(inp["k_norm_w"][0], f), 2)
    scw = np.asarray(inp["ssm_conv_w"][0], f)
    pvec[:, PV["scw"]:PV["scw"] + 32] = scw.reshape(4, 8, 128).transpose(2, 1, 0).reshape(128, 32)
    pvec[:, PV["scb"]:PV["scb"] + 8] = fm(inp["ssm_conv_b"][0], 8)
    pvec[:, PV["pw1b"]:PV["pw1b"] + 16] = fm(inp["conf_pw1_b"][0], 16)
    dww = np.asarray(inp["conf_dw_w"][0], f)
    pvec[:, PV["dww"]:PV["dww"] + 248] = dww.reshape(31, 8, 128).transpose(2, 1, 0).reshape(128, 248)
    pvec[:, PV["dwb"]:PV["dwb"] + 8] = fm(inp["conf_dw_b"][0], 8)
    pvec[:, PV["lnw"]:PV["lnw"] + 8] = fm(inp["conf_ln_w"][0], 8)
    pvec[:, PV["lnb"]:PV["lnb"] + 8] = fm(inp["conf_ln_b"][0], 8)
    pvec[:, PV["pw2b"]:PV["pw2b"] + 8] = fm(inp["conf_pw2_b"][0], 8)
    for l in range(2):
        fcw = np.asarray(inp["ffn_conv_w"][l], f)
        pvec[:, PV[f"fcw{l}"]:PV[f"fcw{l}"] + 132] = fcw.reshape(3, 44, 128).transpose(2, 1, 0).reshape(128, 132)
        pvec[:, PV[f"fcb{l}"]:PV[f"fcb{l}"] + 44] = fm(inp["ffn_conv_b"][l], 44)

    def bc(vec):
        return np.broadcast_to(np.asarray(vec, f)[None, :], (128, len(vec)))

    rowbc[:, RB["subln"]:RB["subln"] + 128] = bc(inp["attn_subln_w"][0])
    rowbc[:, RB["dtb"]:RB["dtb"] + 8] = bc(inp["ssm_dt_bias"][0])
    rowbc[:, RB["alog"]:RB["alog"] + 8] = bc(inp["ssm_A_log"][0])
    rowbc[:, RB["dD"]:RB["dD"] + 8] = bc(inp["ssm_D"][0])
    rowbc[:, RB["snw"]:RB["snw"] + 512] = bc(inp["ssm_norm_w"][0])
    for n, k in (("lq1", "lambda_q1"), ("lk1", "lambda_k1"), ("lq2", "lambda_q2"), ("lk2", "lambda_k2")):
        rowbc[:, RB[n]:RB[n] + 64] = bc(inp[k][0])
    p = np.arange(128)
    cbf = np.zeros((128, NCB), f)
    cbf[:, 0:128] = np.eye(128, dtype=f)
    cbf[:, 128:256] = (p[:, None] <= p[None, :]).astype(f)
    cbf[:, 256:384] = 1.0 / 1024
    cbf[:, 384:512] = ((p[:, None] // 64) == (p[None, :] // 64)).astype(f) / 64
    cf = np.zeros((128, NCF), f)
    cf[:, 0:128] = (p[:, None] <= p[None, :]).astype(f)
    sl = (p[:, None] > p[None, :]).astype(f)
    cf[:, 128:256] = sl
    cf[:, 256:384] = 1.0
    cf[:, 384:512] = np.eye(128, dtype=f)
    cf[:, 512:1024] = np.tile(-1e9 * sl, (1, 4))
    return pvec, rowbc, cbf, cf


_CACHE = {}


def _get_nc(cfg_key, cfg):
    if cfg_key not in _CACHE:
        _, P1 = _build(dict(cfg), None)
        nc, P2 = _build(dict(cfg), P1.newneed)
        _CACHE[cfg_key] = nc
    return _CACHE[cfg_key]


def kernel(**inp):
    cfg = inp.pop("_cfg", {})
    f = np.float32
    x = np.asarray(inp["x"], f)
    pvec, rowbc, cbf, cf = _host_pack(inp)
    common = {
        "w_in": np.ascontiguousarray(inp["w_in"][0], f), "w_out": np.ascontiguousarray(inp["w_out"][0], f),
        "pw1": np.ascontiguousarray(inp["conf_pw1_w"][0], f), "pw2": np.ascontiguousarray(inp["conf_pw2_w"][0], f),
        "up0": np.ascontiguousarray(inp["ffn_up_w"][0], f), "up1": np.ascontiguousarray(inp["ffn_up_w"][1], f),
        "dn0": np.ascontiguousarray(inp["ffn_down_w"][0], f), "dn1": np.ascontiguousarray(inp["ffn_down_w"][1], f),
        "rowbc": rowbc, "cbf": cbf, "cf32": cf,
    }
    in_maps = []
    for c in range(8):
        b, half = c // 2, c % 2
        xt = np.zeros((D, S), f)
        pvc = pvec.copy()
        if half == 1:
            xt[:, :] = x[b].T
            pvc[:, PV["flag"]] = 1.0
        else:
            xt[:, 2048:] = x[b, :2048].T
            pvc[:, PV["kbias"]:PV["kbias"] + 16] = -30000.0
        m = dict(common)
        m["xT"] = xt
        m["pvec"] = pvc
        in_maps.append(m)
    nc = _get_nc(str(sorted(cfg.items())), cfg)
    res = run_bass_kernel_spmd(nc, in_maps, core_ids=list(range(8)))
    out = np.zeros((4, S, D), f)
    for c in range(8):
        b, half = c // 2, c % 2
        out[b, half * 2048:(half + 1) * 2048, :] = res.results[c]["yT"].T
    return out
```

```python
import numpy as np
from contextlib import ExitStack
import concourse.bass as bass
import concourse.mybir as mybir
from concourse.bass_utils import run_bass_kernel_spmd

F32 = mybir.dt.float32
BF16 = mybir.dt.bfloat16
U8 = mybir.dt.uint8
AF = mybir.ActivationFunctionType
ALU = mybir.AluOpType
AX = mybir.AxisListType

D = 1024
KC = 8
S = 4096
PRE = 1920
EXT = 2176
OWN0 = 128
FFN = 2816
NJ = 22
RMS_EPS = 1e-6
LN_EPS = 1e-5
IN_W = 3080
QOFF, KOFF, VOFF, ZOFF, XOFF, DTOFF = 0, 512, 1024, 1536, 2048, 3072

CTX_BLOCKS = [(0, 512, False), (512, 512, False), (1024, 512, False), (1536, 384, False),
              (1920, 512, True), (2432, 512, True), (2944, 384, True), (3328, 384, True),
              (3712, 384, True)]
OWN_BLOCKS = [(0, 128), (128, 512), (640, 512), (1152, 512), (1664, 512)]
FM_BLOCKS = [(0, 448), (448, 432), (880, 432), (1312, 432), (1744, 432)]

PV = {}
_o = 0
for _n, _w in [("mixw", 8), ("confw", 8), ("ffnw0", 8), ("ffnw1", 8), ("qg", 1), ("kg", 1),
               ("scw", 32), ("scb", 8), ("pw1b", 16), ("dww", 8 * 31), ("dwb", 8), ("lnw", 8),
               ("lnb", 8), ("pw2b", 8), ("fcw0", 44 * 3), ("fcw1", 44 * 3), ("fcb0", 44),
               ("fcb1", 44), ("flag", 1), ("kbias", 32)]:
    PV[_n] = _o
    _o += _w
NPV = _o
RB = {}
_o = 0
for _n, _w in [("subln", 128), ("dtb", 8), ("alog", 8), ("dD", 8), ("snw", 512),
               ("lq1", 64), ("lk1", 64), ("lq2", 64), ("lk2", 64)]:
    RB[_n] = _o
    _o += _w
NRB = _o
NCB = 4 * 128
NCF = 4 * 128 + 512


class Prog:
    MAXV = 30000

    def __init__(self, nc, es, need=None):
        self.nc = nc
        self.es = es
        self.pass2 = need is not None
        self.need = need if need is not None else set()
        self.newneed = set()
        self.E = {"pe": nc.tensor, "act": nc.scalar, "dve": nc.vector, "pool": nc.gpsimd,
                  "sp": nc.sync}
        self.sems = {}
        self.sigcnt = {e: 0 for e in self.E}
        self.icnt = {e: 0 for e in self.E}
        self.seen = {e: {} for e in self.E}
        self.opidx = 0
        self.st = {}
        self.slots = {}
        self.sigval = {}
        self.nsem = 0
        self.big = es.enter_context(nc.sbuf_tensor("big", [128, 207 * 1024], U8))
        self.off = 0
        self.ps = [es.enter_context(nc.psum_tensor(f"psb{i}", [128, 512], F32)) for i in range(8)]

    def alloc(self, shape, dt):
        n = int(np.prod(shape[1:]))
        bpe = {F32: 4, BF16: 2, U8: 1}[dt]
        nb = (n * bpe + 31) // 32 * 32
        assert self.off + nb <= 207 * 1024, f"SBUF overflow {self.off}+{nb}"
        v = self.big[:, self.off:self.off + nb]
        self.off += nb
        v = v[:, 0:n * bpe]
        if dt != U8:
            v = v.bitcast(dt)
        if len(shape) == 3:
            v = v.rearrange("p (a b) -> p a b", a=shape[1])
        elif len(shape) == 4:
            v = v.rearrange("p (a b c) -> p a b c", a=shape[1], b=shape[2])
        return v

    def view_at(self, off, shape, dt):
        n = int(np.prod(shape[1:]))
        bpe = {F32: 4, BF16: 2, U8: 1}[dt]
        v = self.big[:, off:off + n * bpe]
        if dt != U8:
            v = v.bitcast(dt)
        if len(shape) == 3:
            v = v.rearrange("p (a b) -> p a b", a=shape[1])
        return v

    def mark(self):
        return self.off

    def release(self, m):
        self.off = m

    def _sem(self, key):
        if key not in self.sems:
            self.sems[key] = self.es.enter_context(self.nc.semaphore(f"s{self.nsem}"))
            self.nsem += 1
        return self.sems[key]

    def _deps(self, reads, writes):
        deps = []
        for k in reads:
            s = self.st.get(k)
            if s and s[0]:
                deps.append(s[0])
        for k in writes:
            s = self.st.get(k)
            if s:
                if s[0]:
                    deps.append(s[0])
                deps.extend(s[1].values())
        return deps

    def _wait(self, eng, deps):
        h = self.E[eng]
        for d in deps:
            if d[0] == "c":
                _, q, qidx, qic = d
                if q == eng:
                    if eng == "pe":
                        continue
                    if self.icnt[eng] - qic > 2:
                        continue
                self.newneed.add(qidx)
                if self.pass2:
                    semkey, sem, val = self.sigval[qidx]
                    if self.seen[eng].get(semkey, 0) >= val:
                        continue
                    h.wait_ge(sem, val)
                    self.seen[eng][semkey] = val
            else:
                _, slot, val = d
                if self.seen[eng].get(("d", slot), 0) >= val:
                    continue
                if self.pass2:
                    h.wait_ge(self.slots[slot][0], val)
                self.seen[eng][("d", slot)] = val

    def _record(self, rec, eng, reads, writes):
        for k in reads:
            self.st.setdefault(k, [None, {}])[1][eng] = rec
        for k in writes:
            self.st[k] = [rec, {}]

    PSK = frozenset(f"ps{i}" for i in range(8))

    def op(self, eng, fn, reads=(), writes=()):
        if any(k in self.PSK for k in reads):
            writes = list(writes) + [k for k in reads if k in self.PSK]
            reads = [k for k in reads if k not in self.PSK]
        idx = self.opidx
        self.opidx += 1
        self._wait(eng, self._deps(reads, writes))
        ic = self.icnt[eng]
        self.icnt[eng] += 1
        if self.pass2:
            ins = fn(self.E[eng])
            if idx in self.need:
                self.sigcnt[eng] += 1
                n = self.sigcnt[eng]
                si = (n - 1) // self.MAXV
                val = (n - 1) % self.MAXV + 1
                sem = self._sem((eng, si))
                ins.then_inc(sem, 1)
                self.sigval[idx] = ((eng, si), sem, val)
        self._record(("c", eng, idx, ic), eng, reads, writes)

    def dma(self, q, out, in_, reads=(), writes=(), slot=None):
        self.opidx += 1
        self._wait(q, self._deps(reads, writes))
        self.icnt[q] += 1
        if slot not in self.slots:
            self.slots[slot] = [self._sem(("dma", slot)) if self.pass2 else None, 0]
        sl = self.slots[slot]
        sl[1] += 16
        if self.pass2:
            self.E[q].dma_start(out=out, in_=in_).then_inc(sl[0], 16)
        self._record(("d", slot, sl[1]), "dma:" + slot, reads, writes)

    def barrier(self, cst):
        ident = cst["ident"]
        scr = cst["scr"]
        self.op("pe", lambda e: e.matmul(self.ps[0][:, 0:1], ident[:, 0:128], ident[:, 0:1],
                                         start=True, stop=True),
                reads=["cb"], writes=["ps0", ("bar", "pe")])
        self.op("act", lambda e: e.activation(scr[:, 0:1], scr[:, 4:5], AF.Copy), writes=[("bar", "act")])
        for i, en in enumerate(("dve", "pool")):
            self.op(en, lambda e, i=i: e.memset(scr[:, 1 + i:2 + i], 0.0), writes=[("bar", en)])
        deps = [self.st[("bar", q)][0] for q in ("pe", "act", "dve", "pool")]
        deps += [("d", s, v[1]) for s, v in self.slots.items()]
        for e in self.E:
            self._wait(e, deps)
        self.st.clear()

    def finish(self):
        deps = [("d", s, v[1]) for s, v in self.slots.items()]
        for e in ("sp", "pool"):
            self._wait(e, deps)


    def mm(self, out, lhsT, rhs, start=True, stop=True, r=(), w=()):
        self.op("pe", lambda e: e.matmul(out, lhsT, rhs, start=start, stop=stop), r, w)

    def tr(self, out, in_, ident, r=(), w=()):
        self.op("pe", lambda e: e.transpose(out, in_, ident), r, w)

    def act(self, out, in_, func, r=(), w=(), bias=None, scale=None):
        kw = {}
        if bias is not None:
            kw["bias"] = bias
        if scale is not None:
            kw["scale"] = scale
        self.op("act", lambda e: e.activation(out, in_, func, **kw), r, w)

    def tt(self, out, a, b, op, r=(), w=(), eng="dve"):
        self.op(eng, lambda e: e.tensor_tensor(out, a, b, op), r, w)

    def ts(self, out, a, s1, s2, op0, op1=None, r=(), w=(), eng="dve"):
        if op1 is None:
            self.op(eng, lambda e: e.tensor_scalar(out, a, s1, None, op0), r, w)
        else:
            self.op(eng, lambda e: e.tensor_scalar(out, a, s1, s2, op0, op1), r, w)

    def stt(self, out, a, s, b, op0, op1, r=(), w=(), eng="dve"):
        self.op(eng, lambda e: e.scalar_tensor_tensor(out, a, s, b, op0, op1), r, w)

    def red(self, out, in_, r=(), w=(), eng="dve"):
        self.op(eng, lambda e: e.tensor_reduce(out, in_, AX.X, ALU.add), r, w)

    def cp(self, out, in_, r=(), w=(), eng="dve"):
        self.op(eng, lambda e: e.tensor_copy(out, in_), r, w)

    def ms(self, out, val, r=(), w=(), eng="dve"):
        self.op(eng, lambda e: e.memset(out, val), r, w)


class Buf:
    def __init__(self, P, shape, dt, key):
        self.ap = P.alloc(shape, dt)
        self.k = key


def bufs(P, n, shape, dt, key):
    return [Buf(P, shape, dt, f"{key}{i}") for i in range(n)]


class Rot:
    def __init__(self, items):
        self.items = items
        self.i = 0

    def next(self):
        x = self.items[self.i % len(self.items)]
        self.i += 1
        return x

def _build(cfg, need):
    nc = bass.Bass("TRN2", target_bir_lowering=False)

    def din(name, shape):
        return nc.dram_tensor(name, list(shape), F32, kind="ExternalInput").ap()

    xT = din("xT", [D, S])
    w_in = din("w_in", [D, IN_W])
    w_out = din("w_out", [D, D])
    pw1 = din("pw1", [D, 2 * D])
    pw2 = din("pw2", [D, D])
    upw = [din("up0", [D, 2 * FFN]), din("up1", [D, 2 * FFN])]
    dnw = [din("dn0", [FFN, D]), din("dn1", [FFN, D])]
    pvec = din("pvec", [128, NPV])
    rowbc = din("rowbc", [128, NRB])
    cbf = din("cbf", [128, NCB])
    cf32 = din("cf32", [128, NCF])
    yT = nc.dram_tensor("yT", [D, 2048], F32, kind="ExternalOutput").ap()
    stop_after = cfg.get("stop_after", "all")

    es = ExitStack()
    with es:
        P = Prog(nc, es, need)
        ps = P.ps
        K = [f"ps{i}" for i in range(8)]
        pv = P.alloc([128, NPV], F32)
        rb = P.alloc([128, NRB], F32)
        cb = P.alloc([128, NCB], BF16)
        sm = P.alloc([128, 16], F32)
        negA = P.alloc([128, 8], F32)
        sw8 = P.alloc([128, 128], F32)
        scr = P.alloc([128, 8], F32)
        hv = P.alloc([128, 24], F32)
        lt = P.alloc([128, 128], F32)
        ident, tri, cmean, bd64 = cb[:, 0:128], cb[:, 128:256], cb[:, 256:384], cb[:, 384:512]
        cst = {"ident": ident, "scr": scr}

        def pvc(name, i=0):
            c = PV[name] + i
            return pv[:, c:c + 1]

        def rbc(name, n):
            return rb[:, RB[name]:RB[name] + n]

        P.dma("sp", pv, pvec, w=["pv"], slot="pv") if False else P.dma("sp", pv, pvec, writes=["pv"], slot="pv")
        P.dma("sp", rb, rowbc, writes=["rb"], slot="rb")
        P.dma("pool", cb, cbf, writes=["cb"], slot="cb")
        P.ts(sm[:, 0:1], pvc("qg"), 0.125, None, ALU.mult, r=["pv"], w=["sm0"])
        P.tt(lt[:, 0:64], rbc("lq1", 64), rbc("lk1", 64), ALU.mult, r=["rb"], w=["lt"])
        P.tt(lt[:, 64:128], rbc("lq2", 64), rbc("lk2", 64), ALU.mult, r=["rb", "lt"], w=["lt"])
        P.red(sm[:, 2:4], lt.rearrange("p (a b) -> p a b", a=2), r=["lt"], w=["sm2"])
        P.act(sm[:, 4:6], sm[:, 2:4], AF.Exp, r=["sm2"], w=["sm4"])
        P.tt(sm[:, 6:7], sm[:, 5:6], sm[:, 4:5], ALU.subtract, r=["sm4"], w=["sm6"])
        P.ts(sm[:, 7:8], sm[:, 6:7], -0.2, None, ALU.add, r=["sm6"], w=["neglam"])
        neglam = sm[:, 7:8]
        P.act(negA, rbc("alog", 8), AF.Exp, r=["rb"], w=["negA0"])
        P.ts(negA, negA, -1.0, None, ALU.mult, r=["negA0"], w=["negA"])
        P.ts(sw8, rbc("subln", 128), 0.8, None, ALU.mult, r=["rb"], w=["sw8"])
        P.ts(hv[:, 0:8], pv[:, PV["pw1b"] + 8:PV["pw1b"] + 16], 0.5, None, ALU.mult, r=["pv"], w=["hv"])
        P.ts(hv[:, 8:16], pv[:, PV["lnw"]:PV["lnw"] + 8], 0.5, None, ALU.mult, r=["pv", "hv"], w=["hv"])
        P.ts(hv[:, 16:24], pv[:, PV["lnb"]:PV["lnb"] + 8], 0.5, None, ALU.mult, r=["pv", "hv"], w=["hv"])
        flag = pvc("flag")
        epsc = scr[:, 5:6]
        P.ms(epsc, RMS_EPS, w=["epsc"])
        onec = scr[:, 6:7]
        P.ms(onec, 1.0, w=["onec"])
        CK = ["pv", "rb", "cb", "sm0", "neglam", "negA", "sw8", "hv"]
        base = P.mark()
        TOP = 207 * 1024
        xT_v = xT.rearrange("(c p) t -> p c t", p=128)

        def wview(src):
            return src.rearrange("(c p) w -> p c w", p=128)

        def rmsnorm(xap, xkeys, N, wname, hT, hkey, sq, sqkey, rstd, rkey, psi):
            P.act(sq[:, :, 0:N], xap, AF.Square, r=xkeys, w=[sqkey])
            for c in range(KC):
                P.mm(ps[psi][:, 0:N], cmean, sq[:, c, 0:N], start=(c == 0), stop=(c == KC - 1),
                     r=[sqkey], w=[K[psi]])
            P.act(rstd[:, 0:N], ps[psi][:, 0:N], AF.Ln, r=[K[psi]], w=[rkey], bias=epsc)
            P.act(rstd[:, 0:N], rstd[:, 0:N], AF.Exp, r=[rkey], w=[rkey], scale=-0.5)
            for c in range(KC):
                P.stt(hT[:, c, 0:N], xap[:, c, :], pvc(wname, c), rstd[:, 0:N], ALU.mult, ALU.mult,
                      r=xkeys + [rkey], w=[hkey])

        P.off_top = TOP - 4 * EXT * 2
        attT = P.big[:, P.off_top:TOP].bitcast(BF16).rearrange("p (a b) -> p a b", a=4)

        def pass_a():
            KT = P.alloc([128, 4, S], BF16)
            V = P.alloc([128, 32, 4, 132], BF16)
            wq_off = P.off
            Wqkv = P.alloc([128, KC, 1536], BF16)
            xa = bufs(P, 2, [128, KC, 512], F32, "xa")
            xa_end = P.off
            hTb = bufs(P, 2, [128, KC, 512], BF16, "hT")
            rstd = P.alloc([128, 512], F32)
            QTb = bufs(P, 1, [128, 4, 512], BF16, "QT") * 2
            sq2 = Rot(bufs(P, 2, [128, 512], BF16, "sq2"))
            raw = Rot(bufs(P, 2, [128, 512], F32, "raw"))
            rs2 = Rot(bufs(P, 2, [128, 512], F32, "rs2"))
            pt = Rot(bufs(P, 6, [128, 512], BF16, "pt"))
            Ocp = P.alloc([128, 4, 512], F32)
            o1 = bufs(P, 4, [128, 128], F32, "o1")
            araw4 = bufs(P, 4, [128, 128], F32, "araw")
            asq = P.alloc([128, 128], F32)
            st = P.alloc([128, 32], F32)
            atok = bufs(P, 4, [128, 512], BF16, "atok")
            print('passA mem', P.off, P.off_top)
            assert P.off <= P.off_top, (P.off, P.off_top)
            for h_ in range(4):
                c0 = 512 + h_ * 128
                P.dma("pool", Wqkv[:, :, c0:c0 + 128], wview(w_in[:, c0:c0 + 128]), writes=[f"Wk{h_}"], slot=f"Wk{h_}")
            for i in (2, 0):
                P.dma("pool", Wqkv[:, :, i * 512:(i + 1) * 512], wview(w_in[:, i * 512:(i + 1) * 512]),
                      writes=[f"Wqkv{i}"], slot=f"Wqkv{i}")
            P.ms(V[:, :, :, 128:129], 1.0, w=["Vones"])
            psrot = Rot([0, 1, 3])
            orot = Rot([4, 6])
            qkrot = Rot([(3, 2), (1, 0)])

            def qk_a(hT, N, woff, wkey):
                a, b, c2 = sq2.next(), raw.next(), rs2.next()
                pr, pst = qkrot.next()
                for c in range(KC):
                    P.mm(ps[pr][:, 0:N], Wqkv[:, c, woff:woff + 128], hT.ap[:, c, 0:N], start=(c == 0),
                         stop=(c == KC - 1), r=[hT.k, wkey], w=[K[pr]])
                P.act(b.ap[:, 0:N], ps[pr][:, 0:N], AF.Copy, r=[K[pr]], w=[b.k])
                P.tt(a.ap[:, 0:N], b.ap[:, 0:N], b.ap[:, 0:N], ALU.mult, r=[b.k], w=[a.k])
                return (a, b, c2, pst)

            def qk_b(ctx, N, gcol, dst, dkey):
                a, b, c2, pst = ctx
                P.mm(ps[pst][:, 0:N], bd64, a.ap[:, 0:N], r=[a.k], w=[K[pst]])
                P.act(c2.ap[:, 0:N], ps[pst][:, 0:N], AF.Ln, r=[K[pst]], w=[c2.k], bias=epsc)
                P.act(c2.ap[:, 0:N], c2.ap[:, 0:N], AF.Exp, r=[c2.k], w=[c2.k], scale=-0.5)
                P.stt(dst, b.ap[:, 0:N], gcol, c2.ap[:, 0:N], ALU.mult, ALU.mult, r=[b.k, c2.k], w=[dkey])

            def pa_norm(bi):
                t0, N, own = CTX_BLOCKS[bi]
                xb, hT = xa[bi % 2], hTb[bi % 2]
                P.dma("sp", xb.ap[:, :, 0:N], xT_v[:, :, t0:t0 + N], writes=[xb.k], slot=xb.k)
                rmsnorm(xb.ap[:, :, 0:N], [xb.k], N, "mixw", hT.ap, hT.k, hT.ap, hT.k, rstd, "rstd", 2)

            pa_norm(0)
            deferred = []
            pending_tr = [None]
            for bi, (t0, N, own) in enumerate(CTX_BLOCKS):
                xb, hT = xa[bi % 2], hTb[bi % 2]
                NT, tile0 = N // 128, t0 // 128
                early_norm = (not own) and bi + 1 < len(CTX_BLOCKS)
                if early_norm:
                    pa_norm(bi + 1)
                e0 = t0 - PRE
                qt = QTb[bi % 2]
                qitems = [(512 + h * 128, f"Wk{h}", pvc("kg"), KT[:, h, t0:t0 + N], ("KT", h)) for h in range(4)]
                if own:
                    qitems += [(h * 128, "Wqkv0", sm[:, 0:1], qt.ap[:, h, 0:N], (qt.k, h)) for h in range(4)]
                vdone = [0]

                def vtile():
                    i = vdone[0]
                    if i >= NT:
                        return
                    vdone[0] += 1
                    pi = 4 + (i % 2)
                    for c in range(KC):
                        P.mm(ps[pi][:, 0:512], hT.ap[:, c, i * 128:(i + 1) * 128], Wqkv[:, c, 1024:1536],
                             start=(c == 0), stop=(c == KC - 1), r=[hT.k, "Wqkv2"], w=[K[pi]])
                    P.act(V[:, tile0 + i, :, 0:128], ps[pi][:, 0:512].rearrange("p (h v) -> p h v", h=4), AF.Copy,
                          r=[K[pi]], w=[("V", tile0 + i)])

                ctxs = {}
                nq = len(qitems)
                for t_ in range(nq + 1):
                    if t_ < nq:
                        ctxs[t_] = qk_a(hT, N, qitems[t_][0], qitems[t_][1])
                    if t_ >= 1:
                        woff_, wk_, gcol_, dst_, dkey_ = qitems[t_ - 1]
                        qk_b(ctxs.pop(t_ - 1), N, gcol_, dst_, dkey_)
                        vtile()
                        if deferred:
                            f_, a_ = deferred.pop(0)
                            f_(a_)
                while vdone[0] < NT:
                    vtile()
                while deferred:
                    f_, a_ = deferred.pop(0)
                    f_(a_)
                if pending_tr[0] is not None:
                    pending_tr[0]()
                    pending_tr[0] = None
                if bi + 1 < len(CTX_BLOCKS):
                    if not early_norm:
                        pa_norm(bi + 1)
                else:
                    o_wz = base + KC * EXT * 4
                    o_wo = o_wz + KC * 1544 * 2
                    o_cf = o_wo + KC * 1024 * 2
                    assert o_wz >= wq_off and o_cf + NCF * 4 <= xa_end, (o_wz, wq_off, o_cf, xa_end)
                    dead = ["Wqkv0", "Wk0", "Wk1", "Wk2", "Wk3", "Wqkv2", xa[0].k, xa[1].k]
                    WzP = P.view_at(o_wz, [128, KC, 1544], BF16)
                    WoP = P.view_at(o_wo, [128, KC, 1024], BF16)
                    cfP = P.view_at(o_cf, [128, NCF], F32)
                    P.dma("pool", WzP[:, :, 512:1024], wview(w_in[:, 2048:2560]), writes=["pfB"] + dead, slot="Wz1")
                    P.dma("pool", WzP[:, :, 0:512], wview(w_in[:, 1536:2048]), writes=["pfB"] + dead, slot="Wz0")
                    P.dma("pool", WzP[:, :, 1024:1544], wview(w_in[:, 2560:3080]), writes=["pfB"] + dead, slot="Wz2")
                    P.dma("pool", WoP, wview(w_out), writes=["pfB"] + dead, slot="Wo")
                    P.dma("sp", cfP, cf32, writes=["pfB"] + dead, slot="cf")
                if not own:
                    continue
                nkt = tile0 + NT
                ats = [atok[i] for i in range(NT)]
                sbank = [Rot([0, 1]), Rot([2, 3])]
                for h in range(4):

                    def score(j, h=h):
                        qs = max(0, j - tile0) * 128
                        res = []
                        for c in range(2):
                            r0 = 64 * c
                            psi = sbank[c].next()
                            p_ = pt.next()
                            P.mm(ps[psi][:, qs:N], KT[r0:r0 + 64, h, j * 128:(j + 1) * 128],
                                 qt.ap[r0:r0 + 64, h, qs:N], r=[("KT", h), (qt.k, h)], w=[K[psi]])
                            res.append((psi, p_))
                        for c in range(2):
                            psi, p_ = res[c]
                            P.act(p_.ap[:, qs:N], ps[psi][:, qs:N], AF.Exp, r=[K[psi]], w=[p_.k],
                                  bias=pvc("kbias", j))
                            if j >= tile0:
                                P.tt(p_.ap[:, qs:qs + 128], p_.ap[:, qs:qs + 128], tri, ALU.mult,
                                     r=[p_.k], w=[p_.k])
                        return [p for _, p in res]

                    def av(j, pts, h=h):
                        for c in range(2):
                            for i in range(NT):
                                if tile0 + i < j:
                                    continue
                                bank, col = 4 + 2 * c + i // 2, (i % 2) * 256
                                P.mm(ps[bank][:, col:col + 129], pts[c].ap[:, i * 128:(i + 1) * 128],
                                     V[:, j, h, 0:129], start=(j == 0 and i % 2 == 0), stop=(j == tile0 + i),
                                     r=[pts[c].k, ("V", j), "Vones"], w=[K[bank]])

                    pend = []
                    for j in range(nkt):
                        if j >= 2 and deferred:
                            f_, a_ = deferred.pop(0)
                            f_(a_)
                        pend.append((j, score(j)))
                        if len(pend) > 1:
                            av(*pend.pop(0))
                    while pend:
                        av(*pend.pop(0))
                    nb_ = (NT + 1) // 2
                    for c in range(2):
                        for bb in range(nb_):
                            bank = 4 + 2 * c + bb
                            P.cp(Ocp[:, 2 * c + bb, 0:385], ps[bank][:, 0:385], r=[K[bank]], w=[("Ocp", 2 * c + bb)])
                    def stage_a(i, h=h):
                        ar = araw4[i]
                        for c in range(2):
                            ok = ("Ocp", 2 * c + i // 2)
                            src = Ocp[:, 2 * c + i // 2, :]
                            col = (i % 2) * 256
                            sc = 4 * c + i
                            P.ts(st[:, sc:sc + 1], src[:, col + 128:col + 129], 1e-30, None, ALU.add, r=[ok],
                                 w=[("st", sc)])
                            P.op("dve", lambda e, sc=sc: e.reciprocal(st[:, sc:sc + 1], st[:, sc:sc + 1]),
                                 [("st", sc)], [("st", sc)])
                            if c == 0:
                                P.ts(o1[i].ap, src[:, col:col + 128], st[:, sc:sc + 1], None, ALU.mult,
                                     r=[ok, ("st", sc)], w=[o1[i].k])
                                continue
                            s2 = 8 + sc
                            s3 = 16 + sc
                            P.tt(st[:, s2:s2 + 1], st[:, sc:sc + 1], neglam, ALU.mult, r=[("st", sc)],
                                 w=[("st", s2)])
                            P.stt(ar.ap, src[:, col:col + 128], st[:, s2:s2 + 1], o1[i].ap, ALU.mult,
                                  ALU.add, r=[ok, ("st", s2), o1[i].k], w=[ar.k])
                            P.tt(asq, ar.ap, ar.ap, ALU.mult, r=[ar.k], w=["asq"])
                            P.red(st[:, s3:s3 + 1], asq, r=["asq"], w=[("st", s3)])
                            P.ts(st[:, s3:s3 + 1], st[:, s3:s3 + 1], 1.0 / 128, RMS_EPS, ALU.mult, ALU.add,
                                 r=[("st", s3)], w=[("st", s3)])

                    def stage_b(i, h=h):
                        ar = araw4[i]
                        s3 = 16 + 4 + i
                        P.act(st[:, s3:s3 + 1], st[:, s3:s3 + 1], AF.Ln, r=[("st", s3)], w=[("st", s3)])
                        P.act(st[:, s3:s3 + 1], st[:, s3:s3 + 1], AF.Exp, r=[("st", s3)], w=[("st", s3)],
                              scale=-0.5)
                        P.stt(ats[i].ap[:, h * 128:(h + 1) * 128], ar.ap, st[:, s3:s3 + 1], sw8, ALU.mult,
                              ALU.mult, r=[ar.k, ("st", s3)], w=[ats[i].k])

                    order = []
                    for t_ in range(NT + 2):
                        if t_ < NT:
                            order.append((stage_a, t_))
                        if 0 <= t_ - 2 < NT:
                            order.append((stage_b, t_ - 2))
                    deferred.extend(order)
                def do_tr(NT=NT, e0=e0):
                    for i in range(NT):
                        tb = 2 + (i % 2)
                        pbf = ps[tb][:, :].bitcast(BF16)
                        for h in range(4):
                            P.tr(pbf[:, h * 128:(h + 1) * 128], atok[i].ap[:, h * 128:(h + 1) * 128], ident,
                                 r=[atok[i].k], w=[K[tb]])
                        P.cp(attT[:, :, e0 + i * 128:e0 + (i + 1) * 128],
                             pbf[:, 0:512].rearrange("p (h t) -> p h t", h=4), r=[K[tb]],
                             w=[("attT", e0 // 128 + i)])
                pending_tr[0] = do_tr
            xres_v = P.view_at(base, [128, KC, EXT], F32)
            N0 = CTX_BLOCKS[0][1]
            deadkv = [("KT", h_) for h_ in range(4)] + [("V", j_) for j_ in range(32)] + ["Vones"]
            P.dma("sp", xres_v[:, :, 0:N0], xT_v[:, :, 0:N0], writes=["pfX"] + deadkv, slot="xtmp0")
            while deferred:
                f_, a_ = deferred.pop(0)
                f_(a_)
            if pending_tr[0] is not None:
                pending_tr[0]()

        pass_a()
        P.barrier(cst)
        P.release(base)
        xres = P.alloc([128, KC, EXT], F32)
        m_x = P.mark()
        if stop_after == "A":
            for c in range(4):
                P.cp(xres[:, c, :], attT[:, c, :], w=["xres"])
            P.dma("sp", yT.rearrange("(c p) t -> p c t", p=128), xres[:, :, OWN0:EXT], reads=["xres"],
                  slot="out")
            P.finish()
            return nc, P

        def ffn_prefetch(l, dead):
            o_g = base + KC * EXT * 4
            gj0 = 6
            WgP = P.view_at(o_g, [128, KC, 768], BF16)
            WvP = P.view_at(o_g + KC * 768 * 2, [128, KC, 768], BF16)
            P.dma("pool", WgP[:, :, 0:gj0 * 128], wview(upw[l][:, 0:gj0 * 128]), writes=["pfF"] + dead, slot="Wg")
            P.dma("pool", WvP[:, :, 0:gj0 * 128], wview(upw[l][:, FFN:FFN + gj0 * 128]), writes=["pfF"] + dead, slot="Wv")

        def pass_b():
            Wz = P.alloc([128, KC, 1544], BF16)
            Wo = P.alloc([128, KC, 1024], BF16)
            cf = P.alloc([128, NCF], F32)
            U32, SL32, ones32, id32, negrep = cf[:, 0:128], cf[:, 128:256], cf[:, 256:384], cf[:, 384:512], cf[:, 512:1024]
            hTb = bufs(P, 1, [128, KC, 512], BF16, "hT") * 2
            rstd = P.alloc([128, 512], F32)
            us = [P.alloc([128, 515], BF16) for _ in range(2)]
            dgall = P.alloc([128, 32, 128], BF16)
            xbcTs = [P.alloc([128, 8, 512], BF16) for _ in range(2)]
            ucar = P.alloc([128, 8, 3], BF16)
            dtbs = [P.alloc([128, 4, 8], F32) for _ in range(2)]
            adts = [P.alloc([128, 4, 8], F32) for _ in range(2)]
            t8 = P.alloc([128, 4, 8], F32)
            Rs = [P.alloc([128, 8, 128], F32)] * 2
            MTs = [P.alloc([128, 8, 128], BF16)] * 2
            xsBs = [P.alloc([128, 768], BF16) for _ in range(2)]
            xdts = [P.alloc([128, 512], BF16)] * 2
            xdds = [P.alloc([128, 512], BF16) for _ in range(2)]
            Sst = P.alloc([128, 512], F32)
            Sbf = P.alloc([128, 512], BF16)
            szb = P.alloc([128, 4, 512], BF16)
            Dg = P.alloc([128, 8, 128], BF16)
            y1s = [P.alloc([128, 512], F32) for _ in range(2)]
            y2s = [rstd] * 2
            ytoks = [P.alloc([128, 512], BF16) for _ in range(2)]
            yT_ = P.alloc([128, 4, 512], BF16)
            c8s = [P.alloc([128, 48], F32) for _ in range(2)]
            print("passB mem", P.off, P.off_top)
            assert P.off <= P.off_top, (P.off, P.off_top)
            assert m_x == base + KC * EXT * 4
            P.ms(ucar, 0.0, w=["ucar"])
            P.tt(dgall, ident.unsqueeze(1).broadcast_to([128, 32, 128]),
                 pv[:, PV["scw"]:PV["scw"] + 32].unsqueeze(2).broadcast_to([128, 32, 128]), ALU.mult, w=["dgall"])
            P.tt(Dg, ident.unsqueeze(1).broadcast_to([128, 8, 128]), rbc("dD", 8).unsqueeze(2).broadcast_to([128, 8, 128]),
                 ALU.mult, w=["Dg"])
            P.ms(Sst, 0.0, w=["S"])
            P.ms(Sbf, 0.0, w=["Sbf"])
            WZ = ["Wz0", "Wz1", "Wz2"]
            def blk(bi):
                t0, N, own = CTX_BLOCKS[bi]
                hT = hTb[bi % 2]
                return t0, N, own, N // 128, t0 // 128, t0 - PRE, hT, xbcTs[bi % 2], dtbs[bi % 2], adts[bi % 2], bi % 2

            def xkey(bi):
                t0, N, own = CTX_BLOCKS[bi]
                return ("xres", t0 - PRE) if own else f"xtmp{bi % 2}"

            def front_norm(bi):
                t0, N, own, NT, tile0, e0, hT, xbcT, dtb, adt, par = blk(bi)
                if own:
                    xap = xres[:, :, e0:e0 + N]
                    xk = ("xres", e0)
                    P.dma("sp", xap, xT_v[:, :, t0:t0 + N], writes=[xk, "xtmp0", "xtmp1"], slot=f"xr{e0}")
                else:
                    o_ = (bi % 2) * 512
                    xap = xres[:, :, o_:o_ + N]
                    xk = f"xtmp{bi % 2}"
                    if bi > 0:
                        P.dma("sp", xap, xT_v[:, :, t0:t0 + N], writes=[xk], slot=xk)
                rmsnorm(xap, [xk], N, "mixw", hT.ap, hT.k, hT.ap, hT.k, rstd, "rstd", 0)

            def fx_a(bi, ch):
                t0, N, own, NT, tile0, e0, hT, xbcT, dtb, adt, par = blk(bi)
                pi = (1, 7)[ch % 2]
                u, uk = us[ch % 2], f"u{ch % 2}"
                for c in range(KC):
                    P.mm(ps[pi][:, 0:N], Wz[:, c, 512 + ch * 128:512 + (ch + 1) * 128], hT.ap[:, c, 0:N],
                         start=(c == 0), stop=(c == KC - 1), r=[hT.k] + WZ, w=[K[pi]])
                P.cp(u[:, 0:3], ucar[:, ch, :], r=["ucar"], w=[uk])
                P.act(u[:, 3:3 + N], ps[pi][:, 0:N], AF.Copy, r=[K[pi], uk], w=[uk])
                if t0 == PRE:
                    P.ts(u[:, 3:3 + OWN0], u[:, 3:3 + OWN0], flag, None, ALU.mult, r=[uk], w=[uk])
                P.cp(ucar[:, ch, :], u[:, N:N + 3], r=[uk], w=["ucar"])

            def fx_b(bi, ch):
                t0, N, own, NT, tile0, e0, hT, xbcT, dtb, adt, par = blk(bi)
                pc = 5 + ch % 2
                u, uk = us[ch % 2], f"u{ch % 2}"
                for k_ in range(4):
                    P.mm(ps[pc][:, 0:N], dgall[:, ch * 4 + k_, :], u[:, k_:k_ + N], start=(k_ == 0), stop=(k_ == 3),
                         r=["dgall", uk], w=[K[pc]])
                P.act(xbcT[:, ch, 0:N], ps[pc][:, 0:N], AF.Silu, r=[K[pc]], w=[("xbc", par, ch)], bias=pvc("scb", ch))

            def fx_stages(bi):
                nch = 8 if CTX_BLOCKS[bi][2] else 6
                out = []
                for t_ in range(nch + 1):
                    if t_ < nch:
                        out.append((fx_a, bi, t_))
                    if t_ >= 1:
                        out.append((fx_b, bi, t_ - 1))
                return out

            def front_dt(bi):
                t0, N, own, NT, tile0, e0, hT, xbcT, dtb, adt, par = blk(bi)
                for i in range(NT):
                    for c in range(KC):
                        P.mm(ps[4][:, i * 8:(i + 1) * 8], hT.ap[:, c, i * 128:(i + 1) * 128], Wz[:, c, 1536:1544],
                             start=(c == 0), stop=(c == KC - 1), r=[hT.k] + WZ, w=[K[4]])
                v3 = ps[4][:, 0:NT * 8].rearrange("p (i h) -> p i h", h=8)
                P.tt(dtb[:, 0:NT, :], v3, rbc("dtb", 8).unsqueeze(1).broadcast_to([128, NT, 8]), ALU.add,
                     r=[K[4]], w=[f"dtb{par}"])
                P.act(t8[:, 0:NT, :], dtb[:, 0:NT, :], AF.Abs, r=[f"dtb{par}"], w=["t8"])
                P.act(t8[:, 0:NT, :], t8[:, 0:NT, :], AF.Exp, r=["t8"], w=["t8"], scale=-1.0)
                P.act(t8[:, 0:NT, :], t8[:, 0:NT, :], AF.Ln, r=["t8"], w=["t8"], bias=onec)
                P.stt(dtb[:, 0:NT, :], dtb[:, 0:NT, :], 0.0, t8[:, 0:NT, :], ALU.max, ALU.add, r=[f"dtb{par}", "t8"],
                      w=[f"dt{par}"])
                P.tt(adt[:, 0:NT, :], dtb[:, 0:NT, :], negA.unsqueeze(1).broadcast_to([128, NT, 8]), ALU.mult,
                     r=[f"dt{par}"], w=[f"adt{par}"])

            def front_z(bi):
                t0, N, own, NT, tile0, e0, hT, xbcT, dtb, adt, par = blk(bi)
                for i in range(NT):
                    for c in range(KC):
                        P.mm(ps[2][:, 0:512], hT.ap[:, c, i * 128:(i + 1) * 128], Wz[:, c, 0:512], start=(c == 0),
                             stop=(c == KC - 1), r=[hT.k] + WZ, w=[K[2]])
                    P.act(szb[:, i, :], ps[2][:, 0:512], AF.Silu, r=[K[2]], w=["sz"])

            def chunk_pre(bi, i, q):
                t0, N, own, NT, tile0, e0, hT, xbcT, dtb, adt, par = blk(bi)
                ti = tile0 + i
                full = ti >= 15
                cs = slice(i * 128, (i + 1) * 128)
                xsB, c8, xdd, xdt, R, MT = xsBs[q], c8s[q], xdds[q], xdts[q], Rs[q], MTs[q]
                kx, kc, kd, kt, kR, kM = f"xsB{q}", f"c8{q}", f"xdd{q}", "xdt", "R", "MT"
                if full:
                    P.tt(R, adt[:, i, :].unsqueeze(2).broadcast_to([128, 8, 128]),
                         U32.unsqueeze(1).broadcast_to([128, 8, 128]), ALU.mult, r=[f"adt{par}", "cf"], w=[kR])
                pbf = ps[3][:, :].bitcast(BF16)
                for q_ in range(6):
                    P.tr(pbf[:, q_ * 128:(q_ + 1) * 128], xbcT[:, q_, cs], ident, r=[("xbc", par, q_)], w=[K[3]])
                P.cp(xsB, pbf[:, 0:768], r=[K[3]], w=[kx])
                P.mm(ps[4][:, 64:72], U32, adt[:, i, :], r=[f"adt{par}", "cf"], w=[K[4]])
                P.mm(ps[4][:, 72:80], SL32, adt[:, i, :], r=[f"adt{par}", "cf"], w=[K[4]])
                P.mm(ps[4][:, 80:88], ones32, adt[:, i, :], r=[f"adt{par}", "cf"], w=[K[4]])
                if full:
                    for g in range(2):
                        P.mm(ps[4][:, 128 + g * 128:256 + g * 128], xbcT[:, 4 + g, cs], xbcT[:, 6 + g, cs],
                             r=[("xbc", par, 4 + g), ("xbc", par, 6 + g)], w=[K[4]])
                    Rf0 = R.rearrange("p h l -> p (h l)")
                    for hf in range(2):
                        P.mm(ps[5 + hf][:, 0:512], ones32, Rf0[:, hf * 512:(hf + 1) * 512], start=True, stop=False,
                             r=[kR, "cf"], w=[K[5 + hf]])
                        P.mm(ps[5 + hf][:, 0:512], id32, negrep, start=False, stop=True, r=["cf"], w=[K[5 + hf]])
                P.act(c8[:, 0:24], ps[4][:, 64:88], AF.Exp, r=[K[4]], w=[kc + "e"])
                P.ts(c8[:, 24:32], ps[4][:, 64:72], -1.0, None, ALU.mult, r=[K[4]], w=[kc + "n"])
                P.tt(c8[:, 32:40], c8[:, 8:16], dtb[:, i, :], ALU.mult, r=[kc + "e", f"dt{par}"], w=[kc + "w"])
                xs3 = xsB[:, 0:512].rearrange("p (h d) -> p h d", h=8)
                P.tt(xdd.rearrange("p (h d) -> p h d", h=8), xs3, c8[:, 32:40].unsqueeze(2).broadcast_to([128, 8, 64]),
                     ALU.mult, r=[kx, kc + "w"], w=[kd])

            def chunk_pre_b(bi, i, q):
                t0, N, own, NT, tile0, e0, hT, xbcT, dtb, adt, par = blk(bi)
                if tile0 + i < 15:
                    return
                cs = slice(i * 128, (i + 1) * 128)
                xsB, c8, xdd, xdt, R, MT = xsBs[q], c8s[q], xdds[q], xdts[q], Rs[q], MTs[q]
                kx, kc, kd, kt, kR, kM = f"xsB{q}", f"c8{q}", f"xdd{q}", "xdt", "R", "MT"
                xs3 = xsB[:, 0:512].rearrange("p (h d) -> p h d", h=8)
                P.tt(xdt.rearrange("p (h d) -> p h d", h=8), xs3, dtb[:, i, :].unsqueeze(2).broadcast_to([128, 8, 64]),
                     ALU.mult, r=[kx, f"dt{par}"], w=[kt])
                for h_ in range(8):
                    P.act(R[:, h_, :], ps[5 + h_ // 4][:, (h_ % 4) * 128:(h_ % 4 + 1) * 128], AF.Exp,
                          r=[K[5 + h_ // 4], kc + "n"], w=[kR], bias=c8[:, 24 + h_:25 + h_])
                cb4 = ps[4][:, 128:384].rearrange("p (g l) -> p g l", g=2).unsqueeze(2).broadcast_to([128, 2, 4, 128])
                P.tt(MT.rearrange("p (g r) l -> p g r l", g=2), R.rearrange("p (g r) l -> p g r l", g=2), cb4,
                     ALU.mult, r=[kR, K[4]], w=[kM])
                pd = (0, 2)[q]
                for h_ in range(8):
                    P.mm(ps[pd][:, h_ * 64:(h_ + 1) * 64], MT[:, h_, :], xdt[:, h_ * 64:(h_ + 1) * 64], start=True,
                         stop=False, r=[kM, kt], w=[K[pd]])
                    P.mm(ps[pd][:, h_ * 64:(h_ + 1) * 64], Dg[:, h_, :], xsB[:, h_ * 64:(h_ + 1) * 64], start=False,
                         stop=True, r=["Dg", kx], w=[K[pd]])

            def chunk_post(bi, i, q):
                t0, N, own, NT, tile0, e0, hT, xbcT, dtb, adt, par = blk(bi)
                ti = tile0 + i
                full = ti >= 15
                cs = slice(i * 128, (i + 1) * 128)
                xsB, c8, xdd, y1, y2, ytok = xsBs[q], c8s[q], xdds[q], y1s[q], y2s[q], ytoks[q]
                kx, kc, kd, k1, k2, ky = f"xsB{q}", f"c8{q}", f"xdd{q}", f"y1{q}", "rstd", f"ytok{q}"
                pd = (0, 2)[q]
                for g in range(2):
                    P.mm(ps[7][:, g * 256:(g + 1) * 256], xsB[:, 512 + g * 128:512 + (g + 1) * 128],
                         xdd[:, g * 256:(g + 1) * 256], r=[kx, kd], w=[K[7]])
                if full:
                    for g in range(2):
                        P.mm(ps[1][:, g * 256:(g + 1) * 256], xbcT[:, 6 + g, cs], Sbf[:, g * 256:(g + 1) * 256],
                             r=[("xbc", par, 6 + g), "Sbf"], w=[K[1]])
                P.tt(Sst.rearrange("p (h d) -> p h d", h=8), Sst.rearrange("p (h d) -> p h d", h=8),
                     c8[:, 16:24].unsqueeze(2).broadcast_to([128, 8, 64]), ALU.mult, r=["S", kc + "e"], w=["S"])
                P.tt(Sst, Sst, ps[7][:, 0:512], ALU.add, r=["S", K[7]], w=["S"])
                if ti == 15:
                    P.ts(Sst, Sst, flag, None, ALU.mult, r=["S"], w=["S"])
                if ti >= 14:
                    P.act(Sbf, Sst, AF.Copy, r=["S"], w=["Sbf"])
                if not full:
                    return
                P.tt(y1.rearrange("p (h d) -> p h d", h=8), ps[1][:, 0:512].rearrange("p (h d) -> p h d", h=8),
                     c8[:, 0:8].unsqueeze(2).broadcast_to([128, 8, 64]), ALU.mult, r=[K[1], kc + "e"], w=[k1])
                P.tt(y1, y1, ps[pd][:, 0:512], ALU.add, r=[k1, K[pd]], w=[k1])
                P.tt(y1, y1, szb[:, i, :], ALU.mult, r=[k1, "sz"], w=[k1])
                P.tt(y2, y1, y1, ALU.mult, r=[k1], w=[k2])
                P.red(c8[:, 40:42], y2.rearrange("p (g f) -> p g f", g=2), r=[k2], w=[kc + "r"])
                P.ts(c8[:, 40:42], c8[:, 40:42], 1.0 / 256, RMS_EPS, ALU.mult, ALU.add, r=[kc + "r"], w=[kc + "r"])

            def chunk_post_b(bi, i, q):
                t0, N, own, NT, tile0, e0, hT, xbcT, dtb, adt, par = blk(bi)
                if tile0 + i < 15:
                    return
                cs = slice(i * 128, (i + 1) * 128)
                c8, y1, ytok = c8s[q], y1s[q], ytoks[q]
                kc, k1, ky = f"c8{q}", f"y1{q}", f"ytok{q}"
                P.act(c8[:, 40:42], c8[:, 40:42], AF.Ln, r=[kc + "r"], w=[kc + "r"])
                P.act(c8[:, 40:42], c8[:, 40:42], AF.Exp, r=[kc + "r"], w=[kc + "r"], scale=-0.5)
                for g in range(2):
                    P.stt(ytok[:, g * 256:(g + 1) * 256], y1[:, g * 256:(g + 1) * 256], c8[:, 40 + g:41 + g],
                          rb[:, RB["snw"] + g * 256:RB["snw"] + (g + 1) * 256], ALU.mult, ALU.mult, r=[k1, kc + "r"],
                          w=[ky])
                pbf2 = ps[3][:, :].bitcast(BF16)
                for q_ in range(4):
                    P.tr(pbf2[:, q_ * 128:(q_ + 1) * 128], ytok[:, q_ * 128:(q_ + 1) * 128], ident, r=[ky],
                         w=[K[3]])
                P.act(yT_[:, :, cs], pbf2[:, 0:512].rearrange("p (h t) -> p h t", h=4), AF.Copy, r=[K[3]],
                      w=["yT"])

            def outproj(bi):
                t0, N, own, NT, tile0, e0, hT, xbcT, dtb, adt, par = blk(bi)
                for m in range(KC):
                    pi = m % 2
                    for c in range(8):
                        rhs = attT[:, c, e0:e0 + N] if c < 4 else yT_[:, c - 4, 0:N]
                        P.mm(ps[pi][:, 0:N], Wo[:, c, m * 128:(m + 1) * 128], rhs, start=(c == 0), stop=(c == 7),
                             r=["Wo", "yT"], w=[K[pi]])
                    P.tt(xres[:, m, e0:e0 + N], xres[:, m, e0:e0 + N], ps[pi][:, 0:N], ALU.add, r=[K[pi], xkey(bi)], w=[xkey(bi)])


            nb = len(CTX_BLOCKS)
            gq = [0]
            front_norm(0)
            for f_, b_, c_ in fx_stages(0):
                f_(b_, c_)
            front_dt(0)
            for bi in range(nb):
                t0, N, own = CTX_BLOCKS[bi]
                NT = N // 128
                nxt = bi + 1 if bi + 1 < nb else None
                nchn = 0
                if nxt is None:
                    ffn_prefetch(0, ["Wz0", "Wz1", "Wz2"])
                if nxt is not None:
                    front_norm(nxt)
                    nchn = 8 if CTX_BLOCKS[nxt][2] else 6
                qs_ = []
                for i in range(NT):
                    qs_.append(gq[0] % 2)
                    gq[0] += 1
                chunk_pre(bi, 0, qs_[0])
                fxq = fx_stages(nxt) if nxt is not None else []
                nst = len(fxq)
                done = 0
                chunk_pre_b(bi, 0, qs_[0])
                pend_b = []
                for i in range(NT):
                    if i + 1 < NT:
                        chunk_pre(bi, i + 1, qs_[i + 1])
                    while pend_b:
                        chunk_post_b(*pend_b.pop(0))
                    chunk_post(bi, i, qs_[i])
                    if i + 1 < NT:
                        chunk_pre_b(bi, i + 1, qs_[i + 1])
                    upto = (nst * (i + 1)) // NT
                    while done < upto:
                        f_, b_, c_ = fxq[done]
                        f_(b_, c_)
                        done += 1
                    pend_b.append((bi, i, qs_[i]))
                while pend_b:
                    chunk_post_b(*pend_b.pop(0))
                if nxt is not None:
                    if CTX_BLOCKS[nxt][2]:
                        front_z(nxt)
                    front_dt(nxt)
                if own:
                    outproj(bi)

        pass_b()
        P.barrier(cst)
        P.release(m_x)
        P.off_top = TOP

        def dump_out():
            P.dma("sp", yT.rearrange("(c p) t -> p c t", p=128), xres[:, :, OWN0:EXT], slot="out")
            P.finish()

        if stop_after == "B":
            dump_out()
            return nc, P

        GROUPS = [(0, 6), (6, 6), (12, 5), (17, 5)]

        def ffn(l):
            m0 = P.mark()
            assert m0 == base + KC * EXT * 4
            Wg = P.alloc([128, KC, 768], BF16)
            Wv = P.alloc([128, KC, 768], BF16)
            hT = P.alloc([128, KC, EXT], BF16)
            gbuf = P.alloc([128, 6 * EXT], BF16)
            gT = gbuf.rearrange("p (a b) -> p a b", a=6)
            sq = gbuf[:, 0:4096].rearrange("p (a b) -> p a b", a=8)
            Wd = P.alloc([128, 6, 1024], BF16)
            rstds = [P.alloc([128, 512], F32) for _ in range(2)]
            nrm = 0
            dgs = [P.alloc([128, 6, 128], BF16) for _ in range(2)]
            sets = []
            for i in range(2):
                sets.append(dict(ug=P.alloc([128, 514], BF16), uv=P.alloc([128, 514], BF16), th=P.alloc([128, 512], F32),
                                 i=i, banks=(0, 1, 2, 3) if i == 0 else (4, 5, 6, 7)))
            wn = f"ffnw{l}"
            for (e0, N) in FM_BLOCKS:
                rmsnorm(xres[:, :, e0:e0 + N], ["xres"], N, wn, hT[:, :, e0:e0 + N], ("hTall", e0), hT[:, :, e0:e0 + N],
                        ("hTall", e0), rstds[nrm % 2], f"rstd{nrm % 2}", (6, 7)[nrm % 2])
                nrm += 1
            cnt = 0
            for (j0, gj) in GROUPS:
                if j0 > 0:
                    P.dma("pool", Wg[:, :, 0:gj * 128], wview(upw[l][:, j0 * 128:(j0 + gj) * 128]), writes=["Wg"], slot="Wg")
                    P.dma("pool", Wv[:, :, 0:gj * 128], wview(upw[l][:, FFN + j0 * 128:FFN + (j0 + gj) * 128]), writes=["Wv"],
                          slot="Wv")
                P.dma("pool", Wd[:, 0:gj, :], dnw[l][j0 * 128:(j0 + gj) * 128, :].rearrange("(j p) d -> p j d", p=128),
                      writes=["Wd"], slot="Wd")
                items = []
                for jj in range(gj):
                    for (e0, N) in FM_BLOCKS:
                        items.append((jj, j0 + jj, e0, N))
                state = {"prev": None}

                def stage_a(k):
                    jj, j, e0, N = items[k]
                    S_ = sets[(cnt0 + k) % 2]
                    si = S_["i"]
                    pg, pvv, cg, cv = S_["banks"]
                    if e0 == FM_BLOCKS[0][0]:
                        dg, dk = dgs[j % 2], f"dg{j % 2}"
                        for half, jo in ((0, j), (1, 22 + j)):
                            wc = PV[f"fcw{l}"] + jo * 3
                            P.tt(dg[:, half * 3:half * 3 + 3, :], ident.unsqueeze(1).broadcast_to([128, 3, 128]),
                                 pv[:, wc:wc + 3].unsqueeze(2).broadcast_to([128, 3, 128]), ALU.mult, r=[dk], w=[dk])
                        state["prev"] = None
                    for c in range(KC):
                        P.mm(ps[pg][:, 0:N], Wg[:, c, jj * 128:(jj + 1) * 128], hT[:, c, e0:e0 + N], start=(c == 0),
                             stop=(c == KC - 1), r=[("hTall", e0), "Wg"], w=[K[pg]])
                    for c in range(KC):
                        P.mm(ps[pvv][:, 0:N], Wv[:, c, jj * 128:(jj + 1) * 128], hT[:, c, e0:e0 + N], start=(c == 0),
                             stop=(c == KC - 1), r=[("hTall", e0), "Wv"], w=[K[pvv]])
                    prev = state["prev"]
                    for (un, pi) in (("ug", pg), ("uv", pvv)):
                        ub, uk = S_[un], f"{un}{si}"
                        if prev is None:
                            P.ms(ub[:, 0:2], 0.0, w=[uk])
                        else:
                            pS, pN = prev
                            pub, puk = pS[un], f"{un}{pS['i']}"
                            P.cp(ub[:, 0:2], pub[:, pN:pN + 2], r=[puk], w=[uk])
                        P.act(ub[:, 2:2 + N], ps[pi][:, 0:N], AF.Copy, r=[K[pi], uk], w=[uk])
                        if e0 == 0:
                            P.ts(ub[:, 2:2 + OWN0], ub[:, 2:2 + OWN0], flag, None, ALU.mult, r=[uk], w=[uk])
                    state["prev"] = (S_, N)

                def stage_b(k):
                    jj, j, e0, N = items[k]
                    S_ = sets[(cnt0 + k) % 2]
                    si = S_["i"]
                    pg, pvv, cg, cv = S_["banks"]
                    dg, dk = dgs[j % 2], f"dg{j % 2}"
                    for (un, pc, half) in (("ug", cg, 0), ("uv", cv, 1)):
                        ub, uk = S_[un], f"{un}{si}"
                        for k_ in range(3):
                            P.mm(ps[pc][:, 0:N], dg[:, half * 3 + k_, :], ub[:, k_:k_ + N], start=(k_ == 0), stop=(k_ == 2),
                                 r=[dk, uk], w=[K[pc]])
                    P.act(S_["th"][:, 0:N], ps[cg][:, 0:N], AF.Silu, r=[K[cg]], w=[f"th{si}"], bias=pvc(f"fcb{l}", j))
                    P.stt(gT[:, jj, e0:e0 + N], ps[cv][:, 0:N], pvc(f"fcb{l}", 22 + j), S_["th"][:, 0:N], ALU.add, ALU.mult,
                          r=[K[cv], f"th{si}"], w=["gT"])

                cnt0 = cnt
                stage_a(0)
                for k in range(len(items)):
                    if k + 1 < len(items):
                        stage_a(k + 1)
                    stage_b(k)
                cnt += len(items)
                for m in range(KC):
                    for bi_, (e0, N) in enumerate(FM_BLOCKS):
                        pi = (0, 4, 1, 5)[(m * 5 + bi_) % 4]
                        for jj in range(gj):
                            P.mm(ps[pi][:, 0:N], Wd[:, jj, m * 128:(m + 1) * 128], gT[:, jj, e0:e0 + N], start=(jj == 0),
                                 stop=(jj == gj - 1), r=["gT", "Wd"], w=[K[pi]])
                        P.tt(xres[:, m, e0:e0 + N], xres[:, m, e0:e0 + N], ps[pi][:, 0:N], ALU.add, r=[K[pi], "xres", ("hTall", e0)],
                             w=["xres", ("xo", m)])
                    if l == 1 and (j0, gj) == GROUPS[-1]:
                        P.dma("sp", yT[m * 128:(m + 1) * 128, :], xres[:, m, OWN0:EXT], reads=[("xo", m)], slot=f"out{m}")
            P.release(m0)

        ffn(0)
        P.barrier(cst)
        if stop_after == "F0":
            dump_out()
            return nc, P

        def conformer():
            m0 = P.mark()
            assert m0 == base + KC * EXT * 4
            W1c = [P.alloc([128, KC, 256], BF16) for _ in range(3)]
            hTs = [P.alloc([128, KC, 512], BF16) for _ in range(2)]
            W2 = P.alloc([128, KC, 1024], BF16)
            sq = P.alloc([128, KC, 512], BF16)
            rstd = P.alloc([128, 512], F32)
            gl = P.alloc([128, 8, 542], BF16)
            dgs = [P.alloc([128, 31, 128], BF16) for _ in range(2)]
            accbs = [P.alloc([128, 8, 512], BF16) for _ in range(2)]
            ths = [P.alloc([128, 512], F32) for _ in range(2)]
            xns = [P.alloc([128, 512], F32) for _ in range(2)]
            means = [P.alloc([128, 512], F32) for _ in range(2)]
            vars_ = [P.alloc([128, 512], F32) for _ in range(2)]
            glc = P.alloc([128, 8, 30], BF16)
            P.ms(glc, 0.0, w=["glc"])
            items = [(bi, ch) for bi in range(len(FM_BLOCKS)) for ch in range(8)]

            def w1_load(k):
                if k >= len(items):
                    return
                ch = items[k][1]
                i = k % 3
                P.dma("pool", W1c[i][:, :, 0:128], wview(pw1[:, ch * 128:(ch + 1) * 128]), writes=[f"W1c{i}"],
                      slot=f"W1c{i}a")
                P.dma("pool", W1c[i][:, :, 128:256], wview(pw1[:, 1024 + ch * 128:1024 + (ch + 1) * 128]),
                      writes=[f"W1c{i}"], slot=f"W1c{i}g")

            def norm(bi):
                e0, N = FM_BLOCKS[bi]
                rmsnorm(xres[:, :, e0:e0 + N], [("xres", bi)], N, "confw", hTs[bi % 2], f"hT{bi % 2}", hTs[bi % 2], f"hT{bi % 2}",
                        rstd, "rstd", 6)

            def ln_ch(bi, ch):
                e0, N = FM_BLOCKS[bi]
                accb, p = accbs[bi % 2], bi % 2
                xn, xk = xns[ch % 2], f"xn{ch % 2}"
                ak = ("accb", p, ch)
                P.tt(xn[:, 0:N], accb[:, ch, 0:N], means[p][:, 0:N], ALU.subtract, r=[ak, f"mean{p}"], w=[xk])
                P.tt(xn[:, 0:N], xn[:, 0:N], vars_[p][:, 0:N], ALU.mult, r=[xk, f"var{p}"], w=[xk])
                P.act(accb[:, ch, 0:N], xn[:, 0:N], AF.Silu, r=[xk], w=[ak], scale=pvc("lnw", ch), bias=pvc("lnb", ch))

            def pw2_blk(bi):
                e0, N = FM_BLOCKS[bi]
                accb, p = accbs[bi % 2], bi % 2
                for m in range(KC):
                    pi = 6 + m % 2
                    for c in range(8):
                        P.mm(ps[pi][:, 0:N], W2[:, c, m * 128:(m + 1) * 128], accb[:, c, 0:N], start=(c == 0), stop=(c == 7),
                             r=[("accb", p, c), "W2"], w=[K[pi]])
                    P.stt(xres[:, m, e0:e0 + N], ps[pi][:, 0:N], pvc("pw2b", m), xres[:, m, e0:e0 + N], ALU.add, ALU.add,
                          r=[K[pi], ("xres", bi)], w=[("xres", bi)])

            w1_load(0)
            w1_load(1)
            P.dma("pool", W2, wview(pw2), writes=["W2"], slot="W2")
            norm(0)
            k = 0
            for bi, (e0, N) in enumerate(FM_BLOCKS):
                hT, hk = hTs[bi % 2], f"hT{bi % 2}"
                accb, p = accbs[bi % 2], bi % 2
                def part1(ch, k):
                    w1_load(k + 2)
                    wi = k % 3
                    di = k % 2
                    dg, dk, th, tk = dgs[di], f"dg{di}", ths[di], f"th{di}"
                    pa, pg, pc = (0, 1, 2) if di == 0 else (4, 5, 3)
                    wc = PV["dww"] + ch * 31
                    P.tt(dg, ident.unsqueeze(1).broadcast_to([128, 31, 128]),
                         pv[:, wc:wc + 31].unsqueeze(2).broadcast_to([128, 31, 128]), ALU.mult, w=[dk])
                    for c in range(KC):
                        P.mm(ps[pa][:, 0:N], W1c[wi][:, c, 0:128], hT[:, c, 0:N], start=(c == 0),
                             stop=(c == KC - 1), r=[hk, f"W1c{wi}"], w=[K[pa]])
                    for c in range(KC):
                        P.mm(ps[pg][:, 0:N], W1c[wi][:, c, 128:256], hT[:, c, 0:N], start=(c == 0),
                             stop=(c == KC - 1), r=[hk, f"W1c{wi}"], w=[K[pg]])
                    P.act(th[:, 0:N], ps[pg][:, 0:N], AF.Tanh, r=[K[pg]], w=[tk], scale=0.5, bias=hv[:, ch:ch + 1])
                    P.ts(th[:, 0:N], th[:, 0:N], 0.5, 0.5, ALU.mult, ALU.add, r=[tk], w=[tk])
                    gk = ("gl", ch)
                    P.cp(gl[:, ch, 0:30], glc[:, ch, :], r=["glc"], w=[gk])
                    P.stt(gl[:, ch, 30:30 + N], ps[pa][:, 0:N], pvc("pw1b", ch), th[:, 0:N], ALU.add, ALU.mult,
                          r=[K[pa], tk, gk], w=[gk])
                    if e0 == 0:
                        P.ts(gl[:, ch, 30:30 + OWN0], gl[:, ch, 30:30 + OWN0], flag, None, ALU.mult, r=[gk], w=[gk])
                    P.cp(glc[:, ch, :], gl[:, ch, N:N + 30], r=[gk], w=["glc"])

                def part2(ch, k):
                    di = k % 2
                    dg, dk = dgs[di], f"dg{di}"
                    pc = 2 if di == 0 else 3
                    gk = ("gl", ch)
                    for k_ in range(31):
                        P.mm(ps[pc][:, 0:N], dg[:, k_, :], gl[:, ch, k_:k_ + N], start=(k_ == 0), stop=(k_ == 30),
                             r=[dk, gk], w=[K[pc]])
                    ak = ("accb", p, ch)
                    P.act(accb[:, ch, 0:N], ps[pc][:, 0:N], AF.Identity, r=[K[pc]], w=[ak], bias=pvc("dwb", ch))
                    P.act(sq[:, ch, 0:N], accb[:, ch, 0:N], AF.Square, r=[ak], w=["sq"])
                    if bi > 0:
                        ln_ch(bi - 1, ch)

                part1(0, k)
                for ch in range(8):
                    if ch + 1 < 8:
                        part1(ch + 1, k + ch + 1)
                    elif bi == len(FM_BLOCKS) - 1:
                        ffn_prefetch(1, ["W1c0", "W1c1", "W1c2", "hT0", "hT1"])
                    part2(ch, k + ch)
                    if ch == 5 and bi + 1 < len(FM_BLOCKS):
                        norm(bi + 1)
                k += 8
                if bi > 0:
                    pw2_blk(bi - 1)
                for c in range(8):
                    P.mm(ps[6][:, 0:N], cmean, accb[:, c, 0:N], start=(c == 0), stop=(c == 7), r=[("accb", p, c)], w=[K[6]])
                for c in range(8):
                    P.mm(ps[7][:, 0:N], cmean, sq[:, c, 0:N], start=(c == 0), stop=(c == 7), r=["sq"], w=[K[7]])
                mean, var = means[p], vars_[p]
                P.act(mean[:, 0:N], ps[6][:, 0:N], AF.Copy, r=[K[6]], w=[f"mean{p}"])
                P.tt(var[:, 0:N], mean[:, 0:N], mean[:, 0:N], ALU.mult, r=[f"mean{p}"], w=[f"var{p}"])
                P.tt(var[:, 0:N], ps[7][:, 0:N], var[:, 0:N], ALU.subtract, r=[K[7], f"var{p}"], w=[f"var{p}"])
                P.ts(var[:, 0:N], var[:, 0:N], LN_EPS, None, ALU.add, r=[f"var{p}"], w=[f"var{p}"])
                P.act(var[:, 0:N], var[:, 0:N], AF.Ln, r=[f"var{p}"], w=[f"var{p}"])
                P.act(var[:, 0:N], var[:, 0:N], AF.Exp, r=[f"var{p}"], w=[f"var{p}"], scale=-0.5)
            last = len(FM_BLOCKS) - 1
            for ch in range(8):
                ln_ch(last, ch)
            pw2_blk(last)
            P.release(m0)

        conformer()
        P.barrier(cst)
        if stop_after == "C":
            dump_out()
            return nc, P
        ffn(1)
        P.finish()
    return nc, P

def _host_pack(inp):
    f = np.float32
    pvec = np.zeros((128, NPV), f)
    rowbc = np.zeros((128, NRB), f)

    def fm(vec, nch):
        return np.asarray(vec, f).reshape(nch, 128).T

    pvec[:, PV["mixw"]:PV["mixw"] + 8] = fm(inp["mix_norm_w"][0], 8)
    pvec[:, PV["confw"]:PV["confw"] + 8] = fm(inp["conf_norm_w"][0], 8)
    pvec[:, PV["ffnw0"]:PV["ffnw0"] + 8] = fm(inp["ffn_norm_w"][0], 8)
    pvec[:, PV["ffnw1"]:PV["ffnw1"] + 8] = fm(inp["ffn_norm_w"][1], 8)
    pvec[:, PV["qg"]] = np.tile(np.asarray(inp["q_norm_w"][0], f), 2)
    pvec[:, PV["kg"]] = np.tile(np.asarray(inp["k_norm_w"][0], f), 2)
    scw = np.asarray(inp["ssm_conv_w"][0], f)
    pvec[:, PV["scw"]:PV["scw"] + 32] = scw.reshape(4, 8, 128).transpose(2, 1, 0).reshape(128, 32)
    pvec[:, PV["scb"]:PV["scb"] + 8] = fm(inp["ssm_conv_b"][0], 8)
    pvec[:, PV["pw1b"]:PV["pw1b"] + 16] = fm(inp["conf_pw1_b"][0], 16)
    dww = np.asarray(inp["conf_dw_w"][0], f)
    pvec[:, PV["dww"]:PV["dww"] + 248] = dww.reshape(31, 8, 128).transpose(2, 1, 0).reshape(128, 248)
    pvec[:, PV["dwb"]:PV["dwb"] + 8] = fm(inp["conf_dw_b"][0], 8)
    pvec[:, PV["lnw"]:PV["lnw"] + 8] = fm(inp["conf_ln_w"][0], 8)
    pvec[:, PV["lnb"]:PV["lnb"] + 8] = fm(inp["conf_ln_b"][0], 8)
    pvec[:, PV["pw2b"]:PV["pw2b"] + 8] = fm(inp["conf_pw2_b"][0], 8)
    for l in range(2):
        fcw = np.asarray(inp["ffn_conv_w"][l], f)
        pvec[:, PV[f"fcw{l}"]:PV[f"fcw{l}"] + 132] = fcw.reshape(3, 44, 128).transpose(2, 1, 0).reshape(128, 132)
        pvec[:, PV[f"fcb{l}"]:PV[f"fcb{l}"] + 44] = fm(inp["ffn_conv_b"][l], 44)

    def bc(vec):
        return np.broadcast_to(np.asarray(vec, f)[None, :], (128, len(vec)))

    rowbc[:, RB["subln"]:RB["subln"] + 128] = bc(inp["attn_subln_w"][0])
    rowbc[:, RB["dtb"]:RB["dtb"] + 8] = bc(inp["ssm_dt_bias"][0])
    rowbc[:, RB["alog"]:RB["alog"] + 8] = bc(inp["ssm_A_log"][0])
    rowbc[:, RB["dD"]:RB["dD"] + 8] = bc(inp["ssm_D"][0])
    rowbc[:, RB["snw"]:RB["snw"] + 512] = bc(inp["ssm_norm_w"][0])
    for n, k in (("lq1", "lambda_q1"), ("lk1", "lambda_k1"), ("lq2", "lambda_q2"), ("lk2", "lambda_k2")):
        rowbc[:, RB[n]:RB[n] + 64] = bc(inp[k][0])
    p = np.arange(128)
    cbf = np.zeros((128, NCB), f)
    cbf[:, 0:128] = np.eye(128, dtype=f)
    cbf[:, 128:256] = (p[:, None] <= p[None, :]).astype(f)
    cbf[:, 256:384] = 1.0 / 1024
    cbf[:, 384:512] = ((p[:, None] // 64) == (p[None, :] // 64)).astype(f) / 64
    cf = np.zeros((128, NCF), f)
    cf[:, 0:128] = (p[:, None] <= p[None, :]).astype(f)
    sl = (p[:, None] > p[None, :]).astype(f)
    cf[:, 128:256] = sl
    cf[:, 256:384] = 1.0
    cf[:, 384:512] = np.eye(128, dtype=f)
    cf[:, 512:1024] = np.tile(-1e9 * sl, (1, 4))
    return pvec, rowbc, cbf, cf


_CACHE = {}


def _get_nc(cfg_key, cfg):
    if cfg_key not in _CACHE:
        _, P1 = _build(dict(cfg), None)
        nc, P2 = _build(dict(cfg), P1.newneed)
        _CACHE[cfg_key] = nc
    return _CACHE[cfg_key]


def kernel(**inp):
    cfg = inp.pop("_cfg", {})
    f = np.float32
    x = np.asarray(inp["x"], f)
    pvec, rowbc, cbf, cf = _host_pack(inp)
    common = {
        "w_in": np.ascontiguousarray(inp["w_in"][0], f), "w_out": np.ascontiguousarray(inp["w_out"][0], f),
        "pw1": np.ascontiguousarray(inp["conf_pw1_w"][0], f), "pw2": np.ascontiguousarray(inp["conf_pw2_w"][0], f),
        "up0": np.ascontiguousarray(inp["ffn_up_w"][0], f), "up1": np.ascontiguousarray(inp["ffn_up_w"][1], f),
        "dn0": np.ascontiguousarray(inp["ffn_down_w"][0], f), "dn1": np.ascontiguousarray(inp["ffn_down_w"][1], f),
        "rowbc": rowbc, "cbf": cbf, "cf32": cf,
    }
    in_maps = []
    for c in range(8):
        b, half = c // 2, c % 2
        xt = np.zeros((D, S), f)
        pvc = pvec.copy()
        if half == 1:
            xt[:, :] = x[b].T
            pvc[:, PV["flag"]] = 1.0
        else:
            xt[:, 2048:] = x[b, :2048].T
            pvc[:, PV["kbias"]:PV["kbias"] + 16] = -30000.0
        m = dict(common)
        m["xT"] = xt
        m["pvec"] = pvc
        in_maps.append(m)
    nc = _get_nc(str(sorted(cfg.items())), cfg)
    res = run_bass_kernel_spmd(nc, in_maps, core_ids=list(range(8)))
    out = np.zeros((4, S, D), f)
    for c in range(8):
        b, half = c // 2, c % 2
        out[b, half * 2048:(half + 1) * 2048, :] = res.results[c]["yT"].T
    return out
```
